# Optimizing a Trainium2 kernel written in Bass

```python
import math
import jax, jax.numpy as jnp
from jax import lax
import numpy as np

D_MODEL = 1024
BATCH = 16
SEQ = 256
DEPTH = 2
DEC_BATCH = 8
DEC_SEQ = 2048
PAST_LEN = 512

GRID_W = 64
BRANCH_W = D_MODEL
N_BRANCH = 3
HY_ORDER = 2
HY_CONV = 3
HY_EMB = 33
HY_FILTER_HIDDEN = 64
HY_DECAY_TARGET = 1e-2
HY_SHORT_DECAY_PCT = 0.3
HY_LONG_DECAY_PCT = 1.5
S5_GROUP_CH = 16
S5_GROUPS = BRANCH_W // S5_GROUP_CH
S5_STATE = 64
GDN_DK = 128
GDN_DV = 128
GDN_HEADS = BRANCH_W // GDN_DV
GDN_CONV = 3
GDN_CHUNK = 64
NORM_EPS = 1e-6
IN_SIZES = ((HY_ORDER + 1) * BRANCH_W, BRANCH_W,
            BRANCH_W, BRANCH_W,
            3 * BRANCH_W, 2 * GDN_HEADS, 2 * GDN_HEADS, BRANCH_W,
            N_BRANCH * D_MODEL)
IN_COLS = sum(IN_SIZES)

kernel_name = 'hybrid_hyena_s5_gdn_diffusion_step'


def rmsnorm(x, g):
    xf = x.astype(jnp.float32)
    y = xf * lax.rsqrt(jnp.mean(xf * xf, axis=-1, keepdims=True) + NORM_EPS)
    return y.astype(x.dtype) * g


def l2norm(x):
    return x * lax.rsqrt(jnp.sum(x * x, axis=-1, keepdims=True) + NORM_EPS)


def dwconv_centred(x, w):
    k = w.shape[0]
    return lax.conv_general_dilated(x, w[:, None, :].astype(x.dtype), window_strides=(1,),
                                    padding=[(k // 2, k // 2)],
                                    dimension_numbers=('NWC', 'WIO', 'NWC'),
                                    feature_group_count=x.shape[-1])


def grid_pos_embed(n_tokens, dim):
    rows = n_tokens // GRID_W
    t = jnp.arange(rows * GRID_W)
    r = (t // GRID_W).astype(jnp.float32)
    col = (t % GRID_W).astype(jnp.float32)
    quarter = dim // 4
    omega = 1.0 / (10000.0 ** (jnp.arange(quarter, dtype=jnp.float32) / quarter))
    er = r[:, None] * omega
    ec = col[:, None] * omega
    return jnp.concatenate([jnp.sin(er), jnp.cos(er), jnp.sin(ec), jnp.cos(ec)], axis=-1)


def hyena_filter_spectra(L, w1, b1, w2, b2, w3, freq):
    f32 = jnp.float32
    t = jnp.linspace(0.0, 1.0, L, dtype=f32)[:, None]
    bands = (HY_EMB - 1) // 2
    fb = jnp.linspace(1e-4, bands - 1, bands, dtype=f32)[None, :]
    wpos = 2.0 * math.pi * jnp.arange(L, dtype=f32)[:, None] / L
    z = jnp.concatenate([t, jnp.cos(fb * wpos), -jnp.sin(fb * wpos)], axis=-1)
    freq = freq.astype(f32)
    hdn = jnp.sin(freq * (z @ w1.astype(f32) + b1.astype(f32)))
    hdn = jnp.sin(freq * (hdn @ w2.astype(f32) + b2.astype(f32)))
    filt = (hdn @ w3.astype(f32)).reshape(L, HY_ORDER, 2, BRANCH_W)
    deltas = jnp.abs(jnp.linspace(math.log(HY_DECAY_TARGET) / HY_LONG_DECAY_PCT,
                                  math.log(HY_DECAY_TARGET) / HY_SHORT_DECAY_PCT, BRANCH_W, dtype=f32))
    filt = filt * jnp.exp(-t * deltas)[:, None, None, :]
    fwd, bwd = filt[:, :, 0], filt[:, :, 1]
    k = jnp.concatenate([fwd, jnp.zeros((1, HY_ORDER, BRANCH_W), f32), jnp.flip(bwd[1:], axis=0)], axis=0)
    k = k / jnp.sum(jnp.abs(k), axis=0, keepdims=True)
    return jnp.fft.rfft(k, axis=0)


def hyena_mixer(u, kf, bias):
    L = u.shape[1]
    parts = jnp.split(u, HY_ORDER + 1, axis=-1)
    z = parts[0]
    for o in range(HY_ORDER):
        conv = jnp.fft.irfft(jnp.fft.rfft(z, n=2 * L, axis=1) * kf[None, :, o], n=2 * L, axis=1)[:, :L]
        z = parts[o + 1] * (conv + bias[o] * z)
    return z


def _lin_combine(e1, e2):
    a1, b1 = e1
    a2, b2 = e2
    return a1 * a2, a2 * b1 + b2


def s5_mixer(u, lam_re, lam_im, log_step, b_re, b_im, c_re, c_im, d_skip, s0):
    f32 = jnp.float32
    bsz, L, _ = u.shape
    ug = u.reshape(bsz, L, S5_GROUPS, S5_GROUP_CH)
    y = d_skip.astype(f32).reshape(S5_GROUPS, S5_GROUP_CH) * ug
    s0 = s0.astype(f32)
    finals = []
    for d in range(2):
        lam = lax.complex(lam_re[d].astype(f32), lam_im[d].astype(f32))
        step = jnp.exp(log_step[d].astype(f32))[:, None]
        lam_bar = jnp.exp(lam * step)
        b_bar = ((lam_bar - 1.0) / lam)[..., None] * lax.complex(b_re[d].astype(f32), b_im[d].astype(f32))
        init = lax.complex(s0[:, d, :, :, 0], s0[:, d, :, :, 1])
        seq = ug if d == 0 else jnp.flip(ug, axis=1)
        bu = jnp.einsum('gph,blgh->blgp', b_bar, seq.astype(jnp.complex64))
        bu = bu.at[:, 0].add(lam_bar * init)
        a = jnp.broadcast_to(lam_bar, bu.shape)
        _, xs = lax.associative_scan(_lin_combine, (a, bu), axis=1)
        finals.append(xs[:, -1])
        yd = (jnp.einsum('ghp,blgp->blgh', c_re[d].astype(f32), xs.real)
              - jnp.einsum('ghp,blgp->blgh', c_im[d].astype(f32), xs.imag))
        y = y + (yd if d == 0 else jnp.flip(yd, axis=1))
    fin = jnp.stack(finals, axis=1)
    return y.reshape(bsz, L, S5_GROUPS * S5_GROUP_CH), jnp.stack([fin.real, fin.imag], axis=-1)


def chunk_gated_delta(q, k, v, g, beta, s0):
    bsz, L, h, _ = k.shape
    dv = v.shape[-1]
    C = GDN_CHUNK
    n = L // C

    def blk(t):
        return jnp.moveaxis(t.reshape((bsz, n, C, h) + t.shape[3:]), 3, 1)

    q, k, v, g, beta = blk(q), blk(k), blk(v), blk(g), blk(beta)
    gc = jnp.cumsum(g, axis=-1)
    lower = jnp.tril(jnp.ones((C, C), dtype=bool))
    strict = jnp.tril(jnp.ones((C, C), dtype=bool), -1)
    diff = gc[..., :, None] - gc[..., None, :]
    decay = jnp.where(lower, jnp.exp(jnp.where(lower, diff, 0.0)), 0.0)
    kb = k * beta[..., None]
    a_mat = jnp.where(strict, jnp.einsum('bhncd,bhnsd->bhncs', kb, k) * decay, 0.0)
    eye = jnp.eye(C, dtype=jnp.float32)
    t_mat = lax.linalg.triangular_solve(a_mat + eye, jnp.broadcast_to(eye, a_mat.shape),
                                        left_side=True, lower=True, unit_diagonal=True)
    u = jnp.einsum('bhncs,bhnsv->bhncv', t_mat, v * beta[..., None])
    w = jnp.einsum('bhncs,bhnsd->bhncd', t_mat, kb * jnp.exp(gc)[..., None])
    qk = jnp.where(lower, jnp.einsum('bhncd,bhnsd->bhncs', q, k) * decay, 0.0)

    def step(S, inp):
        q_i, k_i, u_i, w_i, g_i, qk_i = inp
        v_new = u_i - jnp.einsum('bhcd,bhdv->bhcv', w_i, S)
        o = (jnp.einsum('bhcd,bhdv->bhcv', q_i * jnp.exp(g_i)[..., None], S)
             + jnp.einsum('bhcs,bhsv->bhcv', qk_i, v_new))
        g_last = g_i[..., -1:]
        S = S * jnp.exp(g_last)[..., None] + jnp.einsum(
            'bhcd,bhcv->bhdv', k_i * jnp.exp(g_last - g_i)[..., None], v_new)
        return S, o

    xs = tuple(jnp.moveaxis(t, 2, 0) for t in (q, k, u, w, gc, qk))
    s_fin, o = lax.scan(step, s0, xs)
    o = jnp.moveaxis(jnp.moveaxis(o, 0, 2), 1, 3).reshape(bsz, L, h, dv)
    return o, s_fin


def gdn_mixer(qkv, beta_in, alpha_in, a_log, dt_bias, norm_g, s0):
    f32 = jnp.float32
    bsz, L, _ = qkv.shape
    q, k, v = jnp.split(qkv, 3, axis=-1)
    q = l2norm(q.reshape(bsz, L, GDN_HEADS, GDN_DK)) * (GDN_DK ** -0.5)
    k = l2norm(k.reshape(bsz, L, GDN_HEADS, GDN_DK))
    v = v.reshape(bsz, L, GDN_HEADS, GDN_DV)
    beta = jax.nn.sigmoid(beta_in.reshape(bsz, L, 2, GDN_HEADS))
    g = -jnp.exp(a_log.astype(f32)) * jax.nn.softplus(alpha_in.reshape(bsz, L, 2, GDN_HEADS) + dt_bias.astype(f32))
    s0 = s0.astype(f32)
    o_f, s_f = chunk_gated_delta(q, k, v, g[:, :, 0], beta[:, :, 0], s0[:, 0])
    rev = lambda t: jnp.flip(t, axis=1)
    o_b, s_b = chunk_gated_delta(rev(q), rev(k), rev(v), rev(g[:, :, 1]), rev(beta[:, :, 1]), s0[:, 1])
    o = rmsnorm(o_f + rev(o_b), norm_g.astype(f32))
    return o.reshape(bsz, L, GDN_HEADS * GDN_DV), jnp.stack([s_f, s_b], axis=1)


def trunk_layer(x, mod, s0_gdn, s0_s5, lp):
    f32 = jnp.float32
    bsz, L, _ = x.shape
    shift, scale, gate = jnp.split(mod, 3, axis=-1)
    h = rmsnorm(x, lp['norm_g']) * (1.0 + scale) + shift
    z = h @ lp['w_in']
    cuts = np.cumsum(IN_SIZES)[:-1].tolist()
    hy_in, hy_gate, s5_in, s5_gate, gdn_in, gdn_beta, gdn_alpha, gdn_gate, merge_in = jnp.split(z, cuts, axis=-1)
    hy_u = dwconv_centred(hy_in, lp['hy_conv_w']) + lp['hy_conv_b']
    kf = hyena_filter_spectra(L, lp['hy_f_w1'], lp['hy_f_b1'], lp['hy_f_w2'], lp['hy_f_b2'],
                              lp['hy_f_w3'], lp['hy_f_freq'])
    y_hy = hyena_mixer(hy_u.astype(f32), kf, lp['hy_bias'].astype(f32)) * jax.nn.silu(hy_gate.astype(f32))
    s5_raw, s5_fin = s5_mixer(s5_in.astype(f32), lp['s5_lambda_re'], lp['s5_lambda_im'], lp['s5_log_step'],
                              lp['s5_B_re'], lp['s5_B_im'], lp['s5_C_re'], lp['s5_C_im'], lp['s5_D'], s0_s5)
    glu = jax.nn.gelu(s5_raw) @ lp['s5_glu_w'].astype(f32) + lp['s5_glu_b'].astype(f32)
    glu_a, glu_g = jnp.split(glu, 2, axis=-1)
    y_s5 = glu_a * jax.nn.sigmoid(glu_g) * jax.nn.silu(s5_gate.astype(f32))
    qkv = jax.nn.silu(dwconv_centred(gdn_in, lp['gdn_conv_w']).astype(f32))
    o_gdn, gdn_fin = gdn_mixer(qkv, gdn_beta.astype(f32), gdn_alpha.astype(f32), lp['gdn_A_log'],
                               lp['gdn_dt_bias'], lp['gdn_norm_g'], s0_gdn)
    y_gdn = o_gdn * jax.nn.silu(gdn_gate.astype(f32))
    ys = jnp.stack([y_hy, y_s5, y_gdn], axis=2).astype(x.dtype)
    proj = jnp.einsum('blnw,nwd->blnd', ys, lp['w_branch'])
    gates = jax.nn.sigmoid(merge_in.reshape(bsz, L, N_BRANCH, D_MODEL))
    merged = jnp.sum(gates * proj, axis=2)
    x = x + gate * (merged @ lp['w_out'])
    return x, gdn_fin, s5_fin


def setup_inputs(seed: int = 0) -> dict:
    key = jax.random.key(seed)
    ks = iter(jax.random.split(key, 48))
    f32 = jnp.float32
    W, H, G, P = BRANCH_W, GDN_HEADS, S5_GROUPS, S5_STATE

    def nrm(shape, s):
        return s * jax.random.normal(next(ks), shape, f32)

    n_idx = jnp.arange(P, dtype=f32)
    gdn_dt = jnp.exp(jax.random.uniform(next(ks), (DEPTH, 2, H), f32, math.log(1e-3), math.log(1e-1)))
    return {
        'x_prompt': nrm((BATCH, SEQ, D_MODEL), 1.0),
        'x_sample': nrm((DEC_BATCH, DEC_SEQ, D_MODEL), 1.0),
        'c': nrm((DEC_BATCH, D_MODEL), 1.0),
        'c_ctx': nrm((D_MODEL,), 1.0),
        'state_gdn': nrm((DEC_BATCH, DEPTH, 2, H, GDN_DK, GDN_DV), 0.1),
        'state_s5': nrm((DEC_BATCH, DEPTH, 2, G, P, 2), 0.05),
        'w_mod': nrm((DEPTH, D_MODEL, 3 * D_MODEL), 0.5 * D_MODEL ** -0.5),
        'b_mod': nrm((DEPTH, 3 * D_MODEL), 0.01),
        'norm_g': 1.0 + nrm((DEPTH, D_MODEL), 0.01),
        'w_in': nrm((DEPTH, D_MODEL, IN_COLS), D_MODEL ** -0.5),
        'hy_conv_w': nrm((DEPTH, HY_CONV, (HY_ORDER + 1) * W), HY_CONV ** -0.5),
        'hy_conv_b': nrm((DEPTH, (HY_ORDER + 1) * W), 0.01),
        'hy_f_w1': nrm((DEPTH, HY_EMB, HY_FILTER_HIDDEN), HY_EMB ** -0.5),
        'hy_f_b1': nrm((DEPTH, HY_FILTER_HIDDEN), 0.1),
        'hy_f_w2': nrm((DEPTH, HY_FILTER_HIDDEN, HY_FILTER_HIDDEN), HY_FILTER_HIDDEN ** -0.5),
        'hy_f_b2': nrm((DEPTH, HY_FILTER_HIDDEN), 0.1),
        'hy_f_w3': nrm((DEPTH, HY_FILTER_HIDDEN, HY_ORDER * 2 * W), HY_FILTER_HIDDEN ** -0.5),
        'hy_f_freq': 1.0 + nrm((DEPTH, HY_FILTER_HIDDEN), 0.1),
        'hy_bias': nrm((DEPTH, HY_ORDER, W), 0.5),
        's5_lambda_re': -0.5 + nrm((DEPTH, 2, G, P), 0.01),
        's5_lambda_im': math.pi * n_idx + nrm((DEPTH, 2, G, P), 0.01),
        's5_log_step': jax.random.uniform(next(ks), (DEPTH, 2, G), f32, math.log(1e-3), math.log(1e-1)),
        's5_B_re': nrm((DEPTH, 2, G, P, S5_GROUP_CH), (2 * S5_GROUP_CH) ** -0.5),
        's5_B_im': nrm((DEPTH, 2, G, P, S5_GROUP_CH), (2 * S5_GROUP_CH) ** -0.5),
        's5_C_re': nrm((DEPTH, 2, G, S5_GROUP_CH, P), (2 * P) ** -0.5),
        's5_C_im': nrm((DEPTH, 2, G, S5_GROUP_CH, P), (2 * P) ** -0.5),
        's5_D': nrm((DEPTH, W), 1.0),
        's5_glu_w': nrm((DEPTH, W, 2 * W), W ** -0.5),
        's5_glu_b': nrm((DEPTH, 2 * W), 0.01),
        'gdn_conv_w': nrm((DEPTH, GDN_CONV, 3 * W), GDN_CONV ** -0.5),
        'gdn_A_log': jnp.log(jax.random.uniform(next(ks), (DEPTH, 2, H), f32, 1.0, 16.0)),
        'gdn_dt_bias': gdn_dt + jnp.log(-jnp.expm1(-gdn_dt)),
        'gdn_norm_g': 1.0 + nrm((DEPTH, GDN_DV), 0.01),
        'w_branch': nrm((DEPTH, N_BRANCH, W, D_MODEL), W ** -0.5),
        'w_out': nrm((DEPTH, D_MODEL, D_MODEL), D_MODEL ** -0.5),
        'final_norm_g': 1.0 + nrm((D_MODEL,), 0.01),
    }


def reference(x_prompt, x_sample, c, c_ctx, state_gdn, state_s5, w_mod, b_mod, norm_g, w_in,
              hy_conv_w, hy_conv_b, hy_f_w1, hy_f_b1, hy_f_w2, hy_f_b2, hy_f_w3, hy_f_freq, hy_bias,
              s5_lambda_re, s5_lambda_im, s5_log_step, s5_B_re, s5_B_im, s5_C_re, s5_C_im, s5_D,
              s5_glu_w, s5_glu_b, gdn_conv_w, gdn_A_log, gdn_dt_bias, gdn_norm_g, w_branch, w_out,
              final_norm_g):
    bp = x_prompt.shape[0]
    xc = x_prompt
    xl = x_sample + grid_pos_embed(x_sample.shape[1], D_MODEL).astype(x_sample.dtype)
    zero_gdn = jnp.zeros((bp, 2, GDN_HEADS, GDN_DK, GDN_DV), jnp.float32)
    zero_s5 = jnp.zeros((bp, 2, S5_GROUPS, S5_STATE, 2), jnp.float32)
    gdn_states = []
    s5_states = []
    for l in range(DEPTH):
        lp = dict(norm_g=norm_g[l], w_in=w_in[l], hy_conv_w=hy_conv_w[l], hy_conv_b=hy_conv_b[l],
                  hy_f_w1=hy_f_w1[l], hy_f_b1=hy_f_b1[l], hy_f_w2=hy_f_w2[l], hy_f_b2=hy_f_b2[l],
                  hy_f_w3=hy_f_w3[l], hy_f_freq=hy_f_freq[l], hy_bias=hy_bias[l],
                  s5_lambda_re=s5_lambda_re[l], s5_lambda_im=s5_lambda_im[l], s5_log_step=s5_log_step[l],
                  s5_B_re=s5_B_re[l], s5_B_im=s5_B_im[l], s5_C_re=s5_C_re[l], s5_C_im=s5_C_im[l],
                  s5_D=s5_D[l], s5_glu_w=s5_glu_w[l], s5_glu_b=s5_glu_b[l], gdn_conv_w=gdn_conv_w[l],
                  gdn_A_log=gdn_A_log[l], gdn_dt_bias=gdn_dt_bias[l], gdn_norm_g=gdn_norm_g[l],
                  w_branch=w_branch[l], w_out=w_out[l])
        mod_ctx = (jax.nn.silu(c_ctx) @ w_mod[l] + b_mod[l])[None, None, :]
        mod_lat = (jax.nn.silu(c) @ w_mod[l] + b_mod[l])[:, None, :]
        xc, fin_gdn, fin_s5 = trunk_layer(xc, mod_ctx, zero_gdn, zero_s5, lp)
        gdn_states.append(fin_gdn)
        s5_states.append(fin_s5)
        xl, _, _ = trunk_layer(xl, mod_lat, state_gdn[:, l], state_s5[:, l], lp)
    y_prompt = rmsnorm(xc, final_norm_g)
    y_sample = rmsnorm(xl, final_norm_g)
    new_state_gdn = jnp.stack(gdn_states, axis=1)
    new_state_s5 = jnp.stack(s5_states, axis=1)
    return (y_prompt, y_sample, new_state_gdn, new_state_s5)
```

```python
import numpy as np
import concourse.bass as bass
import concourse.mybir as mybir
from concourse.alu_op_type import AluOpType as ALU

F32 = mybir.dt.float32
BF16 = mybir.dt.bfloat16
AF = mybir.ActivationFunctionType
import os as _os0
FP32R = _os0.environ.get("FP32R", "0") == "1"
LOADQ = _os0.environ.get("LOADQ", "act")


class Prog:
    NDMA = 24

    def __init__(self, nc, same_engine_sync=True):
        self.nc = nc
        self.eng = {"pe": nc.tensor, "act": nc.scalar, "dve": nc.vector, "pool": nc.gpsimd, "sp": nc.sync}
        self.stack = []
        self.sem = {}
        for e in self.eng:
            cm = nc.semaphore("s_" + e)
            self.sem[e] = cm.__enter__()
            self.stack.append(cm)
        self.cnt = {e: 0 for e in self.eng}
        self.dsem = []
        for i in range(self.NDMA):
            cm = nc.semaphore("d_%d" % i)
            self.dsem.append(cm.__enter__())
            self.stack.append(cm)
        self.dval = [0] * self.NDMA
        self.dnext = 0
        self.seen = {e: {} for e in self.eng}
        self.recs = {}
        self.same = same_engine_sync
        self.out_dmas = []
        self.n_inst = 0
        self.tensors = {}
        self.ro = set()

    def sb(self, name, shape, dt=F32):
        self.uid = getattr(self, "uid", 0) + 1
        if not hasattr(self, "names"):
            self.names = {}
        self.names.setdefault(name, []).append("%s_%d" % (name, self.uid))
        name = "%s_%d" % (name, self.uid)
        cm = self.nc.sbuf_tensor(name, list(shape), dt)
        t = cm.__enter__()
        self.stack.append(cm)
        return t

    def ps(self, name, shape, dt=F32):
        cm = self.nc.psum_tensor(name, list(shape), dt)
        t = cm.__enter__()
        self.stack.append(cm)
        return t

    @staticmethod
    def region(ap):
        t = ap.tensor
        name = t.name
        shp = list(t.shape)
        space = str(ap.space)
        off = int(ap.offset)
        dims = [(int(s), int(c)) for (s, c) in ap.ap]
        if "DRAM" in space.upper() or "HBM" in space.upper() or type(t).__name__.startswith("DRam"):
            lo = off
            hi = off
            for s, c in dims:
                if s >= 0:
                    hi += s * (c - 1)
                else:
                    lo += s * (c - 1)
            return (name, 0, 1, lo, hi)
        if name.startswith("ps") and "PSUM" in space.upper():
            return (name, 0, 128, 0, 1 << 30)
        pitch = 1
        for d in shp[1:]:
            pitch *= int(d)
        p0 = off // pitch
        f0 = off % pitch
        ps_, pc = dims[0]
        if ps_ == 0:
            p1 = p0 + 1
        else:
            p1 = p0 + pc * max(1, ps_ // pitch)
        lo = f0
        hi = f0
        for s, c in dims[1:]:
            if s >= 0:
                hi += s * (c - 1)
            else:
                lo += s * (c - 1)
        return (name, p0, p1, lo, hi)

    def _deps(self, reads, writes):
        deps = set()
        for ap in reads:
            name, p0, p1, lo, hi = self.region(ap)
            for r in self.recs.get(name, ()):
                if r[0] < p1 and p0 < r[1] and r[2] <= hi and lo <= r[3]:
                    if r[4] is not None:
                        deps.add(r[4])
        for ap in writes:
            name, p0, p1, lo, hi = self.region(ap)
            for r in self.recs.get(name, ()):
                if r[0] < p1 and p0 < r[1] and r[2] <= hi and lo <= r[3]:
                    if r[4] is not None:
                        deps.add(r[4])
                    for rd in r[5].items():
                        deps.add(rd)
        return deps

    def _record(self, reads, writes, me):
        for ap in reads:
            name, p0, p1, lo, hi = self.region(ap)
            if name in self.ro:
                continue
            lst = self.recs.setdefault(name, [])
            found = False
            for r in lst:
                if r[0] == p0 and r[1] == p1 and r[2] == lo and r[3] == hi:
                    if r[5].get(me[0], 0) < me[1]:
                        r[5][me[0]] = me[1]
                    found = True
            if not found:
                lst.append([p0, p1, lo, hi, None, {me[0]: me[1]}])
        for ap in writes:
            name, p0, p1, lo, hi = self.region(ap)
            lst = self.recs.setdefault(name, [])
            keep = []
            for r in lst:
                if p0 <= r[0] and r[1] <= p1 and lo <= r[2] and r[3] <= hi:
                    continue
                keep.append(r)
            keep.append([p0, p1, lo, hi, me, {}])
            self.recs[name] = keep

    def _wait(self, e, deps):
        eng = self.eng[e]
        best = {}
        for (src, val) in deps:
            if src == e and (not self.same or e == "pe"):
                continue
            if val > best.get(src, 0):
                best[src] = val
        for src, val in best.items():
            if self.seen[e].get(src, 0) >= val:
                continue
            self.seen[e][src] = val
            sem = self.sem[src] if isinstance(src, str) else self.dsem[src]
            eng.wait_ge(sem, val)

    def op(self, e, fn, reads, writes):
        deps = self._deps(reads, writes)
        self._wait(e, deps)
        inst = fn(self.eng[e])
        self.cnt[e] += 1
        inst.then_inc(self.sem[e], 1)
        self._record(reads, writes, (e, self.cnt[e]))
        self.n_inst += 1
        return inst

    def dma(self, out, in_, q=None, is_output=False, **kw):
        if q is None:
            is_load = not type(out.tensor).__name__.startswith("DRam")
            q = LOADQ if is_load else "sp"
        deps = self._deps([in_], [out])
        k = self.dnext
        self.dnext = (self.dnext + 1) % self.NDMA
        if self.dval[k] > 0:
            deps.add((k, self.dval[k]))
        self._wait(q, deps)
        inst = self.eng[q].dma_start(out=out, in_=in_, **kw)
        self.dval[k] += 16
        inst.then_inc(self.dsem[k], 16)
        me = (k, self.dval[k])
        self._record([in_], [out], me)
        if is_output:
            self.out_dmas.append(me)
        self.n_inst += 1
        return inst

    def finish(self):
        e = "sp"
        self._wait(e, set(self.out_dmas) | {(k, v) for k, v in enumerate(self.dval) if v > 0}
                   | {(x, self.cnt[x]) for x in self.eng if x != e and self.cnt[x] > 0})

    def barrier(self):
        allw = {(x, self.cnt[x]) for x in self.eng if self.cnt[x] > 0} | {(k, v) for k, v in enumerate(self.dval) if v > 0}
        for e in self.eng:
            self._wait(e, set(allw))

    def mark(self):
        return len(self.stack)

    def release(self, m):
        self.barrier()
        while len(self.stack) > m:
            cm = self.stack.pop()
            cm.__exit__(None, None, None)

    def close(self):
        for cm in reversed(self.stack):
            cm.__exit__(None, None, None)
        self.stack = []

    def mm(self, out, lhsT, rhs, start=True, stop=True, **kw):
        if FP32R and lhsT.dtype == F32 and rhs.dtype == F32 and int(rhs.ap[-1][1]) % 2 == 0:
            lhsT = lhsT.bitcast(mybir.dt.float32r)
            rhs = rhs.bitcast(mybir.dt.float32r)
        return self.op("pe", lambda g: g.matmul(out, lhsT, rhs, start=start, stop=stop, **kw), [lhsT, rhs], [out])

    def transpose(self, out, in_, ident):
        return self.op("pe", lambda g: g.transpose(out, in_, ident), [in_, ident], [out])

    def act(self, out, in_, func, bias=None, scale=None, e="act", extra_reads=()):
        kw = {}
        rd = [in_] + list(extra_reads)
        if bias is not None:
            kw["bias"] = bias
            if not isinstance(bias, (int, float)):
                rd.append(bias)
        if scale is not None:
            kw["scale"] = scale
            if not isinstance(scale, (int, float)):
                rd.append(scale)
        return self.op(e, lambda g: g.activation(out, in_, func, **kw), rd, [out])

    def tt(self, out, in0, in1, op, e="dve"):
        return self.op(e, lambda g: g.tensor_tensor(out, in0, in1, op), [in0, in1], [out])

    def ts(self, out, in0, s1, s2, op0, op1=None, e="dve", accum_out=None):
        rd = [in0]
        for s in (s1, s2):
            if s is not None and not isinstance(s, (int, float)):
                rd.append(s)
        wr = [out]
        kw = {}
        if accum_out is not None:
            kw["accum_out"] = accum_out
            wr.append(accum_out)
        if op1 is None:
            return self.op(e, lambda g: g.tensor_scalar(out, in0, s1, s2, op0, **kw), rd, wr)
        return self.op(e, lambda g: g.tensor_scalar(out, in0, s1, s2, op0, op1, **kw), rd, wr)

    def stt(self, out, in0, scalar, in1, op0, op1, e="dve"):
        rd = [in0, in1]
        if not isinstance(scalar, (int, float)):
            rd.append(scalar)
        return self.op(e, lambda g: g.scalar_tensor_tensor(out, in0, scalar, in1, op0, op1), rd, [out])

    def copy(self, out, in_, e="dve"):
        if e == "act":
            return self.op(e, lambda g: g.copy(out, in_), [in_], [out])
        return self.op(e, lambda g: g.tensor_copy(out, in_), [in_], [out])

    def memset(self, ap, val, e="pool"):
        return self.op(e, lambda g: g.memset(ap, val), [], [ap])

from concourse.bass_utils import run_bass_kernel_spmd
import ml_dtypes
import math

I32 = mybir.dt.int32
T = 2560
NTT = 5
D_MODEL = 1024
SEQS = [(0, 2048), (2048, 256), (2304, 256)]
C_HYG, C_S5, C_S5G, C_GDN, C_BETA, C_ALPHA, C_GDNG, C_MERGE = 3072, 4096, 5120, 6144, 9216, 9232, 9248, 10272
IN_COLS = 13344
MIX = {"hy": True, "s5": True, "gdn": True}
NLAYER = 2


class Ctx:
    pass


def build_program(dbg=None, stop=None):
    nc = bass.Bass("TRN2", target_bir_lowering=False)
    P = Prog(nc)
    K = Ctx()
    K.nc, K.P = nc, P
    K.dbg = dbg
    K.stop = stop

    def din(name, shape, dt=F32):
        P.ro.add(name)
        return nc.dram_tensor(name, list(shape), dt, kind="ExternalInput").ap()

    def dout(name, shape, dt=F32):
        return nc.dram_tensor(name, list(shape), dt, kind="ExternalOutput").ap()

    def dscr(name, shape, dt=F32):
        return nc.dram_tensor(name, list(shape), dt, kind="Internal").ap()

    K.din, K.dout, K.dscr = din, dout, dscr
    I = Ctx()
    K.I = I
    I.xs = din("xs", [2048, 1024]); I.xp = din("xp", [512, 1024]); I.cvec = din("cvec", [2, 1024])
    I.sg = din("sg", [2, 2, 8, 128, 128]); I.ss = din("ss", [2, 2, 64, 64, 2])
    I.w_mod = din("w_mod", [2, 1024, 3072]); I.b_mod = din("b_mod", [2, 3072]); I.norm_g = din("norm_g", [2, 1024])
    I.w_in = din("w_in", [2, 1024, IN_COLS])
    I.hy_conv_w = din("hy_conv_w", [2, 3, 3072]); I.hy_conv_b = din("hy_conv_b", [2, 3072])
    I.hy_f_w1 = din("hy_f_w1", [2, 33, 64]); I.hy_f_b1 = din("hy_f_b1", [2, 64])
    I.hy_f_w2 = din("hy_f_w2", [2, 64, 64]); I.hy_f_b2 = din("hy_f_b2", [2, 64])
    I.hy_f_w3 = din("hy_f_w3", [2, 64, 4096]); I.hy_f_freq = din("hy_f_freq", [2, 64]); I.hy_bias = din("hy_bias", [2, 2, 1024])
    I.s5_lambda_re = din("s5_lambda_re", [2, 2, 64, 64]); I.s5_lambda_im = din("s5_lambda_im", [2, 2, 64, 64])
    I.s5_log_step = din("s5_log_step", [2, 2, 64])
    I.s5_B_re = din("s5_B_re", [2, 2, 64, 64, 16]); I.s5_B_im = din("s5_B_im", [2, 2, 64, 64, 16])
    I.s5_C_re = din("s5_C_re", [2, 2, 64, 16, 64]); I.s5_C_im = din("s5_C_im", [2, 2, 64, 16, 64])
    I.s5_D = din("s5_D", [2, 1024]); I.s5_glu_w = din("s5_glu_w", [2, 1024, 2048]); I.s5_glu_b = din("s5_glu_b", [2, 2048])
    I.gdn_conv_w = din("gdn_conv_w", [2, 3, 3072]); I.gdn_A_log = din("gdn_A_log", [2, 2, 8]); I.gdn_dt_bias = din("gdn_dt_bias", [2, 2, 8])
    I.gdn_norm_g = din("gdn_norm_g", [2, 128]); I.w_branch = din("w_branch", [2, 3, 1024, 1024]); I.w_out = din("w_out", [2, 1024, 1024])
    I.final_norm_g = din("final_norm_g", [1024])
    I.ident = din("ident", [128, 128]); I.pos = din("pos", [2048, 1024])
    O = Ctx()
    K.O = O
    O.ys = dout("ys", [2048, 1024]); O.yp = dout("yp", [512, 1024])
    O.ng = dout("ng", [2, 2, 2, 8, 128, 128]); O.ns = dout("ns", [2, 2, 2, 64, 64, 2])
    if dbg:
        K.dbg_out = dout("dbg", list(dbg), BF16)
        K.dbg32 = dout("dbg32", list(dbg))
    K.xT = dscr("xT_scr", [1024, T])
    K.xTv = K.xT.rearrange("(kc k) t -> k kc t", k=128)

    with nc.allow_non_contiguous_dma("layout"), nc.allow_low_precision("bf16 matmul per problem tolerance"):
        K.ps = [P.ps("ps%d" % i, [128, 512]) for i in range(8)]
        K.ident = P.sb("ident_sb", [128, 128]); P.dma(K.ident[:], I.ident)
        K.ones = P.sb("ones", [128, 128]); P.memset(K.ones[:], 1.0)
        K.wst = [P.sb("wst0", [128, 8, 128])] * 2
        K.wbf = [P.sb("wbf0", [128, 8, 128], BF16)] * 2
        K.eps = P.sb("eps", [128, 1]); P.memset(K.eps[:], 1e-6)
        K.one1 = P.sb("one1", [128, 1]); P.memset(K.one1[:], 1.0)
        K.wi = 0
        K.MG = dscr("mg_scr", [1024, T])
        K.MGv = K.MG.rearrange("(kc k) t -> k kc t", k=128)
        phase_mod(K)
        prep_all(K)
        K.h = P.sb("h", [128, 8, T], BF16)
        phase_input(K)
        for l in range(NLAYER):
            phase_norm(K, l)
            mk = P.mark()
            K.yb = P.sb("yb", [128, 8, T], BF16)
            for n, nm in enumerate(["hy", "s5", "gdn"]):
                if MIX[nm]:
                    {"hy": mixer_hyena, "s5": mixer_s5, "gdn": mixer_gdn}[nm](K, l)
                else:
                    mixer_stub(K, l, n)
                if K.stop == (nm, l):
                    for kc in range(8):
                        P.dma(K.dbg_out[kc * 128:(kc + 1) * 128, :], K.yb[:, kc, :], is_output=True)
                    P.finish(); P.close()
                    return nc, K
                phase_merge(K, l, n)
            P.release(mk)
            phase_out(K, l)
            if K.stop == ("out", l):
                mk_ = P.mark()
                xt_ = P.sb("dbgx", [128, 8, 512])
                for tt in range(NTT):
                    P.dma(xt_[:], K.xTv[:, :, tt * 512:(tt + 1) * 512])
                    P.dma(K.dbg32.rearrange("(kc k) t -> k kc t", k=128)[:, :, tt * 512:(tt + 1) * 512], xt_[:], is_output=True)
                P.finish(); P.close()
                return nc, K
        phase_final(K)
        P.finish()
    P.close()
    return nc, K


def load_w(K, src, ncols, eng="pool"):
    P = K.P
    i = K.wi
    K.wi ^= 1
    st = K.wst[i][:, :, 0:ncols]
    P.dma(st, src.rearrange("(kc k) c -> k kc c", k=128))
    bf = K.wbf[i][:, :, 0:ncols]
    P.copy(bf, st, e=eng)
    return bf


def proj(K, ps, wbf, c0, ncols, tt, act=None, ntok=512, t0=None):
    P = K.P
    if t0 is None:
        t0 = tt * 512
    src = K.h if act is None else act
    for kc in range(8):
        P.mm(ps, wbf[:, kc, c0:c0 + ncols], src[:, kc, t0:t0 + ntok], start=(kc == 0), stop=(kc == 7))


def phase_mod(K):
    P, I = K.P, K.I
    K.modT = P.sb("modT", [128, 2, 24, 2])
    K.gs = P.sb("gs", [128, 2, 8, 2])
    K.fg = P.sb("fgT", [128, 8]); P.dma(K.fg[:], I.final_norm_g.rearrange("(kc k) -> k kc", k=128))
    mk = P.mark()
    cT = P.sb("cT", [128, 8, 2]); [P.dma(cT[:, :, w], I.cvec[w].rearrange("(kc k) -> k kc", k=128)) for w in range(2)]
    sc = P.sb("sc", [128, 8, 2]); P.act(sc[:], cT[:], AF.Silu)
    bmodT = P.sb("bmodT", [128, 2, 24]); [P.dma(bmodT[:, l, :], I.b_mod[l].rearrange("(ft f) -> f ft", f=128)) for l in range(2)]
    wm = [P.sb("wm%d" % i, [128, 8, 512]) for i in range(2)]
    for l in range(NLAYER):
        for g6 in range(6):
            w = wm[g6 % 2]
            P.dma(w[:], I.w_mod[l].rearrange("(kc k) c -> k kc c", k=128)[:, :, g6 * 512:(g6 + 1) * 512])
            for f4 in range(4):
                ft = g6 * 4 + f4
                ps = K.ps[ft % 8][:, 0:2]
                for kc in range(8):
                    P.mm(ps, w[:, kc, f4 * 128:(f4 + 1) * 128], sc[:, kc, :], start=(kc == 0), stop=(kc == 7))
                P.ts(K.modT[:, l, ft, :], ps, bmodT[:, l, ft:ft + 1], None, ALU.add)
    ng = P.sb("normgT", [128, 2, 8]); [P.dma(ng[:, l, :], I.norm_g[l].rearrange("(kc k) -> k kc", k=128)) for l in range(2)]
    for l in range(NLAYER):
        for w in range(2):
            P.ts(K.gs[:, l, :, w], K.modT[:, l, 8:16, w], 1.0, None, ALU.add)
            P.tt(K.gs[:, l, :, w], K.gs[:, l, :, w], ng[:, l, :], ALU.mult)
    P.release(mk)


def phase_input(K):
    P, I = K.P, K.I
    mk = P.mark()
    xt = [P.sb("xin%d" % i, [128, 1024]) for i in range(2)]
    pt = [P.sb("pin%d" % i, [128, 1024]) for i in range(2)]
    xo = [P.sb("xo%d" % i, [128, 8, 128]) for i in range(2)]
    for tb in range(20):
        x = xt[tb % 2]
        if tb < 16:
            P.dma(x[:], I.xs[tb * 128:(tb + 1) * 128, :])
            p = pt[tb % 2]
            P.dma(p[:], I.pos[tb * 128:(tb + 1) * 128, :])
            P.tt(x[:], x[:], p[:], ALU.add)
        else:
            P.dma(x[:], I.xp[(tb - 16) * 128:(tb - 15) * 128, :])
        o = xo[tb % 2]
        for half in range(2):
            ps = K.ps[(tb * 2 + half) % 8]
            for q in range(4):
                kc = half * 4 + q
                P.transpose(ps[:, q * 128:(q + 1) * 128], x[:, kc * 128:(kc + 1) * 128], K.ident[:])
            P.copy(o[:, half * 4:(half + 1) * 4, :], ps[:].rearrange("p (q t) -> p q t", q=4), e=("act" if half else "dve"))
        P.dma(K.xTv[:, :, tb * 128:(tb + 1) * 128], o[:])
    P.release(mk)


def norm_tile(K, tt, xtile, sq, r, psS):
    P = K.P
    P.dma(xtile[:], K.xTv[:, :, tt * 512:(tt + 1) * 512])
    P.act(sq[:], xtile[:], AF.Square)
    for kc in range(8):
        P.mm(psS, K.ones[:], sq[:, kc, :], start=(kc == 0), stop=(kc == 7))
    P.act(r[:], psS, AF.Sqrt, bias=K.eps[:, 0:1], scale=1.0 / 1024.0)
    P.op("dve", lambda g: g.reciprocal(r[:], r[:]), [r[:]], [r[:]])


def phase_norm(K, l):
    P = K.P
    mk = P.mark()
    norm_alloc(K)
    for tt in range(NTT):
        x = K.nx[tt % 2]; r = K.nr[tt % 2]
        norm_tile(K, tt, x, K.nsq, r, K.ps[tt % 8][:])
        w = 1 if tt < 4 else 0
        for kc in range(8):
            P.tt(K.ntmp[:, kc, :], x[:, kc, :], r[:], ALU.mult)
            P.ts(K.h[:, kc, tt * 512:(tt + 1) * 512], K.ntmp[:, kc, :], K.gs[:, l, kc, w:w + 1], K.modT[:, l, kc, w:w + 1],
                 ALU.mult, ALU.add, e=("pool" if kc % 2 else "dve"))
    P.release(mk)


def norm_alloc(K):
    P = K.P
    K.nx = [P.sb("nx0", [128, 8, 512])] * 2
    K.nsq = P.sb("nsq", [128, 8, 512])
    K.nr = [P.sb("nr%d" % i, [128, 512]) for i in range(2)]
    K.ntmp = K.nsq


def mixer_stub(K, l, n):
    P = K.P
    for kc in range(8):
        P.copy(K.yb[:, kc, :], K.h[:, kc, :], e=("pool" if kc % 2 else "dve"))


def phase_merge(K, l, n):
    P, I = K.P, K.I
    mk = P.mark()
    msig = [P.sb("msig%d" % i, [128, 512]) for i in range(2)]
    mpr = [P.sb("mpr%d" % i, [128, 512]) for i in range(2)]
    mold = [P.sb("mold%d" % i, [128, 512]) for i in range(2)]
    wbl = [P.sb("wbl%d" % i, [128, 8, 128], BF16) for i in range(2)]
    it = 0
    for ct in range(8):
        wb0 = load_w(K, I.w_branch[l, n][:, ct * 128:(ct + 1) * 128], 128)
        wb = wbl[ct % 2][:]
        P.copy(wb, wb0, e="pool")
        wm = load_w(K, I.w_in[l][:, C_MERGE + n * 1024 + ct * 128: C_MERGE + n * 1024 + (ct + 1) * 128], 128)
        for tt in range(NTT):
            psp = K.ps[(it * 2) % 8][:]; psg = K.ps[(it * 2 + 1) % 8][:]
            sig = msig[it % 2]; pr = mpr[it % 2]; old = mold[it % 2]
            it += 1
            dst = K.MGv[:, ct, tt * 512:(tt + 1) * 512]
            if n > 0:
                P.dma(old[:], dst)
            proj(K, psg, wm, 0, 128, tt)
            proj(K, psp, wb, 0, 128, tt, act=K.yb)
            P.act(sig[:], psg, AF.Sigmoid)
            P.tt(pr[:], psp, sig[:], ALU.mult)
            if n > 0:
                P.tt(pr[:], pr[:], old[:], ALU.add, e="pool")
            P.dma(dst, pr[:])
    P.release(mk)


def phase_out(K, l):
    P, I = K.P, K.I
    mk = P.mark()
    ox = [P.sb("ox%d" % i, [128, 512]) for i in range(2)]
    wo = P.sb("wo_res", [128, 8, 1024], BF16)
    mg = P.sb("mg_in", [128, 8, 512])
    mgb = [P.sb("mg_bf%d" % i, [128, 8, 512], BF16) for i in range(2)]
    for c4 in range(8):
        w = load_w(K, I.w_out[l][:, c4 * 128:(c4 + 1) * 128], 128)
        P.copy(wo[:, :, c4 * 128:(c4 + 1) * 128], w, e="pool")
    it = 0
    for tt in range(NTT):
        P.dma(mg[:], K.MGv[:, :, tt * 512:(tt + 1) * 512])
        m = mgb[tt % 2]
        P.copy(m[:], mg[:], e="act")
        for ct in range(8):
            ps = K.ps[it % 8][:]
            x = ox[it % 2]
            it += 1
            P.dma(x[:], K.xTv[:, ct, tt * 512:(tt + 1) * 512])
            for kc in range(8):
                P.mm(ps, wo[:, kc, ct * 128:(ct + 1) * 128], m[:, kc, :], start=(kc == 0), stop=(kc == 7))
            w = 1 if tt < 4 else 0
            P.stt(x[:], ps, K.modT[:, l, 16 + ct, w:w + 1], x[:], ALU.mult, ALU.add)
            P.dma(K.xTv[:, ct, tt * 512:(tt + 1) * 512], x[:])
    P.release(mk)


def phase_final(K):
    P, O = K.P, K.O
    mk = P.mark()
    norm_alloc(K)
    yo = [P.sb("yo%d" % i, [128, 1024]) for i in range(2)]
    it = 0
    for tt in range(NTT):
        x = K.nx[tt % 2]; r = K.nr[tt % 2]
        norm_tile(K, tt, x, K.nsq, r, K.ps[tt % 8][:])
        for kc in range(8):
            P.tt(K.ntmp[:, kc, :], x[:, kc, :], r[:], ALU.mult)
            P.ts(K.ntmp[:, kc, :], K.ntmp[:, kc, :], K.fg[:, kc:kc + 1], None, ALU.mult, e=("pool" if kc % 2 else "dve"))
        for b in range(4):
            y = yo[it % 2]
            for half in range(2):
                ps = K.ps[(it * 2 + half) % 8]
                for q in range(4):
                    kc = half * 4 + q
                    P.transpose(ps[:, q * 128:(q + 1) * 128], K.ntmp[:, kc, b * 128:(b + 1) * 128], K.ident[:])
                P.copy(y[:, half * 512:(half + 1) * 512], ps[:], e=("act" if half else "dve"))
            it += 1
            t0 = tt * 512 + b * 128
            if t0 < 2048:
                P.dma(O.ys[t0:t0 + 128, :], y[:], is_output=True)
            else:
                P.dma(O.yp[t0 - 2048:t0 - 2048 + 128, :], y[:], is_output=True)


HY_L = (2048, 256)


def _dft_consts(L):
    N = 2 * L
    t = np.arange(L, dtype=np.int64)[:, None]
    j = np.arange(N, dtype=np.int64)[None, :]
    f = np.where(j < L, j, j - L)
    ang = 2.0 * np.pi * ((t * f) % N).astype(np.float64) / N
    Dm = np.where(j < L, np.cos(ang), np.sin(ang))
    Dm[:, L] = np.where(np.arange(L) % 2 == 0, 1.0, -1.0)
    wg = np.full(N, 2.0 / N)
    wg[0] = 1.0 / N
    wg[L] = 1.0 / N
    return Dm, wg


def extra_consts():
    c = {}
    f32 = np.float32
    for L in HY_L:
        Dm, wg = _dft_consts(L)
        NTB_, NJ_ = L // 128, 2 * L // 128
        c["dft%d" % L] = np.ascontiguousarray(Dm.reshape(NTB_, 128, NJ_, 128).transpose(2, 1, 0, 3).reshape(NJ_, 128, NTB_ * 128).astype(ml_dtypes.bfloat16))
        c["dftT%d" % L] = np.ascontiguousarray(Dm.T.astype(ml_dtypes.bfloat16))
        c["wgt%d" % L] = np.ascontiguousarray(wg.astype(f32).reshape(-1, 128).T)
        t = np.linspace(0.0, 1.0, L, dtype=f32)[:, None]
        bands = 16
        fb = np.linspace(1e-4, bands - 1, bands, dtype=f32)[None, :]
        wpos = (f32(2.0 * math.pi) * np.arange(L, dtype=f32)[:, None] / f32(L)).astype(f32)
        z = np.concatenate([t, np.cos(fb * wpos), -np.sin(fb * wpos)], axis=-1).astype(f32)
        c["zfT%d" % L] = np.ascontiguousarray(z.T)
        c["negt%d" % L] = np.ascontiguousarray((-t[:, 0]).reshape(-1, 128).T.astype(f32))
    deltas = np.abs(np.linspace(math.log(1e-2) / 1.5, math.log(1e-2) / 0.3, 1024, dtype=f32))
    c["deltas"] = np.ascontiguousarray(np.broadcast_to(deltas[None, :], (128, 1024)).astype(f32))
    c.update(s5_host_consts())
    c.update(gdn_host_consts())
    return c


def sin_rr(K, out, x, tmp_i, tmp_f):
    P = K.P
    P.ts(tmp_i, x, 1.0 / (2.0 * math.pi), None, ALU.mult)
    P.copy(tmp_f, tmp_i)
    P.stt(tmp_f, tmp_f, -2.0 * math.pi, x, ALU.mult, ALU.add)
    P.ts(tmp_f, tmp_f, 3.1415925, -3.1415925, ALU.min, ALU.max)
    P.act(out, tmp_f, AF.Sin)


def prep_all(K):
    P, I = K.P, K.I
    din, dscr = K.din, K.dscr
    I.dft = {L: din("dft%d" % L, [2 * L // 128, 128, L], BF16) for L in HY_L}
    I.dftT = {L: din("dftT%d" % L, [2 * L, L], BF16) for L in HY_L}
    I.wgt = {L: din("wgt%d" % L, [128, 2 * L // 128]) for L in HY_L}
    I.zfT = {L: din("zfT%d" % L, [33, L]) for L in HY_L}
    I.negt = {L: din("negt%d" % L, [128, L // 128]) for L in HY_L}
    I.deltas = din("deltas", [128, 1024])
    for nm in ("g_L", "g_Ls", "g_U", "g_Us", "g_bd", "g_nbd"):
        setattr(I, nm, din(nm, [128, 128]))
    I.s5_bm = din("s5_bm", [128, 128]); I.s5_bmg = din("s5_bmg", [128, 8]); I.s5_mask2 = din("s5_mask2", [128, 32])
    K.KH = {(l, L): dscr("kh_%d_%d" % (l, L), [2 * L, 2048]) for l in range(NLAYER) for L in HY_L}
    if not MIX["hy"]:
        return
    for l in range(NLAYER):
        for L in HY_L:
            filter_gen(K, l, L)


def filter_gen(K, l, L):
    P, I = K.P, K.I
    mk = P.mark()
    N = 2 * L
    NTB = L // 128
    NJ = N // 128
    CH = min(512, L)
    w1 = P.sb("fw1", [33, 64]); P.dma(w1[:], I.hy_f_w1[l])
    w2 = P.sb("fw2", [64, 64]); P.dma(w2[:], I.hy_f_w2[l])
    w3 = P.sb("fw3", [64, 4096]); P.dma(w3[:], I.hy_f_w3[l])
    b1 = P.sb("fb1", [64, 1]); P.dma(b1[:], I.hy_f_b1[l].rearrange("(f o) -> f o", o=1))
    b2 = P.sb("fb2", [64, 1]); P.dma(b2[:], I.hy_f_b2[l].rearrange("(f o) -> f o", o=1))
    fq = P.sb("ffq", [64, 1]); P.dma(fq[:], I.hy_f_freq[l].rearrange("(f o) -> f o", o=1))
    zf = P.sb("fzf", [33, L]); P.dma(zf[:], I.zfT[L])
    negt = P.sb("fnegt", [128, NTB]); P.dma(negt[:], I.negt[L])
    wgt = P.sb("fwgt", [128, NJ]); P.dma(wgt[:], I.wgt[L])
    dl = P.sb("fdl", [128, 1024]); P.dma(dl[:], I.deltas)
    h1 = P.sb("fh1", [64, L]); h2 = P.sb("fh2", [64, L])
    pre = P.sb("fpre", [64, 512]); ti = P.sb("fti", [64, 512], I32); tf = P.sb("ftf", [64, 512])
    for (wm, bb, src, dst, kk) in ((w1, b1, zf, h1, 33), (w2, b2, h1, h2, 64)):
        for c in range(L // CH):
            ps = K.ps[c % 8][0:64, 0:CH]
            P.mm(ps, wm[0:kk, :], src[0:kk, c * CH:(c + 1) * CH])
            P.ts(pre[:, 0:CH], ps, bb[:, 0:1], fq[:, 0:1], ALU.add, ALU.mult)
            sin_rr(K, dst[:, c * CH:(c + 1) * CH], pre[:, 0:CH], ti[:, 0:CH], tf[:, 0:CH])
    FS = P.sb("fFS", [128, NTB, 1024], BF16); FD = P.sb("fFD", [128, NTB, 1024], BF16)
    dec = P.sb("fdec", [128, 1024])
    ffw = P.sb("fffw", [128, 1024]); fbw = P.sb("ffbw", [128, 1024]); fab = P.sb("ffab", [128, 1024])
    rn = P.sb("frn", [128, 1024])
    Dt = [P.sb("fDt%d" % i, [128, NTB, 128], BF16) for i in range(2)]
    kh = [P.sb("fkh%d" % i, [128, 1024]) for i in range(2)]
    Dv = None
    for o in range(2):
        for tb in range(NTB):
            P.act(dec[:], dl[:], AF.Exp, scale=negt[:, tb:tb + 1])
            for d in range(2):
                for hf in range(2):
                    col = o * 2048 + d * 1024 + hf * 512
                    P.mm(K.ps[d * 2 + hf][:], h2[:, tb * 128:(tb + 1) * 128], w3[:, col:col + 512])
            for hf in range(2):
                P.tt(ffw[:, hf * 512:(hf + 1) * 512], K.ps[hf][:], dec[:, hf * 512:(hf + 1) * 512], ALU.mult)
                P.tt(fbw[:, hf * 512:(hf + 1) * 512], K.ps[2 + hf][:], dec[:, hf * 512:(hf + 1) * 512], ALU.mult)
            if tb == 0:
                P.memset(fbw[0:1, :], 0.0)
            first = (tb == 0)
            last = (tb == NTB - 1)
            P.act(fab[:], ffw[:], AF.Abs)
            for hf in range(2):
                P.mm(K.ps[4 + hf][:], K.ones[:], fab[:, hf * 512:(hf + 1) * 512], start=first, stop=False)
            P.act(fab[:], fbw[:], AF.Abs)
            for hf in range(2):
                P.mm(K.ps[4 + hf][:], K.ones[:], fab[:, hf * 512:(hf + 1) * 512], start=False, stop=last)
            P.tt(FS[:, tb, :], ffw[:], fbw[:], ALU.add)
            P.tt(FD[:, tb, :], ffw[:], fbw[:], ALU.subtract, e="pool")
        for q in range(2):
            P.op("dve", lambda g, q=q: g.reciprocal(rn[:, q * 512:(q + 1) * 512], K.ps[4 + q][:]),
                 [K.ps[4 + q][:]], [rn[:, q * 512:(q + 1) * 512]])
        for jt in range(NJ):
            dt_ = Dt[jt % 2]; ko = kh[jt % 2]
            P.dma(dt_[:], I.dft[L][jt].rearrange("t (tb j) -> t tb j", j=128))
            src = FS if jt < NJ // 2 else FD
            for q in range(2):
                ps = K.ps[(jt % 2) * 2 + q][:]
                for tb in range(NTB):
                    P.mm(ps, dt_[:, tb, :], src[:, tb, q * 512:(q + 1) * 512], start=(tb == 0), stop=(tb == NTB - 1))
                P.stt(ko[:, q * 512:(q + 1) * 512], ps, wgt[:, jt:jt + 1], rn[:, q * 512:(q + 1) * 512], ALU.mult, ALU.mult)
            if jt == NJ // 2:
                for q in range(2):
                    ps = K.ps[6 + q][0:1, :]
                    for tb in range(NTB):
                        P.mm(ps, dt_[:, tb, 0:1], FS[:, tb, q * 512:(q + 1) * 512], start=(tb == 0), stop=(tb == NTB - 1))
                    P.stt(ko[0:1, q * 512:(q + 1) * 512], ps, wgt[0:1, jt:jt + 1], rn[0:1, q * 512:(q + 1) * 512], ALU.mult, ALU.mult)
            P.dma(K.KH[(l, L)][jt * 128:(jt + 1) * 128, o * 1024:(o + 1) * 1024], ko[:])
    P.release(mk)


def mixer_hyena(K, l):
    P, I = K.P, K.I
    if not hasattr(K, "hy_scr"):
        K.hy_scr = {nm: K.dscr("hy_" + nm, [1024, T]) for nm in ("x1", "x2", "gs", "zc")}
        K.hy_ch = K.dscr("hy_ch", [4096, 1024], BF16)
    X1, X2, GS, ZC = (K.hy_scr[n] for n in ("x1", "x2", "gs", "zc"))
    mk0 = P.mark()
    zT = P.sb("hzT", [128, 20, 1024], BF16)
    hcw = P.sb("hcw", [128, 3, 24])
    for k in range(3):
        P.dma(hcw[:, k, :], I.hy_conv_w[l, k].rearrange("(ft f) -> f ft", f=128))
    hcb = P.sb("hcb", [128, 24]); P.dma(hcb[:], I.hy_conv_b[l].rearrange("(ft f) -> f ft", f=128))
    hbc = P.sb("hbc", [128, 2, 8])
    for o in range(2):
        P.dma(hbc[:, o, :], I.hy_bias[l, o].rearrange("(ct c) -> c ct", c=128))
    mk = P.mark()
    raw = P.sb("hraw", [128, T]); cv = P.sb("hcv", [128, T])
    for cg in range(8):
        for s in range(4):
            c0 = s * 1024 + cg * 128
            w = load_w(K, I.w_in[l][:, c0:c0 + 128], 128)
            for tt in range(NTT):
                ps = K.ps[(s * NTT + tt) % 8][:]
                proj(K, ps, w, 0, 128, tt)
                if s < 3:
                    P.copy(raw[:, tt * 512:(tt + 1) * 512], ps, e="act")
                else:
                    P.act(raw[:, tt * 512:(tt + 1) * 512], ps, AF.Silu)
            if s == 3:
                P.dma(GS[cg * 128:(cg + 1) * 128, :], raw[:])
                continue
            ft = s * 8 + cg
            P.ts(cv[:], raw[:], hcw[:, 1, ft:ft + 1], hcb[:, ft:ft + 1], ALU.mult, ALU.add)
            for (s0, L) in SEQS:
                P.stt(cv[:, s0 + 1:s0 + L], raw[:, s0:s0 + L - 1], hcw[:, 0, ft:ft + 1], cv[:, s0 + 1:s0 + L], ALU.mult, ALU.add)
                P.stt(cv[:, s0:s0 + L - 1], raw[:, s0 + 1:s0 + L], hcw[:, 2, ft:ft + 1], cv[:, s0:s0 + L - 1], ALU.mult, ALU.add)
            if s == 0:
                P.dma(ZC[cg * 128:(cg + 1) * 128, :], cv[:])
                for g5 in range(5):
                    ps = K.ps[g5 % 8]
                    for q in range(4):
                        tb = g5 * 4 + q
                        P.transpose(ps[:, q * 128:(q + 1) * 128], cv[:, tb * 128:(tb + 1) * 128], K.ident[:])
                    P.copy(zT[:, g5 * 4:(g5 + 1) * 4, cg * 128:(cg + 1) * 128], ps[:].rearrange("p (q t) -> p q t", q=4), e="act")
            else:
                P.dma((X1 if s == 1 else X2)[cg * 128:(cg + 1) * 128, :], cv[:])
    P.release(mk)
    for o in range(2):
        for (s0, L) in SEQS:
            hyena_conv(K, l, o, s0, L, zT, hbc)
    P.release(mk0)


def hyena_conv(K, l, o, s0, L, zT, hbc):
    P, I = K.P, K.I
    X1, X2, GS, ZC = (K.hy_scr[n] for n in ("x1", "x2", "gs", "zc"))
    CHs = K.hy_ch
    N = 2 * L
    NTB = L // 128
    NJ = N // 128
    NJ2 = NJ // 2
    tb0 = s0 // 128
    KH = K.KH[(l, L)]
    mk = P.mark()
    Dre = [P.sb("hDre%d" % i, [128, NTB, 128], BF16) for i in range(2)]
    Dim = [P.sb("hDim%d" % i, [128, NTB, 128], BF16) for i in range(2)]
    khr = [P.sb("hkhr%d" % i, [128, 1024]) for i in range(2)]
    khi = [P.sb("hkhi%d" % i, [128, 1024]) for i in range(2)]
    tmp = [P.sb("htmp%d" % i, [128, 512]) for i in range(4)]
    crb = [P.sb("hcrb%d" % i, [128, 1024], BF16) for i in range(2)]
    cib = [P.sb("hcib%d" % i, [128, 1024], BF16) for i in range(2)]
    for jp in range(NJ2):
        b = jp % 2
        P.dma(Dre[b][:], I.dft[L][jp].rearrange("t (tb j) -> t tb j", j=128))
        P.dma(Dim[b][:], I.dft[L][jp + NJ2].rearrange("t (tb j) -> t tb j", j=128))
        P.dma(khr[b][:], KH[jp * 128:(jp + 1) * 128, o * 1024:(o + 1) * 1024])
        P.dma(khi[b][:], KH[(jp + NJ2) * 128:(jp + NJ2 + 1) * 128, o * 1024:(o + 1) * 1024])
        for hf in range(2):
            psr = K.ps[b * 4 + hf * 2][:]; psi = K.ps[b * 4 + hf * 2 + 1][:]
            for tb in range(NTB):
                P.mm(psr, Dre[b][:, tb, :], zT[:, tb0 + tb, hf * 512:(hf + 1) * 512], start=(tb == 0), stop=(tb == NTB - 1))
            for tb in range(NTB):
                P.mm(psi, Dim[b][:, tb, :], zT[:, tb0 + tb, hf * 512:(hf + 1) * 512], start=(tb == 0), stop=(tb == NTB - 1))
            sl = slice(hf * 512, (hf + 1) * 512)
            P.tt(tmp[0][:], psr, khr[b][:, sl], ALU.mult)
            P.tt(tmp[1][:], psi, khi[b][:, sl], ALU.mult)
            P.tt(tmp[2][:], psr, khi[b][:, sl], ALU.mult)
            P.tt(tmp[3][:], psi, khr[b][:, sl], ALU.mult)
            P.tt(crb[b][:, sl], tmp[0][:], tmp[1][:], ALU.subtract, e="pool")
            P.tt(cib[b][:, sl], tmp[2][:], tmp[3][:], ALU.add, e="pool")
            if jp == 0:
                P.tt(crb[b][0:1, sl], psr[0:1, :], khr[b][0:1, sl], ALU.mult)
                P.tt(cib[b][0:1, sl], psi[0:1, :], khi[b][0:1, sl], ALU.mult)
        P.dma(CHs[jp * 128:(jp + 1) * 128, :], crb[b][:])
        P.dma(CHs[(jp + NJ2) * 128:(jp + NJ2 + 1) * 128, :], cib[b][:])
    P.release(mk)
    mk = P.mark()
    NTK = min(512, L)
    PJ = min(4, NJ)
    CHp = [P.sb("hCHp%d" % i, [128, PJ, 1024], BF16) for i in range(2)]
    DTp = [P.sb("hDTp%d" % i, [128, PJ, NTK], BF16) for i in range(2)]
    zt = [P.sb("hzt%d" % i, [128, NTK]) for i in range(2)]
    xt = [P.sb("hxt%d" % i, [128, NTK]) for i in range(2)]
    gt = [P.sb("hgt%d" % i, [128, NTK]) for i in range(2)]
    zn = [P.sb("hzn%d" % i, [128, NTK]) for i in range(2)]
    XS = X1 if o == 0 else X2
    it = 0
    for tq in range(L // NTK):
        t0 = s0 + tq * NTK
        npiece = NJ // PJ
        for pc in range(npiece):
            b = it % 2
            it += 1
            P.dma(CHp[b][:], CHs[pc * PJ * 128:(pc + 1) * PJ * 128, :].rearrange("(jl p) c -> p jl c", p=128))
            P.dma(DTp[b][:], I.dftT[L][pc * PJ * 128:(pc + 1) * PJ * 128, tq * NTK:(tq + 1) * NTK].rearrange("(jl p) t -> p jl t", p=128))
            for ct in range(8):
                for jl in range(PJ):
                    P.mm(K.ps[ct][:, 0:NTK], CHp[b][:, jl, ct * 128:(ct + 1) * 128], DTp[b][:, jl, :],
                         start=(pc == 0 and jl == 0), stop=(pc == npiece - 1 and jl == PJ - 1))
        for ct in range(8):
            b = ct % 2
            rows = slice(ct * 128, (ct + 1) * 128)
            P.dma(zt[b][:], ZC[rows, t0:t0 + NTK])
            P.dma(xt[b][:], XS[rows, t0:t0 + NTK])
            P.stt(zn[b][:], zt[b][:], hbc[:, o, ct:ct + 1], K.ps[ct][:, 0:NTK], ALU.mult, ALU.add)
            if o == 0:
                P.tt(zn[b][:], zn[b][:], xt[b][:], ALU.mult)
                P.dma(ZC[rows, t0:t0 + NTK], zn[b][:])
                nq = NTK // 128
                for q in range(nq):
                    P.transpose(K.ps[ct][:, q * 128:(q + 1) * 128], zn[b][:, q * 128:(q + 1) * 128], K.ident[:])
                tbs = t0 // 128
                P.copy(zT[:, tbs:tbs + nq, ct * 128:(ct + 1) * 128],
                       K.ps[ct][:, 0:NTK].rearrange("p (q t) -> p q t", q=nq), e="act")
            else:
                P.dma(gt[b][:], GS[rows, t0:t0 + NTK])
                P.tt(zn[b][:], zn[b][:], xt[b][:], ALU.mult)
                P.tt(K.yb[:, ct, t0:t0 + NTK], zn[b][:], gt[b][:], ALU.mult, e="pool")
    P.release(mk)


def s5_host_consts():
    c = {}
    c["s5_bm"] = np.kron(np.eye(8), np.ones((16, 16))).astype(np.float32)
    c["s5_bmg"] = np.kron(np.eye(8), np.ones((16, 1))).astype(np.float32)
    m2 = np.zeros((2, 64, 4, 8), np.float32)
    for g2 in range(2):
        for gp in range(4):
            m2[g2, :, gp, 2 * gp + g2] = 1.0
    c["s5_mask2"] = m2.reshape(128, 32)
    return c


def bc(ap, shape):
    return ap.to_broadcast(list(shape))


def cmul(P, outr, outi, ar, ai, br, bi, t1, t2, e2="pool"):
    P.tt(t1, ar, br, ALU.mult)
    P.tt(t2, ai, bi, ALU.mult)
    P.tt(outr, t1, t2, ALU.subtract)
    P.tt(t1, ar, bi, ALU.mult)
    P.tt(t2, ai, br, ALU.mult)
    P.tt(outi, t1, t2, ALU.add)


def s5_lambar(K, l, R, C, lam_src, step_fill, NK, tag):
    P, I = K.P, K.I
    lr = P.sb(tag + "lr", [R, C]); li = P.sb(tag + "li", [R, C]); st = P.sb(tag + "st", [R, C])
    P.dma(lr[:], lam_src(I.s5_lambda_re[l])); P.dma(li[:], lam_src(I.s5_lambda_im[l]))
    step_fill(st)
    P.act(st[:], st[:], AF.Exp)
    mk = P.mark()
    ar = P.sb(tag + "ar", [R, C]); ang = P.sb(tag + "ang", [R, C]); mag = P.sb(tag + "mag", [R, C])
    s = P.sb(tag + "s", [R, C]); c = P.sb(tag + "c", [R, C]); ti = P.sb(tag + "ti", [R, C], I32); tf = P.sb(tag + "tf", [R, C])
    P.tt(ar[:], lr[:], st[:], ALU.mult)
    P.act(mag[:], ar[:], AF.Exp)
    P.tt(ang[:], li[:], st[:], ALU.mult)
    sin_rr(K, s[:], ang[:], ti[:], tf[:])
    P.ts(ang[:], ang[:], math.pi / 2.0, None, ALU.add)
    sin_rr(K, c[:], ang[:], ti[:], tf[:])
    pwr = P.sb(tag + "pwr", [R, NK, C]); pwi = P.sb(tag + "pwi", [R, NK, C])
    return lr, li, st, mag, s, c, pwr, pwi, mk


def s5_prep(K, l):
    P, I = K.P, K.I
    S = Ctx()
    if not hasattr(K, "s5c"):
        pass
    S.bm = P.sb("s5bm", [128, 128]); P.dma(S.bm[:], I.s5_bm)
    S.bmg = P.sb("s5bmg", [128, 8]); P.dma(S.bmg[:], I.s5_bmg)
    S.mask2 = P.sb("s5m2", [128, 4, 8]); P.dma(S.mask2[:], I.s5_mask2.rearrange("p (a b) -> p a b", a=4))
    def fillA(st):
        P.dma(st[:], I.s5_log_step[l].rearrange("d g -> (d g)").partition_broadcast(64))
    S.pAr = P.sb("s5pAr", [64, 9, 128]); S.pAi = P.sb("s5pAi", [64, 9, 128])
    S.qr = P.sb("s5qr", [64, 128]); S.qi = P.sb("s5qi", [64, 128])
    mk = P.mark()
    lr, li, st, mag, s, c, _, _, _ = s5_lambar(K, l, 64, 128, lambda a: a.rearrange("d g p -> p (d g)"), fillA, 1, "A")
    t1 = P.sb("At1", [64, 128]); t2 = P.sb("At2", [64, 128])
    P.memset(S.pAr[:, 0, :], 1.0); P.memset(S.pAi[:, 0, :], 0.0)
    P.tt(S.pAr[:, 1, :], mag[:], c[:], ALU.mult); P.tt(S.pAi[:, 1, :], mag[:], s[:], ALU.mult)
    for k in range(1, 8):
        cmul(P, S.pAr[:, k + 1, :], S.pAi[:, k + 1, :], S.pAr[:, k, :], S.pAi[:, k, :], S.pAr[:, 1, :], S.pAi[:, 1, :], t1[:], t2[:])
    dd = P.sb("Add", [64, 128]); nr = P.sb("Anr", [64, 128])
    P.tt(dd[:], lr[:], lr[:], ALU.mult); P.tt(t1[:], li[:], li[:], ALU.mult); P.tt(dd[:], dd[:], t1[:], ALU.add)
    P.op("dve", lambda g: g.reciprocal(dd[:], dd[:]), [dd[:]], [dd[:]])
    P.ts(nr[:], S.pAr[:, 1, :], -1.0, None, ALU.add)
    P.tt(t1[:], nr[:], lr[:], ALU.mult); P.tt(t2[:], S.pAi[:, 1, :], li[:], ALU.mult); P.tt(t1[:], t1[:], t2[:], ALU.add)
    P.tt(S.qr[:], t1[:], dd[:], ALU.mult)
    P.tt(t1[:], S.pAi[:, 1, :], lr[:], ALU.mult); P.tt(t2[:], nr[:], li[:], ALU.mult); P.tt(t1[:], t1[:], t2[:], ALU.subtract)
    P.tt(S.qi[:], t1[:], dd[:], ALU.mult)
    P.release(mk)
    def fillB(st):
        for g2 in range(2):
            P.dma(st[g2 * 64:(g2 + 1) * 64, :], I.s5_log_step[l].rearrange("d (gp g2) -> g2 (d gp)", g2=2)[g2].partition_broadcast(64))
    S.pBr = P.sb("s5pBr", [128, 9, 64]); S.pBi = P.sb("s5pBi", [128, 9, 64])
    S.apr = P.sb("s5apr", [128, 17, 64]); S.api = P.sb("s5api", [128, 17, 64])
    mk = P.mark()
    lr, li, st, mag, s, c, _, _, _ = s5_lambar(K, l, 128, 64, lambda a: a.rearrange("d (gp g2) p -> (g2 p) (d gp)", g2=2), fillB, 1, "B")
    t1 = P.sb("Bt1", [128, 64]); t2 = P.sb("Bt2", [128, 64])
    P.memset(S.pBr[:, 0, :], 1.0); P.memset(S.pBi[:, 0, :], 0.0)
    P.tt(S.pBr[:, 1, :], mag[:], c[:], ALU.mult); P.tt(S.pBi[:, 1, :], mag[:], s[:], ALU.mult)
    for k in range(1, 8):
        cmul(P, S.pBr[:, k + 1, :], S.pBi[:, k + 1, :], S.pBr[:, k, :], S.pBi[:, k, :], S.pBr[:, 1, :], S.pBi[:, 1, :], t1[:], t2[:])
    P.memset(S.apr[:, 0, :], 1.0); P.memset(S.api[:, 0, :], 0.0)
    P.copy(S.apr[:, 1, :], S.pBr[:, 8, :]); P.copy(S.api[:, 1, :], S.pBi[:, 8, :])
    for j in range(1, 16):
        cmul(P, S.apr[:, j + 1, :], S.api[:, j + 1, :], S.apr[:, j, :], S.api[:, j, :], S.apr[:, 1, :], S.api[:, 1, :], t1[:], t2[:])
    P.release(mk)
    return S


class StopS5(Exception):
    pass


import os as _os
S5_CUT = float(_os.environ.get("S5_CUT", "99"))


def s5cut(stage):
    if S5_CUT <= stage:
        raise StopS5()


def mixer_s5(K, l):
    P, I = K.P, K.I
    if not hasattr(K, "s5G"):
        K.s5G = K.dscr("s5_G", [1024, T], BF16)
    mk0 = P.mark()
    try:
        S = s5_prep(K, l)
        s5cut(0)
        for gt in range(8):
            s5_gt(K, l, gt, S)
            s5cut(8)
        s5_glu(K, l)
    except StopS5:
        pass
    P.release(mk0)


def s5_gt(K, l, gt, S):
    P, I, O = K.P, K.I, K.O
    mk = P.mark()
    NCH = T // 8
    ub = P.sb("s5ub", [128, T], BF16); yacc = P.sb("s5y", [128, T])
    dsk = P.sb("s5dsk", [128, 1]); P.dma(dsk[:], I.s5_D[l, gt * 128:(gt + 1) * 128].rearrange("(c o) -> c o", o=1))
    m = [P.sb("s5m%d" % i, [128, 4 * NCH]) for i in range(2)]
    s5cut(0.2)
    w = load_w(K, I.w_in[l][:, C_S5 + gt * 128: C_S5 + (gt + 1) * 128], 128)
    s5cut(0.4)
    for tt in range(NTT):
        ps = K.ps[tt % 8][:]
        proj(K, ps, w, 0, 128, tt)
        s5cut(0.6)
        uf = m[tt % 2][:, 0:512]
        P.copy(uf, ps, e="act")
        s5cut(0.8)
        P.copy(ub[:, tt * 512:(tt + 1) * 512], uf, e="pool")
        P.ts(yacc[:, tt * 512:(tt + 1) * 512], uf, dsk[:, 0:1], None, ALU.mult)
    s5cut(1)
    ubv = ub[:].rearrange("p (n s) -> p n s", s=8)
    ubr = P.sb("s5ubr", [128, T], BF16)
    ubrv = ubr[:].rearrange("p (n s) -> p n s", s=8)
    P.copy(ubrv, ubv[:, ::-1, :], e="pool")
    yv = yacc[:].rearrange("p (n s) -> p n s", s=8)
    Bre = P.sb("s5Bre", [64, 8, 16]); Bim = P.sb("s5Bim", [64, 8, 16])
    CAr = P.sb("s5CAr", [64, 8, 16]); CAi = P.sb("s5CAi", [64, 8, 16])
    bbr = P.sb("s5bbr", [64, 8, 16]); bbi = P.sb("s5bbi", [64, 8, 16])
    tA1 = P.sb("s5tA1", [64, 8, 16]); tA2 = P.sb("s5tA2", [64, 8, 16])
    Wr = P.sb("s5Wr", [64, 8, 128]); Wi = P.sb("s5Wi", [64, 8, 128])
    Kb = P.sb("s5Kb", [128, 8, 128], BF16)
    Pm = P.sb("s5Pm", [128, 8, 8, 64], BF16)
    CBr = P.sb("s5CBr", [128, 4, 16]); CBi = P.sb("s5CBi", [128, 4, 16])
    CLr = P.sb("s5CLr", [128, 9, 4, 16]); CLi = P.sb("s5CLi", [128, 9, 4, 16])
    tB1 = P.sb("s5tB1", [128, 4, 16]); tB2 = P.sb("s5tB2", [128, 4, 16])
    CLm = [[P.sb("s5CLm%d%d" % (i, r), [128, 4, 128], BF16) for r in range(2)] for i in range(2)]
    Dst = [P.sb("s5D%d" % r, [128, 4, NCH]) for r in range(2)]
    big = P.sb("s5big", [128, 2720])
    Pl = [big[:, r * 1360:(r + 1) * 1360].rearrange("p (c s j) -> p c s j", c=4, j=17) for r in range(2)]
    S2 = [P.sb("s5S2%d" % r, [128, 4, 20]) for r in range(2)]
    fin = [P.sb("s5fin%d" % r, [128, 4]) for r in range(2)]
    Sb = [P.sb("s5Sb%d" % r, [128, 4, NCH], BF16) for r in range(2)]
    for d in range(2):
        cA = slice(d * 64 + gt * 8, d * 64 + gt * 8 + 8)
        cB = slice(d * 32 + gt * 4, d * 32 + gt * 4 + 4)
        gsl = slice(gt * 8, gt * 8 + 8)
        P.dma(Bre[:], I.s5_B_re[l, d, gsl].rearrange("g p h -> p g h"))
        P.dma(Bim[:], I.s5_B_im[l, d, gsl].rearrange("g p h -> p g h"))
        P.dma(CAr[:], I.s5_C_re[l, d, gsl].rearrange("g h p -> p g h"))
        P.dma(CAi[:], I.s5_C_im[l, d, gsl].rearrange("g h p -> p g h"))
        for g2 in range(2):
            for gp in range(4):
                P.dma(CBr[g2 * 64:(g2 + 1) * 64, gp, :], I.s5_C_re[l, d, gt * 8 + 2 * gp + g2].rearrange("h p -> p h"))
                P.dma(CBi[g2 * 64:(g2 + 1) * 64, gp, :], I.s5_C_im[l, d, gt * 8 + 2 * gp + g2].rearrange("h p -> p h"))
        qrb = bc(S.qr[:, cA].unsqueeze(2), [64, 8, 16]); qib = bc(S.qi[:, cA].unsqueeze(2), [64, 8, 16])
        cmul(P, bbr[:], bbi[:], qrb, qib, Bre[:], Bim[:], tA1[:], tA2[:])
        for k in range(8):
            pr = bc(S.pAr[:, k, cA].unsqueeze(2), [64, 8, 16]); pi_ = bc(S.pAi[:, k, cA].unsqueeze(2), [64, 8, 16])
            cmul(P, Wr[:, k, :].rearrange("p (g h) -> p g h", h=16), Wi[:, k, :].rearrange("p (g h) -> p g h", h=16),
                 pr, pi_, bbr[:], bbi[:], tA1[:], tA2[:])
        s5cut(2)
        P.ts(CAi[:], CAi[:], -1.0, None, ALU.mult)
        car = CAr[:].rearrange("p g h -> p (g h)"); cai = CAi[:].rearrange("p g h -> p (g h)")
        for dl in range(8):
            ps = K.ps[dl % 8][:, 0:128]
            P.mm(ps, Wr[:, dl, :], car, start=True, stop=False)
            P.mm(ps, Wi[:, dl, :], cai, start=False, stop=True)
            P.tt(Kb[:, dl, :], ps, S.bm[:], ALU.mult)
        s5cut(3)
        it = 0
        for ri, Wx in enumerate((Wr, Wi)):
            for sg in range(8):
                k = 7 - sg if d == 0 else sg
                ps = K.ps[(sg + ri) % 8][:, 0:64]
                P.transpose(ps, Wx[:, k, :], K.ident[0:64, 0:64])
                P.tt(Pm[:, sg, :, :], bc(ps.unsqueeze(1), [128, 8, 64]), bc(S.bmg[:].unsqueeze(2), [128, 8, 64]), ALU.mult)
            for gp in range(4):
                ps = K.ps[it % 8][:, 0:NCH]
                it += 1
                for sg in range(8):
                    P.mm(ps, Pm[:, sg, 2 * gp:2 * gp + 2, :].rearrange("p g q -> p (g q)"), (ubv if d == 0 else ubrv)[:, :, sg],
                         start=(sg == 0), stop=(sg == 7))
                P.copy(Dst[ri][:, gp, :], ps, e=("act" if it % 2 else "dve"))
        s5cut(4)
        s5_scan(K, l, gt, d, S, Dst, Pl, S2, fin, m, cB)
        s5cut(5)
        P.copy(Sb[0][:], Dst[0][:])
        P.ts(Sb[1][:], Dst[1][:], -1.0, None, ALU.mult)
        for k in range(1, 9):
            pr = bc(S.pBr[:, k, cB].unsqueeze(2), [128, 4, 16]); pi_ = bc(S.pBi[:, k, cB].unsqueeze(2), [128, 4, 16])
            cmul(P, CLr[:, k], CLi[:, k], pr, pi_, CBr[:], CBi[:], tB1[:], tB2[:])
        for tau in range(8):
            k = tau + 1 if d == 0 else 8 - tau
            cl = CLm[tau % 2]
            for ri, CL in enumerate((CLr, CLi)):
                P.tt(cl[ri][:].rearrange("p a (b h) -> p a b h", h=16), bc(CL[:, k].unsqueeze(2), [128, 4, 8, 16]),
                     bc(S.mask2[:].unsqueeze(3), [128, 4, 8, 16]), ALU.mult)
            ps = K.ps[tau % 8][:, 0:NCH]
            uu = ubv if d == 0 else ubrv
            nd = (tau + 1) if d == 0 else (8 - tau)
            for dl in range(nd):
                P.mm(ps, Kb[:, dl, :], uu[:, :, (tau - dl) if d == 0 else (tau + dl)], start=(dl == 0), stop=False)
            i2 = 0
            for gp in range(4):
                for ri in range(2):
                    P.mm(ps, cl[ri][:, gp, :], Sb[ri][:, gp, :], start=False, stop=(i2 == 7))
                    i2 += 1
            src = ps if d == 0 else ps[:, ::-1]
            P.tt(yv[:, :, tau], yv[:, :, tau], src, ALU.add)
        s5cut(6)
    t = big[:, 0:T]; gb = ub
    P.tt(t, yacc[:], yacc[:], ALU.mult)
    P.ts(t, t, 0.044715, 1.0, ALU.mult, ALU.add)
    P.tt(t, t, yacc[:], ALU.mult)
    P.act(t, t, AF.Sigmoid, scale=2.0 * math.sqrt(2.0 / math.pi))
    P.tt(gb[:], t, yacc[:], ALU.mult)
    if K.dbg and K.stop == ("s5", l):
        P.dma(K.dbg32[gt * 128:(gt + 1) * 128, :], yacc[:], is_output=True)
    P.dma(K.s5G[gt * 128:(gt + 1) * 128, :], gb[:])
    P.release(mk)
    s5cut(7)


def s5_scan(K, l, gt, d, S, Dst, Pl, S2, fin, m, cB):
    P, I, O = K.P, K.I, K.O
    Ar = bc(S.apr[:, 1, cB].unsqueeze(2), [128, 4, 20]); Ai = bc(S.api[:, 1, cB].unsqueeze(2), [128, 4, 20])
    Dv = [Dst[r][:].rearrange("p c (s j) -> p c s j", j=16) for r in range(2)]
    m1 = m[0][:, 0:80].rearrange("p (c s) -> p c s", c=4); m2 = m[1][:, 0:80].rearrange("p (c s) -> p c s", c=4)
    for r in range(2):
        P.memset(Pl[r][:, :, :, 0], 0.0)
        P.copy(Pl[r][:, :, :, 1], Dv[r][:, :, :, 0])
    for j in range(1, 16):
        pr, pi_ = Pl[0][:, :, :, j], Pl[1][:, :, :, j]
        P.tt(m1, pr, Ar, ALU.mult); P.tt(m2, pi_, Ai, ALU.mult)
        P.tt(m1, m1, m2, ALU.subtract)
        P.tt(Pl[0][:, :, :, j + 1], m1, Dv[0][:, :, :, j], ALU.add)
        P.tt(m1, pr, Ai, ALU.mult); P.tt(m2, pi_, Ar, ALU.mult)
        P.tt(m1, m1, m2, ALU.add)
        P.tt(Pl[1][:, :, :, j + 1], m1, Dv[1][:, :, :, j], ALU.add)
    A16r, A16i = S.apr[:, 16, cB], S.api[:, 16, cB]
    seqs = [("s", 0, 16), ("pA", 16, 2), ("pB", 18, 2)] if d == 0 else [("pB", 0, 2), ("pA", 2, 2), ("s", 4, 16)]
    t1 = m[0][:, 100:104]; t2 = m[1][:, 100:104]

    def step(outr, outi, sr, si, dr, di):
        P.tt(t1, sr, A16r, ALU.mult); P.tt(t2, si, A16i, ALU.mult)
        P.tt(t1, t1, t2, ALU.subtract)
        P.tt(outr, t1, dr, ALU.add)
        P.tt(t1, sr, A16i, ALU.mult); P.tt(t2, si, A16r, ALU.mult)
        P.tt(t1, t1, t2, ALU.add)
        P.tt(outi, t1, di, ALU.add)

    for (nm, a, cnt) in seqs:
        if nm == "s":
            for r in range(2):
                for g2 in range(2):
                    P.dma(S2[r][g2 * 64:(g2 + 1) * 64, :, a],
                          I.ss[l, d, gt * 8 + g2:gt * 8 + 8:2, :, r].rearrange("gp p -> p gp"))
            for k in range(cnt - 1):
                step(S2[0][:, :, a + k + 1], S2[1][:, :, a + k + 1], S2[0][:, :, a + k], S2[1][:, :, a + k],
                     Pl[0][:, :, a + k, 16], Pl[1][:, :, a + k, 16])
        else:
            for r in range(2):
                P.memset(S2[r][:, :, a], 0.0)
                P.copy(S2[r][:, :, a + 1], Pl[r][:, :, a, 16])
            step(fin[0][:], fin[1][:], S2[0][:, :, a + 1], S2[1][:, :, a + 1], Pl[0][:, :, a + 1, 16], Pl[1][:, :, a + 1, 16])
            pr_i = 0 if nm == "pA" else 1
            for r in range(2):
                for g2 in range(2):
                    P.dma(O.ns[pr_i, l, d, gt * 8 + g2:gt * 8 + 8:2, :, r].rearrange("gp p -> p gp"),
                          fin[r][g2 * 64:(g2 + 1) * 64, :], is_output=True)
    apr = bc(S.apr[:, 0:16, cB].rearrange("p j c -> p c j").unsqueeze(2), [128, 4, 20, 16])
    api = bc(S.api[:, 0:16, cB].rearrange("p j c -> p c j").unsqueeze(2), [128, 4, 20, 16])
    s2r = bc(S2[0][:].unsqueeze(3), [128, 4, 20, 16]); s2i = bc(S2[1][:].unsqueeze(3), [128, 4, 20, 16])
    M1 = m[0][:].rearrange("p (c s j) -> p c s j", c=4, j=16); M2 = m[1][:].rearrange("p (c s j) -> p c s j", c=4, j=16)
    P.tt(M1, s2r, apr, ALU.mult); P.tt(M2, s2i, api, ALU.mult)
    P.tt(M1, M1, M2, ALU.subtract)
    P.tt(Dv[0], M1, Pl[0][:, :, :, 0:16], ALU.add)
    P.tt(M1, s2r, api, ALU.mult); P.tt(M2, s2i, apr, ALU.mult)
    P.tt(M1, M1, M2, ALU.add)
    P.tt(Dv[1], M1, Pl[1][:, :, :, 0:16], ALU.add)


def s5_glu(K, l):
    P, I = K.P, K.I
    mk = P.mark()
    gbT = P.sb("s5gbT", [128, 2, 8]);
    for hh in range(2):
        P.dma(gbT[:, hh, :], I.s5_glu_b[l, hh * 1024:(hh + 1) * 1024].rearrange("(ct c) -> c ct", c=128))
    Gt = [P.sb("s5Gt%d" % i, [128, 8, 512], BF16) for i in range(2)]
    Gv = K.s5G.rearrange("(kc k) t -> k kc t", k=128)
    wa = P.sb("s5wa", [128, 8, 1024], BF16); wg = P.sb("s5wg", [128, 8, 1024], BF16); wz = P.sb("s5wz", [128, 8, 1024], BF16)
    for c4 in range(8):
        for (dst, src) in ((wa, I.s5_glu_w[l][:, c4 * 128:(c4 + 1) * 128]), (wg, I.s5_glu_w[l][:, 1024 + c4 * 128:1024 + (c4 + 1) * 128]),
                           (wz, I.w_in[l][:, C_S5G + c4 * 128:C_S5G + (c4 + 1) * 128])):
            w = load_w(K, src, 128)
            P.copy(dst[:, :, c4 * 128:(c4 + 1) * 128], w, e="pool")
    sg = [P.sb("s5sg%d" % i, [128, 512]) for i in range(2)]
    sl = [P.sb("s5sl%d" % i, [128, 512]) for i in range(2)]
    tm = [P.sb("s5tm%d" % i, [128, 512]) for i in range(2)]
    it = 0
    for tt in range(NTT):
        G = Gt[tt % 2]
        P.dma(G[:], Gv[:, :, tt * 512:(tt + 1) * 512])
        for ct in range(8):
            b = it % 2
            pa = K.ps[(it * 3) % 8][:]; pg = K.ps[(it * 3 + 1) % 8][:]; pz = K.ps[(it * 3 + 2) % 8][:]
            it += 1
            for kc in range(8):
                P.mm(pa, wa[:, kc, ct * 128:(ct + 1) * 128], G[:, kc, :], start=(kc == 0), stop=(kc == 7))
            for kc in range(8):
                P.mm(pg, wg[:, kc, ct * 128:(ct + 1) * 128], G[:, kc, :], start=(kc == 0), stop=(kc == 7))
            for kc in range(8):
                P.mm(pz, wz[:, kc, ct * 128:(ct + 1) * 128], K.h[:, kc, tt * 512:(tt + 1) * 512], start=(kc == 0), stop=(kc == 7))
            P.act(sg[b][:], pg, AF.Sigmoid, bias=gbT[:, 1, ct:ct + 1])
            P.act(sl[b][:], pz, AF.Silu)
            P.stt(tm[b][:], pa, gbT[:, 0, ct:ct + 1], sg[b][:], ALU.add, ALU.mult)
            P.tt(K.yb[:, ct, tt * 512:(tt + 1) * 512], tm[b][:], sl[b][:], ALU.mult, e="pool")
    P.release(mk)


def gdn_host_consts():
    c = {}
    L = np.tril(np.ones((128, 128), np.float32))
    c["g_L"] = L; c["g_Ls"] = np.tril(np.ones((128, 128), np.float32), -1)
    c["g_U"] = np.ascontiguousarray(L.T); c["g_Us"] = np.triu(np.ones((128, 128), np.float32), 1)
    bd = np.kron(np.eye(4), np.ones((32, 32))).astype(np.float32)
    c["g_bd"] = bd; c["g_nbd"] = (1.0 - bd).astype(np.float32)
    return c


class StopGDN(Exception):
    pass


GDN_STOP = _os.environ.get("GDN_STOP", "")


def mixer_gdn(K, l):
    try:
        mixer_gdn_(K, l)
    except StopGDN:
        pass


def mixer_gdn_(K, l):
    P, I, O = K.P, K.I, K.O
    mk0 = P.mark()
    G = Ctx()
    G.M = {}
    for nm in ("g_L", "g_Ls", "g_U", "g_Us", "g_bd", "g_nbd"):
        G.M[nm] = P.sb(nm, [128, 128]); P.dma(G.M[nm][:], getattr(I, nm))
    G.gcw = P.sb("gcw", [128, 3, 24])
    for k in range(3):
        P.dma(G.gcw[:, k, :], I.gdn_conv_w[l, k].rearrange("(ft f) -> f ft", f=128))
    G.ng = P.sb("gng", [128, 128]); P.dma(G.ng[:], I.gdn_norm_g[l].partition_broadcast(128))
    alog = P.sb("galog", [128, 16]); P.dma(alog[:], I.gdn_A_log[l].rearrange("d h -> (d h)").partition_broadcast(128))
    dtb = P.sb("gdtb", [128, 16]); P.dma(dtb[:], I.gdn_dt_bias[l].rearrange("d h -> (d h)").partition_broadcast(128))
    P.act(alog[:], alog[:], AF.Exp)
    P.ts(alog[:], alog[:], -1.0, None, ALU.mult)
    G.bt = P.sb("gbt", [128, 20, 17]); G.gt = P.sb("ggt", [128, 20, 17]); G.nbt = P.sb("gnbt", [128, 20, 17])
    P.memset(G.gt[:], 0.0); P.memset(G.bt[:], 0.0)
    w32 = load_w(K, I.w_in[l][:, C_BETA:C_BETA + 32], 32)
    mkb = P.mark()
    bgraw = P.sb("gbgraw", [128, 20, 32])
    for c in range(20):
        ps = K.ps[c % 8][:, 0:32]
        for kc in range(8):
            P.mm(ps, K.h[:, kc, c * 128:(c + 1) * 128], w32[:, kc, :], start=(kc == 0), stop=(kc == 7))
        P.copy(bgraw[:, c, :], ps, e="act")
        P.act(G.bt[:, c, 0:16], bgraw[:, c, 0:16], AF.Sigmoid)
        P.tt(G.gt[:, c, 0:16], bgraw[:, c, 16:32], dtb[:], ALU.add)
    P.act(G.gt[:], G.gt[:], AF.Exp)
    P.act(G.gt[:], G.gt[:], AF.Ln, bias=K.one1[:, 0:1])
    for c in range(20):
        P.tt(G.gt[:, c, 0:16], G.gt[:, c, 0:16], alog[:], ALU.mult)
    P.ts(G.nbt[:], G.bt[:], -1.0, None, ALU.mult)
    P.release(mkb)
    for hd in range(8):
        gdn_head(K, l, hd, G)
    P.release(mk0)


def gdn_head(K, l, hd, G):
    P, I, O = K.P, K.I, K.O
    mk = P.mark()
    qT = P.sb("gqT", [128, T]); kT = P.sb("gkT", [128, T])
    ktok = P.sb("gktok", [128, 20, 128]); vtok = P.sb("gvtok", [128, 20, 128])
    oacc = P.sb("goacc", [128, 20, 128])
    mk1 = P.mark()
    vT = P.sb("gvT", [128, T]); raw = P.sb("graw", [128, T])
    rs = P.sb("grs", [128, 512])
    for s, dst in enumerate((qT, kT, vT)):
        c0 = C_GDN + s * 1024 + hd * 128
        w = load_w(K, I.w_in[l][:, c0:c0 + 128], 128)
        for tt in range(NTT):
            ps = K.ps[tt % 8][:]
            proj(K, ps, w, 0, 128, tt)
            P.copy(raw[:, tt * 512:(tt + 1) * 512], ps, e="act")
        ft = s * 8 + hd
        P.ts(dst[:], raw[:], G.gcw[:, 1, ft:ft + 1], None, ALU.mult)
        for (s0, L) in SEQS:
            P.stt(dst[:, s0 + 1:s0 + L], raw[:, s0:s0 + L - 1], G.gcw[:, 0, ft:ft + 1], dst[:, s0 + 1:s0 + L], ALU.mult, ALU.add)
            P.stt(dst[:, s0:s0 + L - 1], raw[:, s0 + 1:s0 + L], G.gcw[:, 2, ft:ft + 1], dst[:, s0:s0 + L - 1], ALU.mult, ALU.add)
        P.act(dst[:], dst[:], AF.Silu)
        if s < 2:
            P.tt(raw[:], dst[:], dst[:], ALU.mult)
            for tt in range(NTT):
                ps = K.ps[(tt + 5) % 8][:]
                P.mm(ps, K.ones[:], raw[:, tt * 512:(tt + 1) * 512])
                P.act(rs[:], ps, AF.Sqrt, bias=K.eps[:, 0:1])
                P.op("dve", lambda g: g.reciprocal(rs[:], rs[:]), [rs[:]], [rs[:]])
                if s == 0:
                    P.stt(dst[:, tt * 512:(tt + 1) * 512], dst[:, tt * 512:(tt + 1) * 512], 128.0 ** -0.5, rs[:], ALU.mult, ALU.mult)
                else:
                    P.tt(dst[:, tt * 512:(tt + 1) * 512], dst[:, tt * 512:(tt + 1) * 512], rs[:], ALU.mult)
    for (src, dstk) in ((kT, ktok), (vT, vtok)):
        for g5 in range(5):
            ps = K.ps[g5 % 8]
            for q in range(4):
                c = g5 * 4 + q
                P.transpose(ps[:, q * 128:(q + 1) * 128], src[:, c * 128:(c + 1) * 128], K.ident[:])
            P.copy(dstk[:, g5 * 4:(g5 + 1) * 4, :], ps[:].rearrange("p (q t) -> p q t", q=4), e="act")
    P.release(mk1)
    for d in range(2):
        mk2 = P.mark()
        col = d * 8 + hd
        uall = P.sb("guall", [128, 20, 128]); wTall = P.sb("gwTall", [128, 20, 128])
        qkTall = P.sb("gqkTall", [128, 20, 128])
        egc = P.sb("gegc", [128, 20]); egl = P.sb("gegl", [128, 20]); e2all = P.sb("ge2all", [128, 20])
        NS = 4
        mks = P.mark()
        slots = []
        for s in range(NS):
            sl = Ctx()
            sl.Pm = [P.sb("gP%d_%d" % (s, i), [128, 128]) for i in range(2)]
            sl.PT = [P.sb("gPT%d_%d" % (s, i), [128, 128]) for i in range(2)]
            sl.TT = [P.sb("gTT%d_%d" % (s, i), [128, 128]) for i in range(2)]
            sl.big = P.sb("gbig%d" % s, [128, 512])
            sl.Rm = sl.big[:, 0:128]; sl.E = sl.big[:, 128:256]; sl.t1 = sl.big[:, 256:384]; sl.qk = sl.big[:, 384:512]
            sl.Y = sl.big[:, 0:256]; sl.Z = sl.big[:, 256:512]
            sl.X = P.sb("gX%d" % s, [128, 256]); sl.bv = sl.X[:, 0:128]; sl.kbg = sl.X[:, 128:256]
            sl.Bf = P.sb("gBf%d" % s, [128, 128]); sl.NAo = P.sb("gNAo%d" % s, [128, 128])
            sl.cols = P.sb("gcols%d" % s, [128, 8])
            sl.banks = [K.ps[(2 * s) % 8], K.ps[(2 * s + 1) % 8]]
            slots.append(sl)
        Minc_ij = G.M["g_L"] if d == 0 else G.M["g_U"]
        Mstr_ij = G.M["g_Ls"] if d == 0 else G.M["g_Us"]
        Mkj = G.M["g_U"] if d == 0 else G.M["g_L"]
        jl = 127 if d == 0 else 0

        def R(sl, i):
            return sl.banks[(i // 4) % 2][:, (i % 4) * 128:(i % 4 + 1) * 128]

        def st1(sl, c):
            ch = slice(c * 128, (c + 1) * 128)
            P.ts(sl.Rm, Mkj[:], G.gt[:, c, col:col + 1], None, ALU.mult)
            P.mm(R(sl, 0)[:, 0:2], Mkj[:], G.gt[:, c, col:col + 2])
            P.mm(R(sl, 1), K.ones[:], sl.Rm)
            P.mm(R(sl, 2), kT[:, ch], kT[:, ch])
            P.mm(R(sl, 3), qT[:, ch], kT[:, ch])

        def st2(sl, c):
            P.copy(sl.cols[:, 0:1], R(sl, 0)[:, 0:1])
            P.copy(sl.cols[:, 1:2], R(sl, 1)[:, jl:jl + 1])
            P.ts(sl.t1, R(sl, 1), sl.cols[:, 0:1], 0.0, ALU.subtract, ALU.max)
            P.act(sl.E, sl.t1, AF.Exp, scale=-1.0)
            P.act(egc[:, c:c + 1], sl.cols[:, 0:1], AF.Exp)
            P.act(egl[:, c:c + 1], sl.cols[:, 1:2], AF.Exp)
            P.act(e2all[:, c:c + 1], sl.cols[:, 0:1], AF.Exp, scale=-1.0, bias=sl.cols[:, 1:2])
            P.tt(sl.cols[:, 3:4], G.bt[:, c, col:col + 1], egc[:, c:c + 1], ALU.mult)

        def st3(sl, c):
            P.tt(sl.t1, R(sl, 2), sl.E, ALU.mult)
            P.stt(sl.Bf[:], sl.t1, G.nbt[:, c, col:col + 1], Mstr_ij[:], ALU.mult, ALU.mult)
            P.tt(sl.t1, R(sl, 3), sl.E, ALU.mult)
            P.tt(sl.qk, sl.t1, Minc_ij[:], ALU.mult, e="pool")
            P.ts(sl.kbg, ktok[:, c, :], sl.cols[:, 3:4], None, ALU.mult, e="pool")
            P.ts(sl.bv, vtok[:, c, :], G.bt[:, c, col:col + 1], None, ALU.mult, e="pool")

        def st4(sl, c):
            P.transpose(R(sl, 0), sl.Bf[:], K.ident[:])
            P.transpose(R(sl, 5), sl.qk, K.ident[:])
            P.tt(sl.Pm[0][:], sl.Bf[:], G.M["g_bd"][:], ALU.mult, e="pool")
            P.tt(sl.PT[0][:], R(sl, 0), G.M["g_bd"][:], ALU.mult)
            P.tt(sl.NAo[:], R(sl, 0), G.M["g_nbd"][:], ALU.mult)
            P.copy(qkTall[:, c, :], R(sl, 5), e="act")
            P.tt(sl.TT[0][:], sl.PT[0][:], K.ident[:], ALU.add, e="pool")

        def st5a(sl, c, a):
            b = 1 - a
            P.mm(R(sl, 6), sl.PT[a][:], sl.Pm[a][:])
            P.mm(R(sl, 7), sl.Pm[a][:], sl.PT[a][:])
            P.copy(sl.Pm[b][:], R(sl, 6), e="act")
            P.copy(sl.PT[b][:], R(sl, 7), e="act")

        def st5b(sl, c, a):
            b = 1 - a
            P.mm(R(sl, 0), sl.Pm[b][:], sl.TT[a][:])
            P.tt(sl.TT[b][:], R(sl, 0), sl.TT[a][:], ALU.add)

        def st6a(sl, c, a):
            P.mm(sl.banks[0][:, 0:256], sl.TT[a][:], sl.X[:])
            P.copy(sl.Y, sl.banks[0][:, 0:256])

        def st6b(sl, c, a):
            P.mm(sl.banks[0][:, 256:512], sl.NAo[:], sl.Y)
            P.tt(sl.Z, sl.X[:], sl.banks[0][:, 256:512], ALU.add)
            P.mm(sl.banks[0][:, 0:256], sl.TT[a][:], sl.Z)
            P.copy(sl.Y, sl.banks[0][:, 0:256])

        def st6c(sl, c, a):
            P.copy(uall[:, c, :], sl.Y[:, 0:128], e="pool")
            P.transpose(R(sl, 6), sl.Y[:, 128:256], K.ident[:])
            P.copy(wTall[:, c, :], R(sl, 6), e="act")

        for c0 in range(0, min(20, int(_os.environ.get("GDN_NB", "99")) * NS), NS):
            batch = [(slots[i], c0 + i) for i in range(NS) if c0 + i < 20]
            for st in (st1, st2, st3, st4):
                for (sl, c) in batch:
                    st(sl, c)
            a = 0
            for lev in range(4):
                for (sl, c) in batch:
                    st5a(sl, c, a)
                for (sl, c) in batch:
                    st5b(sl, c, a)
                a = 1 - a
            for (sl, c) in batch:
                st6a(sl, c, a)
            for it in range(3):
                for (sl, c) in batch:
                    st6b(sl, c, a)
            for (sl, c) in batch:
                st6c(sl, c, a)
        if GDN_STOP == "%d,%d,g2" % (hd, d):
            raise StopGDN()
        P.release(mks)
        chains = [("s", list(range(0, 16))), ("pA", [16, 17]), ("pB", [18, 19])]
        if d == 1:
            chains = [(nm, cs[::-1]) for nm, cs in chains]
        Sst = {nm: [P.sb("gS%s%d" % (nm, i), [128, 128]) for i in range(2)] for nm, _ in chains}
        vnew = [P.sb("gvnew%d" % i, [128, 128]) for i in range(3)]
        tq = [P.sb("gtq%d" % i, [128, 128]) for i in range(3)]
        kg2 = [P.sb("gkg2%d" % i, [128, 128]) for i in range(3)]
        P.dma(Sst["s"][0][:], I.sg[l, d, hd])
        P.memset(Sst["pA"][0][:], 0.0); P.memset(Sst["pB"][0][:], 0.0)
        cur = {nm: 0 for nm, _ in chains}
        for step in range(16):
            for ci, (nm, cs) in enumerate(chains):
                if step >= len(cs):
                    continue
                c = cs[step]
                ch = slice(c * 128, (c + 1) * 128)
                S = Sst[nm][cur[nm]]; Sn = Sst[nm][1 - cur[nm]]
                bk = K.ps[6 + (ci % 2)] if ci < 2 else K.ps[0]
                r0, r1, r2, r3 = (bk[:, i * 128:(i + 1) * 128] for i in range(4))
                vn = vnew[ci]; t = tq[ci]
                P.mm(r0, wTall[:, c, :], S[:])
                P.mm(r1, qT[:, ch], S[:])
                P.tt(vn[:], uall[:, c, :], r0, ALU.subtract)
                P.mm(r2, qkTall[:, c, :], vn[:])
                P.ts(kg2[ci][:], ktok[:, c, :], e2all[:, c:c + 1], None, ALU.mult, e="pool")
                P.mm(r3, kg2[ci][:], vn[:])
                P.ts(t[:], r1, egc[:, c:c + 1], None, ALU.mult)
                if d == 0:
                    P.tt(oacc[:, c, :], r2, t[:], ALU.add)
                else:
                    P.tt(t[:], r2, t[:], ALU.add)
                    P.tt(oacc[:, c, :], oacc[:, c, :], t[:], ALU.add, e="pool")
                P.stt(Sn[:], S[:], egl[:, c:c + 1], r3, ALU.mult, ALU.add)
                cur[nm] = 1 - cur[nm]
        if GDN_STOP == "%d,%d,g3" % (hd, d):
            raise StopGDN()
        for pi, nm in enumerate(("pA", "pB")):
            P.dma(O.ng[pi, l, d, hd], Sst[nm][cur[nm]][:], is_output=True)
        P.release(mk2)
    mk3 = P.mark()
    gsl = P.sb("ggsl", [128, T])
    w = load_w(K, I.w_in[l][:, C_GDNG + hd * 128:C_GDNG + (hd + 1) * 128], 128)
    for tt in range(NTT):
        ps = K.ps[tt % 8][:]
        proj(K, ps, w, 0, 128, tt)
        P.act(gsl[:, tt * 512:(tt + 1) * 512], ps, AF.Silu)
    sq = P.sb("gsq", [128, 128]); ssum = P.sb("gssum", [128, 20])
    for c in range(20):
        P.tt(sq[:], oacc[:, c, :], oacc[:, c, :], ALU.mult)
        P.op("dve", lambda g, c=c: g.tensor_reduce(ssum[:, c:c + 1], sq[:], mybir.AxisListType.X, ALU.add),
             [sq[:]], [ssum[:, c:c + 1]])
    P.act(ssum[:], ssum[:], AF.Sqrt, bias=K.eps[:, 0:1], scale=1.0 / 128.0)
    P.op("dve", lambda g: g.reciprocal(ssum[:], ssum[:]), [ssum[:]], [ssum[:]])
    for c in range(20):
        P.stt(oacc[:, c, :], oacc[:, c, :], ssum[:, c:c + 1], G.ng[:], ALU.mult, ALU.mult)
    for g5 in range(5):
        ps = K.ps[g5 % 8]
        for q in range(4):
            c = g5 * 4 + q
            P.transpose(ps[:, q * 128:(q + 1) * 128], oacc[:, c, :], K.ident[:])
        P.tt(K.yb[:, hd, g5 * 512:(g5 + 1) * 512], ps[:], gsl[:, g5 * 512:(g5 + 1) * 512], ALU.mult)
    if K.dbg and K.stop == ("gdn", l):
        pass
    P.release(mk3)
    P.release(mk)


def _grid_pos(n_tokens, dim):
    t = np.arange(n_tokens)
    r = (t // 64).astype(np.float32)
    col = (t % 64).astype(np.float32)
    quarter = dim // 4
    omega = (1.0 / (10000.0 ** (np.arange(quarter, dtype=np.float32) / np.float32(quarter)))).astype(np.float32)
    er = r[:, None] * omega
    ec = col[:, None] * omega
    return np.concatenate([np.sin(er), np.cos(er), np.sin(ec), np.cos(ec)], axis=-1).astype(np.float32)


_CONSTS = None


def host_consts():
    global _CONSTS
    if _CONSTS is None:
        c = {}
        c["ident"] = np.eye(128, dtype=np.float32)
        c["pos"] = _grid_pos(2048, 1024)
        c.update(extra_consts())
        _CONSTS = c
    return _CONSTS


_PROG = None

WEIGHT_NAMES = ["w_mod", "b_mod", "norm_g", "w_in", "hy_conv_w", "hy_conv_b", "hy_f_w1", "hy_f_b1", "hy_f_w2", "hy_f_b2",
                "hy_f_w3", "hy_f_freq", "hy_bias", "s5_lambda_re", "s5_lambda_im", "s5_log_step", "s5_B_re", "s5_B_im",
                "s5_C_re", "s5_C_im", "s5_D", "s5_glu_w", "s5_glu_b", "gdn_conv_w", "gdn_A_log", "gdn_dt_bias",
                "gdn_norm_g", "w_branch", "w_out", "final_norm_g"]


def make_in_maps(inputs, ncores=8):
    consts = host_consts()
    f = lambda a: np.ascontiguousarray(np.asarray(a, dtype=np.float32))
    w = {k: f(inputs[k]) for k in WEIGHT_NAMES}
    xs, xp, c, cctx = f(inputs["x_sample"]), f(inputs["x_prompt"]), f(inputs["c"]), f(inputs["c_ctx"])
    sg, ss = f(inputs["state_gdn"]), f(inputs["state_s5"])
    maps = []
    for b in range(ncores):
        m = dict(w)
        m.update(consts)
        m["xs"] = xs[b]
        m["xp"] = np.ascontiguousarray(xp[2 * b:2 * b + 2].reshape(512, 1024))
        m["cvec"] = np.ascontiguousarray(np.stack([cctx, c[b]], axis=0))
        m["sg"] = sg[b]
        m["ss"] = ss[b]
        maps.append(m)
    return maps


def kernel(**inputs):
    global _PROG
    if _PROG is None:
        _PROG = build_program()
    nc, K = _PROG
    maps = make_in_maps(inputs)
    res = run_bass_kernel_spmd(nc, maps, core_ids=list(range(8)))
    r = res.results
    y_sample = np.stack([r[b]["ys"] for b in range(8)], axis=0).astype(np.float32)
    y_prompt = np.concatenate([r[b]["yp"].reshape(2, 256, 1024) for b in range(8)], axis=0).astype(np.float32)
    ng = np.concatenate([r[b]["ng"] for b in range(8)], axis=0).astype(np.float32)
    ns = np.concatenate([r[b]["ns"] for b in range(8)], axis=0).astype(np.float32)
    return (y_prompt, y_sample, ng, ns)
```

```python
import numpy as np
import concourse.bass as bass
import concourse.mybir as mybir
from concourse.alu_op_type import AluOpType as ALU

F32 = mybir.dt.float32
BF16 = mybir.dt.bfloat16
AF = mybir.ActivationFunctionType
import os as _os0
FP32R = _os0.environ.get("FP32R", "0") == "1"
LOADQ = _os0.environ.get("LOADQ", "act")


class Prog:
    NDMA = 24

    def __init__(self, nc, same_engine_sync=True):
        self.nc = nc
        self.eng = {"pe": nc.tensor, "act": nc.scalar, "dve": nc.vector, "pool": nc.gpsimd, "sp": nc.sync}
        self.stack = []
        self.sem = {}
        for e in self.eng:
            cm = nc.semaphore("s_" + e)
            self.sem[e] = cm.__enter__()
            self.stack.append(cm)
        self.cnt = {e: 0 for e in self.eng}
        self.dsem = []
        for i in range(self.NDMA):
            cm = nc.semaphore("d_%d" % i)
            self.dsem.append(cm.__enter__())
            self.stack.append(cm)
        self.dval = [0] * self.NDMA
        self.dnext = 0
        self.seen = {e: {} for e in self.eng}
        self.recs = {}
        self.same = same_engine_sync
        self.out_dmas = []
        self.n_inst = 0
        self.tensors = {}
        self.ro = set()

    def sb(self, name, shape, dt=F32):
        self.uid = getattr(self, "uid", 0) + 1
        if not hasattr(self, "names"):
            self.names = {}
        self.names.setdefault(name, []).append("%s_%d" % (name, self.uid))
        name = "%s_%d" % (name, self.uid)
        cm = self.nc.sbuf_tensor(name, list(shape), dt)
        t = cm.__enter__()
        self.stack.append(cm)
        return t

    def ps(self, name, shape, dt=F32):
        cm = self.nc.psum_tensor(name, list(shape), dt)
        t = cm.__enter__()
        self.stack.append(cm)
        return t

    @staticmethod
    def region(ap):
        t = ap.tensor
        name = t.name
        shp = list(t.shape)
        space = str(ap.space)
        off = int(ap.offset)
        dims = [(int(s), int(c)) for (s, c) in ap.ap]
        if "DRAM" in space.upper() or "HBM" in space.upper() or type(t).__name__.startswith("DRam"):
            lo = off
            hi = off
            for s, c in dims:
                if s >= 0:
                    hi += s * (c - 1)
                else:
                    lo += s * (c - 1)
            return (name, 0, 1, lo, hi)
        if name.startswith("ps") and "PSUM" in space.upper():
            return (name, 0, 128, 0, 1 << 30)
        pitch = 1
        for d in shp[1:]:
            pitch *= int(d)
        p0 = off // pitch
        f0 = off % pitch
        ps_, pc = dims[0]
        if ps_ == 0:
            p1 = p0 + 1
        else:
            p1 = p0 + pc * max(1, ps_ // pitch)
        lo = f0
        hi = f0
        for s, c in dims[1:]:
            if s >= 0:
                hi += s * (c - 1)
            else:
                lo += s * (c - 1)
        return (name, p0, p1, lo, hi)

    def _deps(self, reads, writes):
        deps = set()
        for ap in reads:
            name, p0, p1, lo, hi = self.region(ap)
            for r in self.recs.get(name, ()):
                if r[0] < p1 and p0 < r[1] and r[2] <= hi and lo <= r[3]:
                    if r[4] is not None:
                        deps.add(r[4])
        for ap in writes:
            name, p0, p1, lo, hi = self.region(ap)
            for r in self.recs.get(name, ()):
                if r[0] < p1 and p0 < r[1] and r[2] <= hi and lo <= r[3]:
                    if r[4] is not None:
                        deps.add(r[4])
                    for rd in r[5].items():
                        deps.add(rd)
        return deps

    def _record(self, reads, writes, me):
        for ap in reads:
            name, p0, p1, lo, hi = self.region(ap)
            if name in self.ro:
                continue
            lst = self.recs.setdefault(name, [])
            found = False
            for r in lst:
                if r[0] == p0 and r[1] == p1 and r[2] == lo and r[3] == hi:
                    if r[5].get(me[0], 0) < me[1]:
                        r[5][me[0]] = me[1]
                    found = True
            if not found:
                lst.append([p0, p1, lo, hi, None, {me[0]: me[1]}])
        for ap in writes:
            name, p0, p1, lo, hi = self.region(ap)
            lst = self.recs.setdefault(name, [])
            keep = []
            for r in lst:
                if p0 <= r[0] and r[1] <= p1 and lo <= r[2] and r[3] <= hi:
                    continue
                keep.append(r)
            keep.append([p0, p1, lo, hi, me, {}])
            self.recs[name] = keep

    def _wait(self, e, deps):
        eng = self.eng[e]
        best = {}
        for (src, val) in deps:
            if src == e and (not self.same or e == "pe"):
                continue
            if val > best.get(src, 0):
                best[src] = val
        for src, val in best.items():
            if self.seen[e].get(src, 0) >= val:
                continue
            self.seen[e][src] = val
            sem = self.sem[src] if isinstance(src, str) else self.dsem[src]
            eng.wait_ge(sem, val)

    def op(self, e, fn, reads, writes):
        deps = self._deps(reads, writes)
        self._wait(e, deps)
        inst = fn(self.eng[e])
        self.cnt[e] += 1
        inst.then_inc(self.sem[e], 1)
        self._record(reads, writes, (e, self.cnt[e]))
        self.n_inst += 1
        return inst

    def dma(self, out, in_, q=None, is_output=False, **kw):
        if q is None:
            is_load = not type(out.tensor).__name__.startswith("DRam")
            q = LOADQ if is_load else "sp"
        deps = self._deps([in_], [out])
        k = self.dnext
        self.dnext = (self.dnext + 1) % self.NDMA
        if self.dval[k] > 0:
            deps.add((k, self.dval[k]))
        self._wait(q, deps)
        inst = self.eng[q].dma_start(out=out, in_=in_, **kw)
        self.dval[k] += 16
        inst.then_inc(self.dsem[k], 16)
        me = (k, self.dval[k])
        self._record([in_], [out], me)
        if is_output:
            self.out_dmas.append(me)
        self.n_inst += 1
        return inst

    def finish(self):
        e = "sp"
        self._wait(e, set(self.out_dmas) | {(k, v) for k, v in enumerate(self.dval) if v > 0}
                   | {(x, self.cnt[x]) for x in self.eng if x != e and self.cnt[x] > 0})

    def barrier(self):
        allw = {(x, self.cnt[x]) for x in self.eng if self.cnt[x] > 0} | {(k, v) for k, v in enumerate(self.dval) if v > 0}
        for e in self.eng:
            self._wait(e, set(allw))

    def mark(self):
        return len(self.stack)

    def release(self, m):
        self.barrier()
        while len(self.stack) > m:
            cm = self.stack.pop()
            cm.__exit__(None, None, None)

    def close(self):
        for cm in reversed(self.stack):
            cm.__exit__(None, None, None)
        self.stack = []

    def mm(self, out, lhsT, rhs, start=True, stop=True, **kw):
        if FP32R and lhsT.dtype == F32 and rhs.dtype == F32 and int(rhs.ap[-1][1]) % 2 == 0:
            lhsT = lhsT.bitcast(mybir.dt.float32r)
            rhs = rhs.bitcast(mybir.dt.float32r)
        return self.op("pe", lambda g: g.matmul(out, lhsT, rhs, start=start, stop=stop, **kw), [lhsT, rhs], [out])

    def transpose(self, out, in_, ident):
        return self.op("pe", lambda g: g.transpose(out, in_, ident), [in_, ident], [out])

    def act(self, out, in_, func, bias=None, scale=None, e="act", extra_reads=()):
        kw = {}
        rd = [in_] + list(extra_reads)
        if bias is not None:
            kw["bias"] = bias
            if not isinstance(bias, (int, float)):
                rd.append(bias)
        if scale is not None:
            kw["scale"] = scale
            if not isinstance(scale, (int, float)):
                rd.append(scale)
        return self.op(e, lambda g: g.activation(out, in_, func, **kw), rd, [out])

    def tt(self, out, in0, in1, op, e="dve"):
        return self.op(e, lambda g: g.tensor_tensor(out, in0, in1, op), [in0, in1], [out])

    def ts(self, out, in0, s1, s2, op0, op1=None, e="dve", accum_out=None):
        rd = [in0]
        for s in (s1, s2):
            if s is not None and not isinstance(s, (int, float)):
                rd.append(s)
        wr = [out]
        kw = {}
        if accum_out is not None:
            kw["accum_out"] = accum_out
            wr.append(accum_out)
        if op1 is None:
            return self.op(e, lambda g: g.tensor_scalar(out, in0, s1, s2, op0, **kw), rd, wr)
        return self.op(e, lambda g: g.tensor_scalar(out, in0, s1, s2, op0, op1, **kw), rd, wr)

    def stt(self, out, in0, scalar, in1, op0, op1, e="dve"):
        rd = [in0, in1]
        if not isinstance(scalar, (int, float)):
            rd.append(scalar)
        return self.op(e, lambda g: g.scalar_tensor_tensor(out, in0, scalar, in1, op0, op1), rd, [out])

    def copy(self, out, in_, e="dve"):
        if e == "act":
            return self.op(e, lambda g: g.copy(out, in_), [in_], [out])
        return self.op(e, lambda g: g.tensor_copy(out, in_), [in_], [out])

    def memset(self, ap, val, e="pool"):
        return self.op(e, lambda g: g.memset(ap, val), [], [ap])

from concourse.bass_utils import run_bass_kernel_spmd
import ml_dtypes
import math

I32 = mybir.dt.int32
T = 2560
NTT = 5
D_MODEL = 1024
SEQS = [(0, 2048), (2048, 256), (2304, 256)]
C_HYG, C_S5, C_S5G, C_GDN, C_BETA, C_ALPHA, C_GDNG, C_MERGE = 3072, 4096, 5120, 6144, 9216, 9232, 9248, 10272
IN_COLS = 13344
MIX = {"hy": True, "s5": True, "gdn": True}
NLAYER = 2


class Ctx:
    pass


def build_program(dbg=None, stop=None):
    nc = bass.Bass("TRN2", target_bir_lowering=False)
    P = Prog(nc)
    K = Ctx()
    K.nc, K.P = nc, P
    K.dbg = dbg
    K.stop = stop

    def din(name, shape, dt=F32):
        P.ro.add(name)
        return nc.dram_tensor(name, list(shape), dt, kind="ExternalInput").ap()

    def dout(name, shape, dt=F32):
        return nc.dram_tensor(name, list(shape), dt, kind="ExternalOutput").ap()

    def dscr(name, shape, dt=F32):
        return nc.dram_tensor(name, list(shape), dt, kind="Internal").ap()

    K.din, K.dout, K.dscr = din, dout, dscr
    I = Ctx()
    K.I = I
    I.xs = din("xs", [2048, 1024]); I.xp = din("xp", [512, 1024]); I.cvec = din("cvec", [2, 1024])
    I.sg = din("sg", [2, 2, 8, 128, 128]); I.ss = din("ss", [2, 2, 64, 64, 2])
    I.w_mod = din("w_mod", [2, 1024, 3072]); I.b_mod = din("b_mod", [2, 3072]); I.norm_g = din("norm_g", [2, 1024])
    I.w_in = din("w_in", [2, 1024, IN_COLS])
    I.hy_conv_w = din("hy_conv_w", [2, 3, 3072]); I.hy_conv_b = din("hy_conv_b", [2, 3072])
    I.hy_f_w1 = din("hy_f_w1", [2, 33, 64]); I.hy_f_b1 = din("hy_f_b1", [2, 64])
    I.hy_f_w2 = din("hy_f_w2", [2, 64, 64]); I.hy_f_b2 = din("hy_f_b2", [2, 64])
    I.hy_f_w3 = din("hy_f_w3", [2, 64, 4096]); I.hy_f_freq = din("hy_f_freq", [2, 64]); I.hy_bias = din("hy_bias", [2, 2, 1024])
    I.s5_lambda_re = din("s5_lambda_re", [2, 2, 64, 64]); I.s5_lambda_im = din("s5_lambda_im", [2, 2, 64, 64])
    I.s5_log_step = din("s5_log_step", [2, 2, 64])
    I.s5_B_re = din("s5_B_re", [2, 2, 64, 64, 16]); I.s5_B_im = din("s5_B_im", [2, 2, 64, 64, 16])
    I.s5_C_re = din("s5_C_re", [2, 2, 64, 16, 64]); I.s5_C_im = din("s5_C_im", [2, 2, 64, 16, 64])
    I.s5_D = din("s5_D", [2, 1024]); I.s5_glu_w = din("s5_glu_w", [2, 1024, 2048]); I.s5_glu_b = din("s5_glu_b", [2, 2048])
    I.gdn_conv_w = din("gdn_conv_w", [2, 3, 3072]); I.gdn_A_log = din("gdn_A_log", [2, 2, 8]); I.gdn_dt_bias = din("gdn_dt_bias", [2, 2, 8])
    I.gdn_norm_g = din("gdn_norm_g", [2, 128]); I.w_branch = din("w_branch", [2, 3, 1024, 1024]); I.w_out = din("w_out", [2, 1024, 1024])
    I.final_norm_g = din("final_norm_g", [1024])
    I.ident = din("ident", [128, 128]); I.pos = din("pos", [2048, 1024])
    O = Ctx()
    K.O = O
    O.ys = dout("ys", [2048, 1024]); O.yp = dout("yp", [512, 1024])
    O.ng = dout("ng", [2, 2, 2, 8, 128, 128]); O.ns = dout("ns", [2, 2, 2, 64, 64, 2])
    if dbg:
        K.dbg_out = dout("dbg", list(dbg), BF16)
        K.dbg32 = dout("dbg32", list(dbg))
    K.xT = dscr("xT_scr", [1024, T])
    K.xTv = K.xT.rearrange("(kc k) t -> k kc t", k=128)

    with nc.allow_non_contiguous_dma("layout"), nc.allow_low_precision("bf16 matmul per problem tolerance"):
        K.ps = [P.ps("ps%d" % i, [128, 512]) for i in range(8)]
        K.ident = P.sb("ident_sb", [128, 128]); P.dma(K.ident[:], I.ident)
        K.ones = P.sb("ones", [128, 128]); P.memset(K.ones[:], 1.0)
        K.wst = [P.sb("wst0", [128, 8, 128])] * 2
        K.wbf = [P.sb("wbf0", [128, 8, 128], BF16)] * 2
        K.eps = P.sb("eps", [128, 1]); P.memset(K.eps[:], 1e-6)
        K.one1 = P.sb("one1", [128, 1]); P.memset(K.one1[:], 1.0)
        K.wi = 0
        K.MG = dscr("mg_scr", [1024, T])
        K.MGv = K.MG.rearrange("(kc k) t -> k kc t", k=128)
        phase_mod(K)
        prep_all(K)
        K.h = P.sb("h", [128, 8, T], BF16)
        phase_input(K)
        for l in range(NLAYER):
            phase_norm(K, l)
            mk = P.mark()
            K.yb = P.sb("yb", [128, 8, T], BF16)
            for n, nm in enumerate(["hy", "s5", "gdn"]):
                if MIX[nm]:
                    {"hy": mixer_hyena, "s5": mixer_s5, "gdn": mixer_gdn}[nm](K, l)
                else:
                    mixer_stub(K, l, n)
                if K.stop == (nm, l):
                    for kc in range(8):
                        P.dma(K.dbg_out[kc * 128:(kc + 1) * 128, :], K.yb[:, kc, :], is_output=True)
                    P.finish(); P.close()
                    return nc, K
                phase_merge(K, l, n)
            P.release(mk)
            phase_out(K, l)
            if K.stop == ("out", l):
                mk_ = P.mark()
                xt_ = P.sb("dbgx", [128, 8, 512])
                for tt in range(NTT):
                    P.dma(xt_[:], K.xTv[:, :, tt * 512:(tt + 1) * 512])
                    P.dma(K.dbg32.rearrange("(kc k) t -> k kc t", k=128)[:, :, tt * 512:(tt + 1) * 512], xt_[:], is_output=True)
                P.finish(); P.close()
                return nc, K
        phase_final(K)
        P.finish()
    P.close()
    return nc, K


def load_w(K, src, ncols, eng="dve"):
    P = K.P
    i = K.wi
    K.wi ^= 1
    st = K.wst[i][:, :, 0:ncols]
    P.dma(st, src.rearrange("(kc k) c -> k kc c", k=128))
    bf = K.wbf[i][:, :, 0:ncols]
    P.copy(bf, st, e=eng)
    return bf


def proj(K, ps, wbf, c0, ncols, tt, act=None, ntok=512, t0=None):
    P = K.P
    if t0 is None:
        t0 = tt * 512
    src = K.h if act is None else act
    for kc in range(8):
        P.mm(ps, wbf[:, kc, c0:c0 + ncols], src[:, kc, t0:t0 + ntok], start=(kc == 0), stop=(kc == 7))


def phase_mod(K):
    P, I = K.P, K.I
    K.modT = P.sb("modT", [128, 2, 24, 2])
    K.gs = P.sb("gs", [128, 2, 8, 2])
    K.fg = P.sb("fgT", [128, 8]); P.dma(K.fg[:], I.final_norm_g.rearrange("(kc k) -> k kc", k=128))
    mk = P.mark()
    cT = P.sb("cT", [128, 8, 2]); [P.dma(cT[:, :, w], I.cvec[w].rearrange("(kc k) -> k kc", k=128)) for w in range(2)]
    sc = P.sb("sc", [128, 8, 2]); P.act(sc[:], cT[:], AF.Silu)
    bmodT = P.sb("bmodT", [128, 2, 24]); [P.dma(bmodT[:, l, :], I.b_mod[l].rearrange("(ft f) -> f ft", f=128)) for l in range(2)]
    wm = [P.sb("wm%d" % i, [128, 8, 512]) for i in range(2)]
    for l in range(NLAYER):
        for g6 in range(6):
            w = wm[g6 % 2]
            P.dma(w[:], I.w_mod[l].rearrange("(kc k) c -> k kc c", k=128)[:, :, g6 * 512:(g6 + 1) * 512])
            for f4 in range(4):
                ft = g6 * 4 + f4
                ps = K.ps[ft % 8][:, 0:2]
                for kc in range(8):
                    P.mm(ps, w[:, kc, f4 * 128:(f4 + 1) * 128], sc[:, kc, :], start=(kc == 0), stop=(kc == 7))
                P.ts(K.modT[:, l, ft, :], ps, bmodT[:, l, ft:ft + 1], None, ALU.add)
    ng = P.sb("normgT", [128, 2, 8]); [P.dma(ng[:, l, :], I.norm_g[l].rearrange("(kc k) -> k kc", k=128)) for l in range(2)]
    for l in range(NLAYER):
        for w in range(2):
            P.ts(K.gs[:, l, :, w], K.modT[:, l, 8:16, w], 1.0, None, ALU.add)
            P.tt(K.gs[:, l, :, w], K.gs[:, l, :, w], ng[:, l, :], ALU.mult)
    P.release(mk)


def phase_input(K):
    P, I = K.P, K.I
    mk = P.mark()
    xt = [P.sb("xin%d" % i, [128, 1024]) for i in range(2)]
    pt = [P.sb("pin%d" % i, [128, 1024]) for i in range(2)]
    xo = [P.sb("xo%d" % i, [128, 8, 128]) for i in range(2)]
    for tb in range(20):
        x = xt[tb % 2]
        if tb < 16:
            P.dma(x[:], I.xs[tb * 128:(tb + 1) * 128, :])
            p = pt[tb % 2]
            P.dma(p[:], I.pos[tb * 128:(tb + 1) * 128, :])
            P.tt(x[:], x[:], p[:], ALU.add)
        else:
            P.dma(x[:], I.xp[(tb - 16) * 128:(tb - 15) * 128, :])
        o = xo[tb % 2]
        for half in range(2):
            ps = K.ps[(tb * 2 + half) % 8]
            for q in range(4):
                kc = half * 4 + q
                P.transpose(ps[:, q * 128:(q + 1) * 128], x[:, kc * 128:(kc + 1) * 128], K.ident[:])
            P.copy(o[:, half * 4:(half + 1) * 4, :], ps[:].rearrange("p (q t) -> p q t", q=4), e=("act" if half else "dve"))
        P.dma(K.xTv[:, :, tb * 128:(tb + 1) * 128], o[:])
    P.release(mk)


def norm_tile(K, tt, xtile, sq, r, psS):
    P = K.P
    P.dma(xtile[:], K.xTv[:, :, tt * 512:(tt + 1) * 512])
    P.act(sq[:], xtile[:], AF.Square)
    for kc in range(8):
        P.mm(psS, K.ones[:], sq[:, kc, :], start=(kc == 0), stop=(kc == 7))
    P.act(r[:], psS, AF.Sqrt, bias=K.eps[:, 0:1], scale=1.0 / 1024.0)
    P.op("dve", lambda g: g.reciprocal(r[:], r[:]), [r[:]], [r[:]])


def phase_norm(K, l):
    P = K.P
    mk = P.mark()
    norm_alloc(K)
    for tt in range(NTT):
        x = K.nx[tt % 2]; r = K.nr[tt % 2]
        norm_tile(K, tt, x, K.nsq, r, K.ps[tt % 8][:])
        w = 1 if tt < 4 else 0
        for kc in range(8):
            P.tt(K.ntmp[:, kc, :], x[:, kc, :], r[:], ALU.mult)
            P.ts(K.h[:, kc, tt * 512:(tt + 1) * 512], K.ntmp[:, kc, :], K.gs[:, l, kc, w:w + 1], K.modT[:, l, kc, w:w + 1],
                 ALU.mult, ALU.add, e=("pool" if kc % 2 else "dve"))
    P.release(mk)


def norm_alloc(K):
    P = K.P
    K.nx = [P.sb("nx0", [128, 8, 512])] * 2
    K.nsq = P.sb("nsq", [128, 8, 512])
    K.nr = [P.sb("nr%d" % i, [128, 512]) for i in range(2)]
    K.ntmp = K.nsq


def mixer_stub(K, l, n):
    P = K.P
    for kc in range(8):
        P.copy(K.yb[:, kc, :], K.h[:, kc, :], e=("pool" if kc % 2 else "dve"))


def phase_merge(K, l, n):
    P, I = K.P, K.I
    mk = P.mark()
    msig = [P.sb("msig%d" % i, [128, 512]) for i in range(2)]
    mpr = [P.sb("mpr%d" % i, [128, 512]) for i in range(2)]
    mold = [P.sb("mold%d" % i, [128, 512]) for i in range(2)]
    wbl = [P.sb("wbl%d" % i, [128, 8, 128], BF16) for i in range(2)]
    it = 0
    for ct in range(8):
        wb0 = load_w(K, I.w_branch[l, n][:, ct * 128:(ct + 1) * 128], 128)
        wb = wbl[ct % 2][:]
        P.copy(wb, wb0, e="pool")
        wm = load_w(K, I.w_in[l][:, C_MERGE + n * 1024 + ct * 128: C_MERGE + n * 1024 + (ct + 1) * 128], 128)
        for tt in range(NTT):
            psp = K.ps[(it * 2) % 8][:]; psg = K.ps[(it * 2 + 1) % 8][:]
            sig = msig[it % 2]; pr = mpr[it % 2]; old = mold[it % 2]
            it += 1
            dst = K.MGv[:, ct, tt * 512:(tt + 1) * 512]
            if n > 0:
                P.dma(old[:], dst)
            proj(K, psg, wm, 0, 128, tt)
            proj(K, psp, wb, 0, 128, tt, act=K.yb)
            P.act(sig[:], psg, AF.Sigmoid)
            P.tt(pr[:], psp, sig[:], ALU.mult)
            if n > 0:
                P.tt(pr[:], pr[:], old[:], ALU.add)
            P.dma(dst, pr[:])
    P.release(mk)


def phase_out(K, l):
    P, I = K.P, K.I
    mk = P.mark()
    ox = [P.sb("ox%d" % i, [128, 512]) for i in range(2)]
    wo = P.sb("wo_res", [128, 8, 1024], BF16)
    mg = P.sb("mg_in", [128, 8, 512])
    mgb = [P.sb("mg_bf%d" % i, [128, 8, 512], BF16) for i in range(2)]
    for c4 in range(8):
        w = load_w(K, I.w_out[l][:, c4 * 128:(c4 + 1) * 128], 128)
        P.copy(wo[:, :, c4 * 128:(c4 + 1) * 128], w, e="pool")
    it = 0
    for tt in range(NTT):
        P.dma(mg[:], K.MGv[:, :, tt * 512:(tt + 1) * 512])
        m = mgb[tt % 2]
        P.copy(m[:], mg[:], e="act")
        for ct in range(8):
            ps = K.ps[it % 8][:]
            x = ox[it % 2]
            it += 1
            P.dma(x[:], K.xTv[:, ct, tt * 512:(tt + 1) * 512])
            for kc in range(8):
                P.mm(ps, wo[:, kc, ct * 128:(ct + 1) * 128], m[:, kc, :], start=(kc == 0), stop=(kc == 7))
            w = 1 if tt < 4 else 0
            P.stt(x[:], ps, K.modT[:, l, 16 + ct, w:w + 1], x[:], ALU.mult, ALU.add)
            P.dma(K.xTv[:, ct, tt * 512:(tt + 1) * 512], x[:])
    P.release(mk)


def phase_final(K):
    P, O = K.P, K.O
    mk = P.mark()
    norm_alloc(K)
    yo = [P.sb("yo%d" % i, [128, 1024]) for i in range(2)]
    it = 0
    for tt in range(NTT):
        x = K.nx[tt % 2]; r = K.nr[tt % 2]
        norm_tile(K, tt, x, K.nsq, r, K.ps[tt % 8][:])
        for kc in range(8):
            P.tt(K.ntmp[:, kc, :], x[:, kc, :], r[:], ALU.mult)
            P.ts(K.ntmp[:, kc, :], K.ntmp[:, kc, :], K.fg[:, kc:kc + 1], None, ALU.mult, e=("pool" if kc % 2 else "dve"))
        for b in range(4):
            y = yo[it % 2]
            for half in range(2):
                ps = K.ps[(it * 2 + half) % 8]
                for q in range(4):
                    kc = half * 4 + q
                    P.transpose(ps[:, q * 128:(q + 1) * 128], K.ntmp[:, kc, b * 128:(b + 1) * 128], K.ident[:])
                P.copy(y[:, half * 512:(half + 1) * 512], ps[:], e=("act" if half else "dve"))
            it += 1
            t0 = tt * 512 + b * 128
            if t0 < 2048:
                P.dma(O.ys[t0:t0 + 128, :], y[:], is_output=True)
            else:
                P.dma(O.yp[t0 - 2048:t0 - 2048 + 128, :], y[:], is_output=True)


HY_L = (2048, 256)


def _dft_consts(L):
    N = 2 * L
    t = np.arange(L, dtype=np.int64)[:, None]
    j = np.arange(N, dtype=np.int64)[None, :]
    f = np.where(j < L, j, j - L)
    ang = 2.0 * np.pi * ((t * f) % N).astype(np.float64) / N
    Dm = np.where(j < L, np.cos(ang), np.sin(ang))
    Dm[:, L] = np.where(np.arange(L) % 2 == 0, 1.0, -1.0)
    wg = np.full(N, 2.0 / N)
    wg[0] = 1.0 / N
    wg[L] = 1.0 / N
    return Dm, wg


def extra_consts():
    c = {}
    f32 = np.float32
    for L in HY_L:
        Dm, wg = _dft_consts(L)
        c["dft%d" % L] = np.ascontiguousarray(Dm.astype(ml_dtypes.bfloat16))
        c["dftT%d" % L] = np.ascontiguousarray(Dm.T.astype(ml_dtypes.bfloat16))
        c["wgt%d" % L] = np.ascontiguousarray(wg.astype(f32).reshape(-1, 128).T)
        t = np.linspace(0.0, 1.0, L, dtype=f32)[:, None]
        bands = 16
        fb = np.linspace(1e-4, bands - 1, bands, dtype=f32)[None, :]
        wpos = (f32(2.0 * math.pi) * np.arange(L, dtype=f32)[:, None] / f32(L)).astype(f32)
        z = np.concatenate([t, np.cos(fb * wpos), -np.sin(fb * wpos)], axis=-1).astype(f32)
        c["zfT%d" % L] = np.ascontiguousarray(z.T)
        c["negt%d" % L] = np.ascontiguousarray((-t[:, 0]).reshape(-1, 128).T.astype(f32))
    deltas = np.abs(np.linspace(math.log(1e-2) / 1.5, math.log(1e-2) / 0.3, 1024, dtype=f32))
    c["deltas"] = np.ascontiguousarray(np.broadcast_to(deltas[None, :], (128, 1024)).astype(f32))
    c.update(s5_host_consts())
    c.update(gdn_host_consts())
    return c


def sin_rr(K, out, x, tmp_i, tmp_f):
    P = K.P
    P.ts(tmp_i, x, 1.0 / (2.0 * math.pi), None, ALU.mult)
    P.copy(tmp_f, tmp_i)
    P.stt(tmp_f, tmp_f, -2.0 * math.pi, x, ALU.mult, ALU.add)
    P.ts(tmp_f, tmp_f, 3.1415925, -3.1415925, ALU.min, ALU.max)
    P.act(out, tmp_f, AF.Sin)


def prep_all(K):
    P, I = K.P, K.I
    din, dscr = K.din, K.dscr
    I.dft = {L: din("dft%d" % L, [L, 2 * L], BF16) for L in HY_L}
    I.dftT = {L: din("dftT%d" % L, [2 * L, L], BF16) for L in HY_L}
    I.wgt = {L: din("wgt%d" % L, [128, 2 * L // 128]) for L in HY_L}
    I.zfT = {L: din("zfT%d" % L, [33, L]) for L in HY_L}
    I.negt = {L: din("negt%d" % L, [128, L // 128]) for L in HY_L}
    I.deltas = din("deltas", [128, 1024])
    for nm in ("g_L", "g_Ls", "g_U", "g_Us", "g_bd", "g_nbd"):
        setattr(I, nm, din(nm, [128, 128]))
    I.s5_bm = din("s5_bm", [128, 128]); I.s5_bmg = din("s5_bmg", [128, 8]); I.s5_mask2 = din("s5_mask2", [128, 32])
    K.KH = {(l, L): dscr("kh_%d_%d" % (l, L), [2 * L, 2048]) for l in range(NLAYER) for L in HY_L}
    if not MIX["hy"]:
        return
    for l in range(NLAYER):
        for L in HY_L:
            filter_gen(K, l, L)


def filter_gen(K, l, L):
    P, I = K.P, K.I
    mk = P.mark()
    N = 2 * L
    NTB = L // 128
    NJ = N // 128
    CH = min(512, L)
    w1 = P.sb("fw1", [33, 64]); P.dma(w1[:], I.hy_f_w1[l])
    w2 = P.sb("fw2", [64, 64]); P.dma(w2[:], I.hy_f_w2[l])
    w3 = P.sb("fw3", [64, 4096]); P.dma(w3[:], I.hy_f_w3[l])
    b1 = P.sb("fb1", [64, 1]); P.dma(b1[:], I.hy_f_b1[l].rearrange("(f o) -> f o", o=1))
    b2 = P.sb("fb2", [64, 1]); P.dma(b2[:], I.hy_f_b2[l].rearrange("(f o) -> f o", o=1))
    fq = P.sb("ffq", [64, 1]); P.dma(fq[:], I.hy_f_freq[l].rearrange("(f o) -> f o", o=1))
    zf = P.sb("fzf", [33, L]); P.dma(zf[:], I.zfT[L])
    negt = P.sb("fnegt", [128, NTB]); P.dma(negt[:], I.negt[L])
    wgt = P.sb("fwgt", [128, NJ]); P.dma(wgt[:], I.wgt[L])
    dl = P.sb("fdl", [128, 1024]); P.dma(dl[:], I.deltas)
    h1 = P.sb("fh1", [64, L]); h2 = P.sb("fh2", [64, L])
    pre = P.sb("fpre", [64, 512]); ti = P.sb("fti", [64, 512], I32); tf = P.sb("ftf", [64, 512])
    for (wm, bb, src, dst, kk) in ((w1, b1, zf, h1, 33), (w2, b2, h1, h2, 64)):
        for c in range(L // CH):
            ps = K.ps[c % 8][0:64, 0:CH]
            P.mm(ps, wm[0:kk, :], src[0:kk, c * CH:(c + 1) * CH])
            P.ts(pre[:, 0:CH], ps, bb[:, 0:1], fq[:, 0:1], ALU.add, ALU.mult)
            sin_rr(K, dst[:, c * CH:(c + 1) * CH], pre[:, 0:CH], ti[:, 0:CH], tf[:, 0:CH])
    FS = P.sb("fFS", [128, NTB, 1024], BF16); FD = P.sb("fFD", [128, NTB, 1024], BF16)
    dec = P.sb("fdec", [128, 1024])
    ffw = P.sb("fffw", [128, 1024]); fbw = P.sb("ffbw", [128, 1024]); fab = P.sb("ffab", [128, 1024])
    rn = P.sb("frn", [128, 1024])
    Dt = [P.sb("fDt%d" % i, [128, NTB, 128], BF16) for i in range(2)]
    kh = [P.sb("fkh%d" % i, [128, 1024]) for i in range(2)]
    Dv = I.dft[L].rearrange("(tb t) j -> t tb j", t=128)
    for o in range(2):
        for tb in range(NTB):
            P.act(dec[:], dl[:], AF.Exp, scale=negt[:, tb:tb + 1])
            for d in range(2):
                for hf in range(2):
                    col = o * 2048 + d * 1024 + hf * 512
                    P.mm(K.ps[d * 2 + hf][:], h2[:, tb * 128:(tb + 1) * 128], w3[:, col:col + 512])
            for hf in range(2):
                P.tt(ffw[:, hf * 512:(hf + 1) * 512], K.ps[hf][:], dec[:, hf * 512:(hf + 1) * 512], ALU.mult)
                P.tt(fbw[:, hf * 512:(hf + 1) * 512], K.ps[2 + hf][:], dec[:, hf * 512:(hf + 1) * 512], ALU.mult)
            if tb == 0:
                P.memset(fbw[0:1, :], 0.0)
            first = (tb == 0)
            last = (tb == NTB - 1)
            P.act(fab[:], ffw[:], AF.Abs)
            for hf in range(2):
                P.mm(K.ps[4 + hf][:], K.ones[:], fab[:, hf * 512:(hf + 1) * 512], start=first, stop=False)
            P.act(fab[:], fbw[:], AF.Abs)
            for hf in range(2):
                P.mm(K.ps[4 + hf][:], K.ones[:], fab[:, hf * 512:(hf + 1) * 512], start=False, stop=last)
            P.tt(FS[:, tb, :], ffw[:], fbw[:], ALU.add)
            P.tt(FD[:, tb, :], ffw[:], fbw[:], ALU.subtract, e="pool")
        for q in range(2):
            P.op("dve", lambda g, q=q: g.reciprocal(rn[:, q * 512:(q + 1) * 512], K.ps[4 + q][:]),
                 [K.ps[4 + q][:]], [rn[:, q * 512:(q + 1) * 512]])
        for jt in range(NJ):
            dt_ = Dt[jt % 2]; ko = kh[jt % 2]
            P.dma(dt_[:], Dv[:, :, jt * 128:(jt + 1) * 128])
            src = FS if jt < NJ // 2 else FD
            for q in range(2):
                ps = K.ps[(jt % 2) * 2 + q][:]
                for tb in range(NTB):
                    P.mm(ps, dt_[:, tb, :], src[:, tb, q * 512:(q + 1) * 512], start=(tb == 0), stop=(tb == NTB - 1))
                P.stt(ko[:, q * 512:(q + 1) * 512], ps, wgt[:, jt:jt + 1], rn[:, q * 512:(q + 1) * 512], ALU.mult, ALU.mult)
            if jt == NJ // 2:
                for q in range(2):
                    ps = K.ps[6 + q][0:1, :]
                    for tb in range(NTB):
                        P.mm(ps, dt_[:, tb, 0:1], FS[:, tb, q * 512:(q + 1) * 512], start=(tb == 0), stop=(tb == NTB - 1))
                    P.stt(ko[0:1, q * 512:(q + 1) * 512], ps, wgt[0:1, jt:jt + 1], rn[0:1, q * 512:(q + 1) * 512], ALU.mult, ALU.mult)
            P.dma(K.KH[(l, L)][jt * 128:(jt + 1) * 128, o * 1024:(o + 1) * 1024], ko[:])
    P.release(mk)


def mixer_hyena(K, l):
    P, I = K.P, K.I
    if not hasattr(K, "hy_scr"):
        K.hy_scr = {nm: K.dscr("hy_" + nm, [1024, T]) for nm in ("x1", "x2", "gs", "zc")}
        K.hy_ch = K.dscr("hy_ch", [4096, 1024], BF16)
    X1, X2, GS, ZC = (K.hy_scr[n] for n in ("x1", "x2", "gs", "zc"))
    mk0 = P.mark()
    zT = P.sb("hzT", [128, 20, 1024], BF16)
    hcw = P.sb("hcw", [128, 3, 24])
    for k in range(3):
        P.dma(hcw[:, k, :], I.hy_conv_w[l, k].rearrange("(ft f) -> f ft", f=128))
    hcb = P.sb("hcb", [128, 24]); P.dma(hcb[:], I.hy_conv_b[l].rearrange("(ft f) -> f ft", f=128))
    hbc = P.sb("hbc", [128, 2, 8])
    for o in range(2):
        P.dma(hbc[:, o, :], I.hy_bias[l, o].rearrange("(ct c) -> c ct", c=128))
    mk = P.mark()
    raw = P.sb("hraw", [128, T]); cv = P.sb("hcv", [128, T])
    for cg in range(8):
        for s in range(4):
            c0 = s * 1024 + cg * 128
            w = load_w(K, I.w_in[l][:, c0:c0 + 128], 128)
            for tt in range(NTT):
                ps = K.ps[(s * NTT + tt) % 8][:]
                proj(K, ps, w, 0, 128, tt)
                if s < 3:
                    P.copy(raw[:, tt * 512:(tt + 1) * 512], ps, e="act")
                else:
                    P.act(raw[:, tt * 512:(tt + 1) * 512], ps, AF.Silu)
            if s == 3:
                P.dma(GS[cg * 128:(cg + 1) * 128, :], raw[:])
                continue
            ft = s * 8 + cg
            P.ts(cv[:], raw[:], hcw[:, 1, ft:ft + 1], hcb[:, ft:ft + 1], ALU.mult, ALU.add)
            for (s0, L) in SEQS:
                P.stt(cv[:, s0 + 1:s0 + L], raw[:, s0:s0 + L - 1], hcw[:, 0, ft:ft + 1], cv[:, s0 + 1:s0 + L], ALU.mult, ALU.add)
                P.stt(cv[:, s0:s0 + L - 1], raw[:, s0 + 1:s0 + L], hcw[:, 2, ft:ft + 1], cv[:, s0:s0 + L - 1], ALU.mult, ALU.add)
            if s == 0:
                P.dma(ZC[cg * 128:(cg + 1) * 128, :], cv[:])
                for g5 in range(5):
                    ps = K.ps[g5 % 8]
                    for q in range(4):
                        tb = g5 * 4 + q
                        P.transpose(ps[:, q * 128:(q + 1) * 128], cv[:, tb * 128:(tb + 1) * 128], K.ident[:])
                    P.copy(zT[:, g5 * 4:(g5 + 1) * 4, cg * 128:(cg + 1) * 128], ps[:].rearrange("p (q t) -> p q t", q=4), e="act")
            else:
                P.dma((X1 if s == 1 else X2)[cg * 128:(cg + 1) * 128, :], cv[:])
    P.release(mk)
    for o in range(2):
        for (s0, L) in SEQS:
            hyena_conv(K, l, o, s0, L, zT, hbc)
    P.release(mk0)


def hyena_conv(K, l, o, s0, L, zT, hbc):
    P, I = K.P, K.I
    X1, X2, GS, ZC = (K.hy_scr[n] for n in ("x1", "x2", "gs", "zc"))
    CHs = K.hy_ch
    N = 2 * L
    NTB = L // 128
    NJ = N // 128
    NJ2 = NJ // 2
    tb0 = s0 // 128
    KH = K.KH[(l, L)]
    Dv = I.dft[L].rearrange("(tb t) j -> t tb j", t=128)
    mk = P.mark()
    Dre = [P.sb("hDre%d" % i, [128, NTB, 128], BF16) for i in range(2)]
    Dim = [P.sb("hDim%d" % i, [128, NTB, 128], BF16) for i in range(2)]
    khr = [P.sb("hkhr%d" % i, [128, 1024]) for i in range(2)]
    khi = [P.sb("hkhi%d" % i, [128, 1024]) for i in range(2)]
    tmp = [P.sb("htmp%d" % i, [128, 512]) for i in range(4)]
    crb = [P.sb("hcrb%d" % i, [128, 1024], BF16) for i in range(2)]
    cib = [P.sb("hcib%d" % i, [128, 1024], BF16) for i in range(2)]
    for jp in range(NJ2):
        b = jp % 2
        P.dma(Dre[b][:], Dv[:, :, jp * 128:(jp + 1) * 128])
        P.dma(Dim[b][:], Dv[:, :, (jp + NJ2) * 128:(jp + NJ2 + 1) * 128])
        P.dma(khr[b][:], KH[jp * 128:(jp + 1) * 128, o * 1024:(o + 1) * 1024])
        P.dma(khi[b][:], KH[(jp + NJ2) * 128:(jp + NJ2 + 1) * 128, o * 1024:(o + 1) * 1024])
        for hf in range(2):
            psr = K.ps[b * 4 + hf * 2][:]; psi = K.ps[b * 4 + hf * 2 + 1][:]
            for tb in range(NTB):
                P.mm(psr, Dre[b][:, tb, :], zT[:, tb0 + tb, hf * 512:(hf + 1) * 512], start=(tb == 0), stop=(tb == NTB - 1))
            for tb in range(NTB):
                P.mm(psi, Dim[b][:, tb, :], zT[:, tb0 + tb, hf * 512:(hf + 1) * 512], start=(tb == 0), stop=(tb == NTB - 1))
            sl = slice(hf * 512, (hf + 1) * 512)
            P.tt(tmp[0][:], psr, khr[b][:, sl], ALU.mult)
            P.tt(tmp[1][:], psi, khi[b][:, sl], ALU.mult)
            P.tt(tmp[2][:], psr, khi[b][:, sl], ALU.mult)
            P.tt(tmp[3][:], psi, khr[b][:, sl], ALU.mult)
            P.tt(crb[b][:, sl], tmp[0][:], tmp[1][:], ALU.subtract, e="pool")
            P.tt(cib[b][:, sl], tmp[2][:], tmp[3][:], ALU.add, e="pool")
            if jp == 0:
                P.tt(crb[b][0:1, sl], psr[0:1, :], khr[b][0:1, sl], ALU.mult)
                P.tt(cib[b][0:1, sl], psi[0:1, :], khi[b][0:1, sl], ALU.mult)
        P.dma(CHs[jp * 128:(jp + 1) * 128, :], crb[b][:])
        P.dma(CHs[(jp + NJ2) * 128:(jp + NJ2 + 1) * 128, :], cib[b][:])
    P.release(mk)
    mk = P.mark()
    NTK = min(512, L)
    PJ = min(4, NJ)
    CHp = [P.sb("hCHp%d" % i, [128, PJ, 1024], BF16) for i in range(2)]
    DTp = [P.sb("hDTp%d" % i, [128, PJ, NTK], BF16) for i in range(2)]
    zt = [P.sb("hzt%d" % i, [128, NTK]) for i in range(2)]
    xt = [P.sb("hxt%d" % i, [128, NTK]) for i in range(2)]
    gt = [P.sb("hgt%d" % i, [128, NTK]) for i in range(2)]
    zn = [P.sb("hzn%d" % i, [128, NTK]) for i in range(2)]
    XS = X1 if o == 0 else X2
    it = 0
    for tq in range(L // NTK):
        t0 = s0 + tq * NTK
        npiece = NJ // PJ
        for pc in range(npiece):
            b = it % 2
            it += 1
            P.dma(CHp[b][:], CHs[pc * PJ * 128:(pc + 1) * PJ * 128, :].rearrange("(jl p) c -> p jl c", p=128))
            P.dma(DTp[b][:], I.dftT[L][pc * PJ * 128:(pc + 1) * PJ * 128, tq * NTK:(tq + 1) * NTK].rearrange("(jl p) t -> p jl t", p=128))
            for ct in range(8):
                for jl in range(PJ):
                    P.mm(K.ps[ct][:, 0:NTK], CHp[b][:, jl, ct * 128:(ct + 1) * 128], DTp[b][:, jl, :],
                         start=(pc == 0 and jl == 0), stop=(pc == npiece - 1 and jl == PJ - 1))
        for ct in range(8):
            b = ct % 2
            rows = slice(ct * 128, (ct + 1) * 128)
            P.dma(zt[b][:], ZC[rows, t0:t0 + NTK])
            P.dma(xt[b][:], XS[rows, t0:t0 + NTK])
            P.stt(zn[b][:], zt[b][:], hbc[:, o, ct:ct + 1], K.ps[ct][:, 0:NTK], ALU.mult, ALU.add)
            if o == 0:
                P.tt(zn[b][:], zn[b][:], xt[b][:], ALU.mult)
                P.dma(ZC[rows, t0:t0 + NTK], zn[b][:])
                nq = NTK // 128
                for q in range(nq):
                    P.transpose(K.ps[ct][:, q * 128:(q + 1) * 128], zn[b][:, q * 128:(q + 1) * 128], K.ident[:])
                tbs = t0 // 128
                P.copy(zT[:, tbs:tbs + nq, ct * 128:(ct + 1) * 128],
                       K.ps[ct][:, 0:NTK].rearrange("p (q t) -> p q t", q=nq), e="act")
            else:
                P.dma(gt[b][:], GS[rows, t0:t0 + NTK])
                P.tt(zn[b][:], zn[b][:], xt[b][:], ALU.mult)
                P.tt(K.yb[:, ct, t0:t0 + NTK], zn[b][:], gt[b][:], ALU.mult, e="pool")
    P.release(mk)


def s5_host_consts():
    c = {}
    c["s5_bm"] = np.kron(np.eye(8), np.ones((16, 16))).astype(np.float32)
    c["s5_bmg"] = np.kron(np.eye(8), np.ones((16, 1))).astype(np.float32)
    m2 = np.zeros((2, 64, 4, 8), np.float32)
    for g2 in range(2):
        for gp in range(4):
            m2[g2, :, gp, 2 * gp + g2] = 1.0
    c["s5_mask2"] = m2.reshape(128, 32)
    return c


def bc(ap, shape):
    return ap.to_broadcast(list(shape))


def cmul(P, outr, outi, ar, ai, br, bi, t1, t2, e2="pool"):
    P.tt(t1, ar, br, ALU.mult)
    P.tt(t2, ai, bi, ALU.mult)
    P.tt(outr, t1, t2, ALU.subtract)
    P.tt(t1, ar, bi, ALU.mult)
    P.tt(t2, ai, br, ALU.mult)
    P.tt(outi, t1, t2, ALU.add)


def s5_lambar(K, l, R, C, lam_src, step_fill, NK, tag):
    P, I = K.P, K.I
    lr = P.sb(tag + "lr", [R, C]); li = P.sb(tag + "li", [R, C]); st = P.sb(tag + "st", [R, C])
    P.dma(lr[:], lam_src(I.s5_lambda_re[l])); P.dma(li[:], lam_src(I.s5_lambda_im[l]))
    step_fill(st)
    P.act(st[:], st[:], AF.Exp)
    mk = P.mark()
    ar = P.sb(tag + "ar", [R, C]); ang = P.sb(tag + "ang", [R, C]); mag = P.sb(tag + "mag", [R, C])
    s = P.sb(tag + "s", [R, C]); c = P.sb(tag + "c", [R, C]); ti = P.sb(tag + "ti", [R, C], I32); tf = P.sb(tag + "tf", [R, C])
    P.tt(ar[:], lr[:], st[:], ALU.mult)
    P.act(mag[:], ar[:], AF.Exp)
    P.tt(ang[:], li[:], st[:], ALU.mult)
    sin_rr(K, s[:], ang[:], ti[:], tf[:])
    P.ts(ang[:], ang[:], math.pi / 2.0, None, ALU.add)
    sin_rr(K, c[:], ang[:], ti[:], tf[:])
    pwr = P.sb(tag + "pwr", [R, NK, C]); pwi = P.sb(tag + "pwi", [R, NK, C])
    return lr, li, st, mag, s, c, pwr, pwi, mk


def s5_prep(K, l):
    P, I = K.P, K.I
    S = Ctx()
    if not hasattr(K, "s5c"):
        pass
    S.bm = P.sb("s5bm", [128, 128]); P.dma(S.bm[:], I.s5_bm)
    S.bmg = P.sb("s5bmg", [128, 8]); P.dma(S.bmg[:], I.s5_bmg)
    S.mask2 = P.sb("s5m2", [128, 4, 8]); P.dma(S.mask2[:], I.s5_mask2.rearrange("p (a b) -> p a b", a=4))
    def fillA(st):
        P.dma(st[:], I.s5_log_step[l].rearrange("d g -> (d g)").partition_broadcast(64))
    S.pAr = P.sb("s5pAr", [64, 9, 128]); S.pAi = P.sb("s5pAi", [64, 9, 128])
    S.qr = P.sb("s5qr", [64, 128]); S.qi = P.sb("s5qi", [64, 128])
    mk = P.mark()
    lr, li, st, mag, s, c, _, _, _ = s5_lambar(K, l, 64, 128, lambda a: a.rearrange("d g p -> p (d g)"), fillA, 1, "A")
    t1 = P.sb("At1", [64, 128]); t2 = P.sb("At2", [64, 128])
    P.memset(S.pAr[:, 0, :], 1.0); P.memset(S.pAi[:, 0, :], 0.0)
    P.tt(S.pAr[:, 1, :], mag[:], c[:], ALU.mult); P.tt(S.pAi[:, 1, :], mag[:], s[:], ALU.mult)
    for k in range(1, 8):
        cmul(P, S.pAr[:, k + 1, :], S.pAi[:, k + 1, :], S.pAr[:, k, :], S.pAi[:, k, :], S.pAr[:, 1, :], S.pAi[:, 1, :], t1[:], t2[:])
    dd = P.sb("Add", [64, 128]); nr = P.sb("Anr", [64, 128])
    P.tt(dd[:], lr[:], lr[:], ALU.mult); P.tt(t1[:], li[:], li[:], ALU.mult); P.tt(dd[:], dd[:], t1[:], ALU.add)
    P.op("dve", lambda g: g.reciprocal(dd[:], dd[:]), [dd[:]], [dd[:]])
    P.ts(nr[:], S.pAr[:, 1, :], -1.0, None, ALU.add)
    P.tt(t1[:], nr[:], lr[:], ALU.mult); P.tt(t2[:], S.pAi[:, 1, :], li[:], ALU.mult); P.tt(t1[:], t1[:], t2[:], ALU.add)
    P.tt(S.qr[:], t1[:], dd[:], ALU.mult)
    P.tt(t1[:], S.pAi[:, 1, :], lr[:], ALU.mult); P.tt(t2[:], nr[:], li[:], ALU.mult); P.tt(t1[:], t1[:], t2[:], ALU.subtract)
    P.tt(S.qi[:], t1[:], dd[:], ALU.mult)
    P.release(mk)
    def fillB(st):
        for g2 in range(2):
            P.dma(st[g2 * 64:(g2 + 1) * 64, :], I.s5_log_step[l].rearrange("d (gp g2) -> g2 (d gp)", g2=2)[g2].partition_broadcast(64))
    S.pBr = P.sb("s5pBr", [128, 9, 64]); S.pBi = P.sb("s5pBi", [128, 9, 64])
    S.apr = P.sb("s5apr", [128, 17, 64]); S.api = P.sb("s5api", [128, 17, 64])
    mk = P.mark()
    lr, li, st, mag, s, c, _, _, _ = s5_lambar(K, l, 128, 64, lambda a: a.rearrange("d (gp g2) p -> (g2 p) (d gp)", g2=2), fillB, 1, "B")
    t1 = P.sb("Bt1", [128, 64]); t2 = P.sb("Bt2", [128, 64])
    P.memset(S.pBr[:, 0, :], 1.0); P.memset(S.pBi[:, 0, :], 0.0)
    P.tt(S.pBr[:, 1, :], mag[:], c[:], ALU.mult); P.tt(S.pBi[:, 1, :], mag[:], s[:], ALU.mult)
    for k in range(1, 8):
        cmul(P, S.pBr[:, k + 1, :], S.pBi[:, k + 1, :], S.pBr[:, k, :], S.pBi[:, k, :], S.pBr[:, 1, :], S.pBi[:, 1, :], t1[:], t2[:])
    P.memset(S.apr[:, 0, :], 1.0); P.memset(S.api[:, 0, :], 0.0)
    P.copy(S.apr[:, 1, :], S.pBr[:, 8, :]); P.copy(S.api[:, 1, :], S.pBi[:, 8, :])
    for j in range(1, 16):
        cmul(P, S.apr[:, j + 1, :], S.api[:, j + 1, :], S.apr[:, j, :], S.api[:, j, :], S.apr[:, 1, :], S.api[:, 1, :], t1[:], t2[:])
    P.release(mk)
    return S


class StopS5(Exception):
    pass


import os as _os
S5_CUT = float(_os.environ.get("S5_CUT", "99"))


def s5cut(stage):
    if S5_CUT <= stage:
        raise StopS5()


def mixer_s5(K, l):
    P, I = K.P, K.I
    if not hasattr(K, "s5G"):
        K.s5G = K.dscr("s5_G", [1024, T], BF16)
    mk0 = P.mark()
    try:
        S = s5_prep(K, l)
        s5cut(0)
        for gt in range(8):
            s5_gt(K, l, gt, S)
            s5cut(8)
        s5_glu(K, l)
    except StopS5:
        pass
    P.release(mk0)


def s5_gt(K, l, gt, S):
    P, I, O = K.P, K.I, K.O
    mk = P.mark()
    NCH = T // 8
    ub = P.sb("s5ub", [128, T], BF16); yacc = P.sb("s5y", [128, T])
    dsk = P.sb("s5dsk", [128, 1]); P.dma(dsk[:], I.s5_D[l, gt * 128:(gt + 1) * 128].rearrange("(c o) -> c o", o=1))
    m = [P.sb("s5m%d" % i, [128, 4 * NCH]) for i in range(2)]
    s5cut(0.2)
    w = load_w(K, I.w_in[l][:, C_S5 + gt * 128: C_S5 + (gt + 1) * 128], 128)
    s5cut(0.4)
    for tt in range(NTT):
        ps = K.ps[tt % 8][:]
        proj(K, ps, w, 0, 128, tt)
        s5cut(0.6)
        uf = m[tt % 2][:, 0:512]
        P.copy(uf, ps, e="act")
        s5cut(0.8)
        P.copy(ub[:, tt * 512:(tt + 1) * 512], uf, e="pool")
        P.ts(yacc[:, tt * 512:(tt + 1) * 512], uf, dsk[:, 0:1], None, ALU.mult)
    s5cut(1)
    ubv = ub[:].rearrange("p (n s) -> p n s", s=8)
    ubr = P.sb("s5ubr", [128, T], BF16)
    ubrv = ubr[:].rearrange("p (n s) -> p n s", s=8)
    P.copy(ubrv, ubv[:, ::-1, :], e="pool")
    yv = yacc[:].rearrange("p (n s) -> p n s", s=8)
    Bre = P.sb("s5Bre", [64, 8, 16]); Bim = P.sb("s5Bim", [64, 8, 16])
    CAr = P.sb("s5CAr", [64, 8, 16]); CAi = P.sb("s5CAi", [64, 8, 16])
    bbr = P.sb("s5bbr", [64, 8, 16]); bbi = P.sb("s5bbi", [64, 8, 16])
    tA1 = P.sb("s5tA1", [64, 8, 16]); tA2 = P.sb("s5tA2", [64, 8, 16])
    Wr = P.sb("s5Wr", [64, 8, 128]); Wi = P.sb("s5Wi", [64, 8, 128])
    Kb = P.sb("s5Kb", [128, 8, 128], BF16)
    Pm = P.sb("s5Pm", [128, 8, 8, 64], BF16)
    CBr = P.sb("s5CBr", [128, 4, 16]); CBi = P.sb("s5CBi", [128, 4, 16])
    CLr = P.sb("s5CLr", [128, 9, 4, 16]); CLi = P.sb("s5CLi", [128, 9, 4, 16])
    tB1 = P.sb("s5tB1", [128, 4, 16]); tB2 = P.sb("s5tB2", [128, 4, 16])
    CLm = [[P.sb("s5CLm%d%d" % (i, r), [128, 4, 128], BF16) for r in range(2)] for i in range(2)]
    Dst = [P.sb("s5D%d" % r, [128, 4, NCH]) for r in range(2)]
    big = P.sb("s5big", [128, 2720])
    Pl = [big[:, r * 1360:(r + 1) * 1360].rearrange("p (c s j) -> p c s j", c=4, j=17) for r in range(2)]
    S2 = [P.sb("s5S2%d" % r, [128, 4, 20]) for r in range(2)]
    fin = [P.sb("s5fin%d" % r, [128, 4]) for r in range(2)]
    Sb = [P.sb("s5Sb%d" % r, [128, 4, NCH], BF16) for r in range(2)]
    for d in range(2):
        cA = slice(d * 64 + gt * 8, d * 64 + gt * 8 + 8)
        cB = slice(d * 32 + gt * 4, d * 32 + gt * 4 + 4)
        gsl = slice(gt * 8, gt * 8 + 8)
        P.dma(Bre[:], I.s5_B_re[l, d, gsl].rearrange("g p h -> p g h"))
        P.dma(Bim[:], I.s5_B_im[l, d, gsl].rearrange("g p h -> p g h"))
        P.dma(CAr[:], I.s5_C_re[l, d, gsl].rearrange("g h p -> p g h"))
        P.dma(CAi[:], I.s5_C_im[l, d, gsl].rearrange("g h p -> p g h"))
        for g2 in range(2):
            for gp in range(4):
                P.dma(CBr[g2 * 64:(g2 + 1) * 64, gp, :], I.s5_C_re[l, d, gt * 8 + 2 * gp + g2].rearrange("h p -> p h"))
                P.dma(CBi[g2 * 64:(g2 + 1) * 64, gp, :], I.s5_C_im[l, d, gt * 8 + 2 * gp + g2].rearrange("h p -> p h"))
        qrb = bc(S.qr[:, cA].unsqueeze(2), [64, 8, 16]); qib = bc(S.qi[:, cA].unsqueeze(2), [64, 8, 16])
        cmul(P, bbr[:], bbi[:], qrb, qib, Bre[:], Bim[:], tA1[:], tA2[:])
        for k in range(8):
            pr = bc(S.pAr[:, k, cA].unsqueeze(2), [64, 8, 16]); pi_ = bc(S.pAi[:, k, cA].unsqueeze(2), [64, 8, 16])
            cmul(P, Wr[:, k, :].rearrange("p (g h) -> p g h", h=16), Wi[:, k, :].rearrange("p (g h) -> p g h", h=16),
                 pr, pi_, bbr[:], bbi[:], tA1[:], tA2[:])
        s5cut(2)
        P.ts(CAi[:], CAi[:], -1.0, None, ALU.mult)
        car = CAr[:].rearrange("p g h -> p (g h)"); cai = CAi[:].rearrange("p g h -> p (g h)")
        for dl in range(8):
            ps = K.ps[dl % 8][:, 0:128]
            P.mm(ps, Wr[:, dl, :], car, start=True, stop=False)
            P.mm(ps, Wi[:, dl, :], cai, start=False, stop=True)
            P.tt(Kb[:, dl, :], ps, S.bm[:], ALU.mult)
        s5cut(3)
        it = 0
        for ri, Wx in enumerate((Wr, Wi)):
            for sg in range(8):
                k = 7 - sg if d == 0 else sg
                ps = K.ps[(sg + ri) % 8][:, 0:64]
                P.transpose(ps, Wx[:, k, :], K.ident[0:64, 0:64])
                P.tt(Pm[:, sg, :, :], bc(ps.unsqueeze(1), [128, 8, 64]), bc(S.bmg[:].unsqueeze(2), [128, 8, 64]), ALU.mult)
            for gp in range(4):
                ps = K.ps[it % 8][:, 0:NCH]
                it += 1
                for sg in range(8):
                    P.mm(ps, Pm[:, sg, 2 * gp:2 * gp + 2, :].rearrange("p g q -> p (g q)"), (ubv if d == 0 else ubrv)[:, :, sg],
                         start=(sg == 0), stop=(sg == 7))
                P.copy(Dst[ri][:, gp, :], ps, e=("act" if it % 2 else "dve"))
        s5cut(4)
        s5_scan(K, l, gt, d, S, Dst, Pl, S2, fin, m, cB)
        s5cut(5)
        P.copy(Sb[0][:], Dst[0][:])
        P.ts(Sb[1][:], Dst[1][:], -1.0, None, ALU.mult)
        for k in range(1, 9):
            pr = bc(S.pBr[:, k, cB].unsqueeze(2), [128, 4, 16]); pi_ = bc(S.pBi[:, k, cB].unsqueeze(2), [128, 4, 16])
            cmul(P, CLr[:, k], CLi[:, k], pr, pi_, CBr[:], CBi[:], tB1[:], tB2[:])
        for tau in range(8):
            k = tau + 1 if d == 0 else 8 - tau
            cl = CLm[tau % 2]
            for ri, CL in enumerate((CLr, CLi)):
                P.tt(cl[ri][:].rearrange("p a (b h) -> p a b h", h=16), bc(CL[:, k].unsqueeze(2), [128, 4, 8, 16]),
                     bc(S.mask2[:].unsqueeze(3), [128, 4, 8, 16]), ALU.mult)
            ps = K.ps[tau % 8][:, 0:NCH]
            uu = ubv if d == 0 else ubrv
            nd = (tau + 1) if d == 0 else (8 - tau)
            for dl in range(nd):
                P.mm(ps, Kb[:, dl, :], uu[:, :, (tau - dl) if d == 0 else (tau + dl)], start=(dl == 0), stop=False)
            i2 = 0
            for gp in range(4):
                for ri in range(2):
                    P.mm(ps, cl[ri][:, gp, :], Sb[ri][:, gp, :], start=False, stop=(i2 == 7))
                    i2 += 1
            src = ps if d == 0 else ps[:, ::-1]
            P.tt(yv[:, :, tau], yv[:, :, tau], src, ALU.add)
        s5cut(6)
    t = big[:, 0:T]; gb = ub
    P.tt(t, yacc[:], yacc[:], ALU.mult)
    P.ts(t, t, 0.044715, 1.0, ALU.mult, ALU.add)
    P.tt(t, t, yacc[:], ALU.mult)
    P.act(t, t, AF.Sigmoid, scale=2.0 * math.sqrt(2.0 / math.pi))
    P.tt(gb[:], t, yacc[:], ALU.mult)
    if K.dbg and K.stop == ("s5", l):
        P.dma(K.dbg32[gt * 128:(gt + 1) * 128, :], yacc[:], is_output=True)
    P.dma(K.s5G[gt * 128:(gt + 1) * 128, :], gb[:])
    P.release(mk)
    s5cut(7)


def s5_scan(K, l, gt, d, S, Dst, Pl, S2, fin, m, cB):
    P, I, O = K.P, K.I, K.O
    Ar = bc(S.apr[:, 1, cB].unsqueeze(2), [128, 4, 20]); Ai = bc(S.api[:, 1, cB].unsqueeze(2), [128, 4, 20])
    Dv = [Dst[r][:].rearrange("p c (s j) -> p c s j", j=16) for r in range(2)]
    m1 = m[0][:, 0:80].rearrange("p (c s) -> p c s", c=4); m2 = m[1][:, 0:80].rearrange("p (c s) -> p c s", c=4)
    for r in range(2):
        P.memset(Pl[r][:, :, :, 0], 0.0)
        P.copy(Pl[r][:, :, :, 1], Dv[r][:, :, :, 0])
    for j in range(1, 16):
        pr, pi_ = Pl[0][:, :, :, j], Pl[1][:, :, :, j]
        P.tt(m1, pr, Ar, ALU.mult); P.tt(m2, pi_, Ai, ALU.mult)
        P.tt(m1, m1, m2, ALU.subtract)
        P.tt(Pl[0][:, :, :, j + 1], m1, Dv[0][:, :, :, j], ALU.add)
        P.tt(m1, pr, Ai, ALU.mult); P.tt(m2, pi_, Ar, ALU.mult)
        P.tt(m1, m1, m2, ALU.add)
        P.tt(Pl[1][:, :, :, j + 1], m1, Dv[1][:, :, :, j], ALU.add)
    A16r, A16i = S.apr[:, 16, cB], S.api[:, 16, cB]
    seqs = [("s", 0, 16), ("pA", 16, 2), ("pB", 18, 2)] if d == 0 else [("pB", 0, 2), ("pA", 2, 2), ("s", 4, 16)]
    t1 = m[0][:, 100:104]; t2 = m[1][:, 100:104]

    def step(outr, outi, sr, si, dr, di):
        P.tt(t1, sr, A16r, ALU.mult); P.tt(t2, si, A16i, ALU.mult)
        P.tt(t1, t1, t2, ALU.subtract)
        P.tt(outr, t1, dr, ALU.add)
        P.tt(t1, sr, A16i, ALU.mult); P.tt(t2, si, A16r, ALU.mult)
        P.tt(t1, t1, t2, ALU.add)
        P.tt(outi, t1, di, ALU.add)

    for (nm, a, cnt) in seqs:
        if nm == "s":
            for r in range(2):
                for g2 in range(2):
                    P.dma(S2[r][g2 * 64:(g2 + 1) * 64, :, a],
                          I.ss[l, d, gt * 8 + g2:gt * 8 + 8:2, :, r].rearrange("gp p -> p gp"))
            for k in range(cnt - 1):
                step(S2[0][:, :, a + k + 1], S2[1][:, :, a + k + 1], S2[0][:, :, a + k], S2[1][:, :, a + k],
                     Pl[0][:, :, a + k, 16], Pl[1][:, :, a + k, 16])
        else:
            for r in range(2):
                P.memset(S2[r][:, :, a], 0.0)
                P.copy(S2[r][:, :, a + 1], Pl[r][:, :, a, 16])
            step(fin[0][:], fin[1][:], S2[0][:, :, a + 1], S2[1][:, :, a + 1], Pl[0][:, :, a + 1, 16], Pl[1][:, :, a + 1, 16])
            pr_i = 0 if nm == "pA" else 1
            for r in range(2):
                for g2 in range(2):
                    P.dma(O.ns[pr_i, l, d, gt * 8 + g2:gt * 8 + 8:2, :, r].rearrange("gp p -> p gp"),
                          fin[r][g2 * 64:(g2 + 1) * 64, :], is_output=True)
    apr = bc(S.apr[:, 0:16, cB].rearrange("p j c -> p c j").unsqueeze(2), [128, 4, 20, 16])
    api = bc(S.api[:, 0:16, cB].rearrange("p j c -> p c j").unsqueeze(2), [128, 4, 20, 16])
    s2r = bc(S2[0][:].unsqueeze(3), [128, 4, 20, 16]); s2i = bc(S2[1][:].unsqueeze(3), [128, 4, 20, 16])
    M1 = m[0][:].rearrange("p (c s j) -> p c s j", c=4, j=16); M2 = m[1][:].rearrange("p (c s j) -> p c s j", c=4, j=16)
    P.tt(M1, s2r, apr, ALU.mult); P.tt(M2, s2i, api, ALU.mult)
    P.tt(M1, M1, M2, ALU.subtract)
    P.tt(Dv[0], M1, Pl[0][:, :, :, 0:16], ALU.add)
    P.tt(M1, s2r, api, ALU.mult); P.tt(M2, s2i, apr, ALU.mult)
    P.tt(M1, M1, M2, ALU.add)
    P.tt(Dv[1], M1, Pl[1][:, :, :, 0:16], ALU.add)


def s5_glu(K, l):
    P, I = K.P, K.I
    mk = P.mark()
    gbT = P.sb("s5gbT", [128, 2, 8]);
    for hh in range(2):
        P.dma(gbT[:, hh, :], I.s5_glu_b[l, hh * 1024:(hh + 1) * 1024].rearrange("(ct c) -> c ct", c=128))
    Gt = [P.sb("s5Gt%d" % i, [128, 8, 512], BF16) for i in range(2)]
    Gv = K.s5G.rearrange("(kc k) t -> k kc t", k=128)
    wa = P.sb("s5wa", [128, 8, 1024], BF16); wg = P.sb("s5wg", [128, 8, 1024], BF16); wz = P.sb("s5wz", [128, 8, 1024], BF16)
    for c4 in range(8):
        for (dst, src) in ((wa, I.s5_glu_w[l][:, c4 * 128:(c4 + 1) * 128]), (wg, I.s5_glu_w[l][:, 1024 + c4 * 128:1024 + (c4 + 1) * 128]),
                           (wz, I.w_in[l][:, C_S5G + c4 * 128:C_S5G + (c4 + 1) * 128])):
            w = load_w(K, src, 128)
            P.copy(dst[:, :, c4 * 128:(c4 + 1) * 128], w, e="pool")
    sg = [P.sb("s5sg%d" % i, [128, 512]) for i in range(2)]
    sl = [P.sb("s5sl%d" % i, [128, 512]) for i in range(2)]
    tm = [P.sb("s5tm%d" % i, [128, 512]) for i in range(2)]
    it = 0
    for tt in range(NTT):
        G = Gt[tt % 2]
        P.dma(G[:], Gv[:, :, tt * 512:(tt + 1) * 512])
        for ct in range(8):
            b = it % 2
            pa = K.ps[(it * 3) % 8][:]; pg = K.ps[(it * 3 + 1) % 8][:]; pz = K.ps[(it * 3 + 2) % 8][:]
            it += 1
            for kc in range(8):
                P.mm(pa, wa[:, kc, ct * 128:(ct + 1) * 128], G[:, kc, :], start=(kc == 0), stop=(kc == 7))
            for kc in range(8):
                P.mm(pg, wg[:, kc, ct * 128:(ct + 1) * 128], G[:, kc, :], start=(kc == 0), stop=(kc == 7))
            for kc in range(8):
                P.mm(pz, wz[:, kc, ct * 128:(ct + 1) * 128], K.h[:, kc, tt * 512:(tt + 1) * 512], start=(kc == 0), stop=(kc == 7))
            P.act(sg[b][:], pg, AF.Sigmoid, bias=gbT[:, 1, ct:ct + 1])
            P.act(sl[b][:], pz, AF.Silu)
            P.stt(tm[b][:], pa, gbT[:, 0, ct:ct + 1], sg[b][:], ALU.add, ALU.mult)
            P.tt(K.yb[:, ct, tt * 512:(tt + 1) * 512], tm[b][:], sl[b][:], ALU.mult, e="pool")
    P.release(mk)


def gdn_host_consts():
    c = {}
    L = np.tril(np.ones((128, 128), np.float32))
    c["g_L"] = L; c["g_Ls"] = np.tril(np.ones((128, 128), np.float32), -1)
    c["g_U"] = np.ascontiguousarray(L.T); c["g_Us"] = np.triu(np.ones((128, 128), np.float32), 1)
    bd = np.kron(np.eye(4), np.ones((32, 32))).astype(np.float32)
    c["g_bd"] = bd; c["g_nbd"] = (1.0 - bd).astype(np.float32)
    return c


class StopGDN(Exception):
    pass


GDN_STOP = _os.environ.get("GDN_STOP", "")


def mixer_gdn(K, l):
    try:
        mixer_gdn_(K, l)
    except StopGDN:
        pass


def mixer_gdn_(K, l):
    P, I, O = K.P, K.I, K.O
    mk0 = P.mark()
    G = Ctx()
    G.M = {}
    for nm in ("g_L", "g_Ls", "g_U", "g_Us", "g_bd", "g_nbd"):
        G.M[nm] = P.sb(nm, [128, 128]); P.dma(G.M[nm][:], getattr(I, nm))
    G.gcw = P.sb("gcw", [128, 3, 24])
    for k in range(3):
        P.dma(G.gcw[:, k, :], I.gdn_conv_w[l, k].rearrange("(ft f) -> f ft", f=128))
    G.ng = P.sb("gng", [128, 128]); P.dma(G.ng[:], I.gdn_norm_g[l].partition_broadcast(128))
    alog = P.sb("galog", [128, 16]); P.dma(alog[:], I.gdn_A_log[l].rearrange("d h -> (d h)").partition_broadcast(128))
    dtb = P.sb("gdtb", [128, 16]); P.dma(dtb[:], I.gdn_dt_bias[l].rearrange("d h -> (d h)").partition_broadcast(128))
    P.act(alog[:], alog[:], AF.Exp)
    P.ts(alog[:], alog[:], -1.0, None, ALU.mult)
    G.bt = P.sb("gbt", [128, 20, 17]); G.gt = P.sb("ggt", [128, 20, 17]); G.nbt = P.sb("gnbt", [128, 20, 17])
    P.memset(G.gt[:], 0.0); P.memset(G.bt[:], 0.0)
    w32 = load_w(K, I.w_in[l][:, C_BETA:C_BETA + 32], 32)
    mkb = P.mark()
    bgraw = P.sb("gbgraw", [128, 20, 32])
    for c in range(20):
        ps = K.ps[c % 8][:, 0:32]
        for kc in range(8):
            P.mm(ps, K.h[:, kc, c * 128:(c + 1) * 128], w32[:, kc, :], start=(kc == 0), stop=(kc == 7))
        P.copy(bgraw[:, c, :], ps, e="act")
        P.act(G.bt[:, c, 0:16], bgraw[:, c, 0:16], AF.Sigmoid)
        P.tt(G.gt[:, c, 0:16], bgraw[:, c, 16:32], dtb[:], ALU.add)
    P.act(G.gt[:], G.gt[:], AF.Exp)
    P.act(G.gt[:], G.gt[:], AF.Ln, bias=K.one1[:, 0:1])
    for c in range(20):
        P.tt(G.gt[:, c, 0:16], G.gt[:, c, 0:16], alog[:], ALU.mult)
    P.ts(G.nbt[:], G.bt[:], -1.0, None, ALU.mult)
    P.release(mkb)
    for hd in range(8):
        gdn_head(K, l, hd, G)
    P.release(mk0)


def gdn_head(K, l, hd, G):
    P, I, O = K.P, K.I, K.O
    mk = P.mark()
    qT = P.sb("gqT", [128, T]); kT = P.sb("gkT", [128, T])
    ktok = P.sb("gktok", [128, 20, 128]); vtok = P.sb("gvtok", [128, 20, 128])
    oacc = P.sb("goacc", [128, 20, 128])
    mk1 = P.mark()
    vT = P.sb("gvT", [128, T]); raw = P.sb("graw", [128, T])
    rs = P.sb("grs", [128, 512])
    for s, dst in enumerate((qT, kT, vT)):
        c0 = C_GDN + s * 1024 + hd * 128
        w = load_w(K, I.w_in[l][:, c0:c0 + 128], 128)
        for tt in range(NTT):
            ps = K.ps[tt % 8][:]
            proj(K, ps, w, 0, 128, tt)
            P.copy(raw[:, tt * 512:(tt + 1) * 512], ps, e="act")
        ft = s * 8 + hd
        P.ts(dst[:], raw[:], G.gcw[:, 1, ft:ft + 1], None, ALU.mult)
        for (s0, L) in SEQS:
            P.stt(dst[:, s0 + 1:s0 + L], raw[:, s0:s0 + L - 1], G.gcw[:, 0, ft:ft + 1], dst[:, s0 + 1:s0 + L], ALU.mult, ALU.add)
            P.stt(dst[:, s0:s0 + L - 1], raw[:, s0 + 1:s0 + L], G.gcw[:, 2, ft:ft + 1], dst[:, s0:s0 + L - 1], ALU.mult, ALU.add)
        P.act(dst[:], dst[:], AF.Silu)
        if s < 2:
            P.tt(raw[:], dst[:], dst[:], ALU.mult)
            for tt in range(NTT):
                ps = K.ps[(tt + 5) % 8][:]
                P.mm(ps, K.ones[:], raw[:, tt * 512:(tt + 1) * 512])
                P.act(rs[:], ps, AF.Sqrt, bias=K.eps[:, 0:1])
                P.op("dve", lambda g: g.reciprocal(rs[:], rs[:]), [rs[:]], [rs[:]])
                if s == 0:
                    P.stt(dst[:, tt * 512:(tt + 1) * 512], dst[:, tt * 512:(tt + 1) * 512], 128.0 ** -0.5, rs[:], ALU.mult, ALU.mult)
                else:
                    P.tt(dst[:, tt * 512:(tt + 1) * 512], dst[:, tt * 512:(tt + 1) * 512], rs[:], ALU.mult)
    for (src, dstk) in ((kT, ktok), (vT, vtok)):
        for g5 in range(5):
            ps = K.ps[g5 % 8]
            for q in range(4):
                c = g5 * 4 + q
                P.transpose(ps[:, q * 128:(q + 1) * 128], src[:, c * 128:(c + 1) * 128], K.ident[:])
            P.copy(dstk[:, g5 * 4:(g5 + 1) * 4, :], ps[:].rearrange("p (q t) -> p q t", q=4), e="act")
    P.release(mk1)
    for d in range(2):
        mk2 = P.mark()
        col = d * 8 + hd
        uall = P.sb("guall", [128, 20, 128]); wTall = P.sb("gwTall", [128, 20, 128])
        qkTall = P.sb("gqkTall", [128, 20, 128])
        egc = P.sb("gegc", [128, 20]); egl = P.sb("gegl", [128, 20]); e2all = P.sb("ge2all", [128, 20])
        NS = 4
        mks = P.mark()
        slots = []
        for s in range(NS):
            sl = Ctx()
            sl.Pm = [P.sb("gP%d_%d" % (s, i), [128, 128]) for i in range(2)]
            sl.PT = [P.sb("gPT%d_%d" % (s, i), [128, 128]) for i in range(2)]
            sl.TT = [P.sb("gTT%d_%d" % (s, i), [128, 128]) for i in range(2)]
            sl.big = P.sb("gbig%d" % s, [128, 512])
            sl.Rm = sl.big[:, 0:128]; sl.E = sl.big[:, 128:256]; sl.t1 = sl.big[:, 256:384]; sl.qk = sl.big[:, 384:512]
            sl.Y = sl.big[:, 0:256]; sl.Z = sl.big[:, 256:512]
            sl.X = P.sb("gX%d" % s, [128, 256]); sl.bv = sl.X[:, 0:128]; sl.kbg = sl.X[:, 128:256]
            sl.Bf = P.sb("gBf%d" % s, [128, 128]); sl.NAo = P.sb("gNAo%d" % s, [128, 128])
            sl.cols = P.sb("gcols%d" % s, [128, 8])
            sl.banks = [K.ps[(2 * s) % 8], K.ps[(2 * s + 1) % 8]]
            slots.append(sl)
        Minc_ij = G.M["g_L"] if d == 0 else G.M["g_U"]
        Mstr_ij = G.M["g_Ls"] if d == 0 else G.M["g_Us"]
        Mkj = G.M["g_U"] if d == 0 else G.M["g_L"]
        jl = 127 if d == 0 else 0

        def R(sl, i):
            return sl.banks[(i // 4) % 2][:, (i % 4) * 128:(i % 4 + 1) * 128]

        def st1(sl, c):
            ch = slice(c * 128, (c + 1) * 128)
            P.ts(sl.Rm, Mkj[:], G.gt[:, c, col:col + 1], None, ALU.mult)
            P.mm(R(sl, 0)[:, 0:2], Mkj[:], G.gt[:, c, col:col + 2])
            P.mm(R(sl, 1), K.ones[:], sl.Rm)
            P.mm(R(sl, 2), kT[:, ch], kT[:, ch])
            P.mm(R(sl, 3), qT[:, ch], kT[:, ch])

        def st2(sl, c):
            P.copy(sl.cols[:, 0:1], R(sl, 0)[:, 0:1])
            P.copy(sl.cols[:, 1:2], R(sl, 1)[:, jl:jl + 1])
            P.ts(sl.t1, R(sl, 1), sl.cols[:, 0:1], 0.0, ALU.subtract, ALU.max)
            P.act(sl.E, sl.t1, AF.Exp, scale=-1.0)
            P.act(egc[:, c:c + 1], sl.cols[:, 0:1], AF.Exp)
            P.act(egl[:, c:c + 1], sl.cols[:, 1:2], AF.Exp)
            P.act(e2all[:, c:c + 1], sl.cols[:, 0:1], AF.Exp, scale=-1.0, bias=sl.cols[:, 1:2])
            P.tt(sl.cols[:, 3:4], G.bt[:, c, col:col + 1], egc[:, c:c + 1], ALU.mult)

        def st3(sl, c):
            P.tt(sl.t1, R(sl, 2), sl.E, ALU.mult)
            P.stt(sl.Bf[:], sl.t1, G.nbt[:, c, col:col + 1], Mstr_ij[:], ALU.mult, ALU.mult)
            P.tt(sl.t1, R(sl, 3), sl.E, ALU.mult)
            P.tt(sl.qk, sl.t1, Minc_ij[:], ALU.mult, e="pool")
            P.ts(sl.kbg, ktok[:, c, :], sl.cols[:, 3:4], None, ALU.mult, e="pool")
            P.ts(sl.bv, vtok[:, c, :], G.bt[:, c, col:col + 1], None, ALU.mult, e="pool")

        def st4(sl, c):
            P.transpose(R(sl, 0), sl.Bf[:], K.ident[:])
            P.transpose(R(sl, 5), sl.qk, K.ident[:])
            P.tt(sl.Pm[0][:], sl.Bf[:], G.M["g_bd"][:], ALU.mult, e="pool")
            P.tt(sl.PT[0][:], R(sl, 0), G.M["g_bd"][:], ALU.mult)
            P.tt(sl.NAo[:], R(sl, 0), G.M["g_nbd"][:], ALU.mult)
            P.copy(qkTall[:, c, :], R(sl, 5), e="act")
            P.tt(sl.TT[0][:], sl.PT[0][:], K.ident[:], ALU.add, e="pool")

        def st5a(sl, c, a):
            b = 1 - a
            P.mm(R(sl, 6), sl.PT[a][:], sl.Pm[a][:])
            P.mm(R(sl, 7), sl.Pm[a][:], sl.PT[a][:])
            P.copy(sl.Pm[b][:], R(sl, 6), e="act")
            P.copy(sl.PT[b][:], R(sl, 7), e="act")

        def st5b(sl, c, a):
            b = 1 - a
            P.mm(R(sl, 0), sl.Pm[b][:], sl.TT[a][:])
            P.tt(sl.TT[b][:], R(sl, 0), sl.TT[a][:], ALU.add)

        def st6a(sl, c, a):
            P.mm(sl.banks[0][:, 0:256], sl.TT[a][:], sl.X[:])
            P.copy(sl.Y, sl.banks[0][:, 0:256])

        def st6b(sl, c, a):
            P.mm(sl.banks[0][:, 256:512], sl.NAo[:], sl.Y)
            P.tt(sl.Z, sl.X[:], sl.banks[0][:, 256:512], ALU.add)
            P.mm(sl.banks[0][:, 0:256], sl.TT[a][:], sl.Z)
            P.copy(sl.Y, sl.banks[0][:, 0:256])

        def st6c(sl, c, a):
            P.copy(uall[:, c, :], sl.Y[:, 0:128], e="pool")
            P.transpose(R(sl, 6), sl.Y[:, 128:256], K.ident[:])
            P.copy(wTall[:, c, :], R(sl, 6), e="act")

        for c0 in range(0, min(20, int(_os.environ.get("GDN_NB", "99")) * NS), NS):
            batch = [(slots[i], c0 + i) for i in range(NS) if c0 + i < 20]
            for st in (st1, st2, st3, st4):
                for (sl, c) in batch:
                    st(sl, c)
            a = 0
            for lev in range(4):
                for (sl, c) in batch:
                    st5a(sl, c, a)
                for (sl, c) in batch:
                    st5b(sl, c, a)
                a = 1 - a
            for (sl, c) in batch:
                st6a(sl, c, a)
            for it in range(3):
                for (sl, c) in batch:
                    st6b(sl, c, a)
            for (sl, c) in batch:
                st6c(sl, c, a)
        if GDN_STOP == "%d,%d,g2" % (hd, d):
            raise StopGDN()
        P.release(mks)
        chains = [("s", list(range(0, 16))), ("pA", [16, 17]), ("pB", [18, 19])]
        if d == 1:
            chains = [(nm, cs[::-1]) for nm, cs in chains]
        Sst = {nm: [P.sb("gS%s%d" % (nm, i), [128, 128]) for i in range(2)] for nm, _ in chains}
        vnew = [P.sb("gvnew%d" % i, [128, 128]) for i in range(3)]
        tq = [P.sb("gtq%d" % i, [128, 128]) for i in range(3)]
        kg2 = [P.sb("gkg2%d" % i, [128, 128]) for i in range(3)]
        P.dma(Sst["s"][0][:], I.sg[l, d, hd])
        P.memset(Sst["pA"][0][:], 0.0); P.memset(Sst["pB"][0][:], 0.0)
        cur = {nm: 0 for nm, _ in chains}
        for step in range(16):
            for ci, (nm, cs) in enumerate(chains):
                if step >= len(cs):
                    continue
                c = cs[step]
                ch = slice(c * 128, (c + 1) * 128)
                S = Sst[nm][cur[nm]]; Sn = Sst[nm][1 - cur[nm]]
                bk = K.ps[6 + (ci % 2)] if ci < 2 else K.ps[0]
                r0, r1, r2, r3 = (bk[:, i * 128:(i + 1) * 128] for i in range(4))
                vn = vnew[ci]; t = tq[ci]
                P.mm(r0, wTall[:, c, :], S[:])
                P.mm(r1, qT[:, ch], S[:])
                P.tt(vn[:], uall[:, c, :], r0, ALU.subtract)
                P.mm(r2, qkTall[:, c, :], vn[:])
                P.ts(kg2[ci][:], ktok[:, c, :], e2all[:, c:c + 1], None, ALU.mult, e="pool")
                P.mm(r3, kg2[ci][:], vn[:])
                P.ts(t[:], r1, egc[:, c:c + 1], None, ALU.mult)
                if d == 0:
                    P.tt(oacc[:, c, :], r2, t[:], ALU.add)
                else:
                    P.tt(t[:], r2, t[:], ALU.add)
                    P.tt(oacc[:, c, :], oacc[:, c, :], t[:], ALU.add, e="pool")
                P.stt(Sn[:], S[:], egl[:, c:c + 1], r3, ALU.mult, ALU.add)
                cur[nm] = 1 - cur[nm]
        if GDN_STOP == "%d,%d,g3" % (hd, d):
            raise StopGDN()
        for pi, nm in enumerate(("pA", "pB")):
            P.dma(O.ng[pi, l, d, hd], Sst[nm][cur[nm]][:], is_output=True)
        P.release(mk2)
    mk3 = P.mark()
    gsl = P.sb("ggsl", [128, T])
    w = load_w(K, I.w_in[l][:, C_GDNG + hd * 128:C_GDNG + (hd + 1) * 128], 128)
    for tt in range(NTT):
        ps = K.ps[tt % 8][:]
        proj(K, ps, w, 0, 128, tt)
        P.act(gsl[:, tt * 512:(tt + 1) * 512], ps, AF.Silu)
    sq = P.sb("gsq", [128, 128]); ssum = P.sb("gssum", [128, 20])
    for c in range(20):
        P.tt(sq[:], oacc[:, c, :], oacc[:, c, :], ALU.mult)
        P.op("dve", lambda g, c=c: g.tensor_reduce(ssum[:, c:c + 1], sq[:], mybir.AxisListType.X, ALU.add),
             [sq[:]], [ssum[:, c:c + 1]])
    P.act(ssum[:], ssum[:], AF.Sqrt, bias=K.eps[:, 0:1], scale=1.0 / 128.0)
    P.op("dve", lambda g: g.reciprocal(ssum[:], ssum[:]), [ssum[:]], [ssum[:]])
    for c in range(20):
        P.stt(oacc[:, c, :], oacc[:, c, :], ssum[:, c:c + 1], G.ng[:], ALU.mult, ALU.mult)
    for g5 in range(5):
        ps = K.ps[g5 % 8]
        for q in range(4):
            c = g5 * 4 + q
            P.transpose(ps[:, q * 128:(q + 1) * 128], oacc[:, c, :], K.ident[:])
        P.tt(K.yb[:, hd, g5 * 512:(g5 + 1) * 512], ps[:], gsl[:, g5 * 512:(g5 + 1) * 512], ALU.mult)
    if K.dbg and K.stop == ("gdn", l):
        pass
    P.release(mk3)
    P.release(mk)


def _grid_pos(n_tokens, dim):
    t = np.arange(n_tokens)
    r = (t // 64).astype(np.float32)
    col = (t % 64).astype(np.float32)
    quarter = dim // 4
    omega = (1.0 / (10000.0 ** (np.arange(quarter, dtype=np.float32) / np.float32(quarter)))).astype(np.float32)
    er = r[:, None] * omega
    ec = col[:, None] * omega
    return np.concatenate([np.sin(er), np.cos(er), np.sin(ec), np.cos(ec)], axis=-1).astype(np.float32)


_CONSTS = None


def host_consts():
    global _CONSTS
    if _CONSTS is None:
        c = {}
        c["ident"] = np.eye(128, dtype=np.float32)
        c["pos"] = _grid_pos(2048, 1024)
        c.update(extra_consts())
        _CONSTS = c
    return _CONSTS


_PROG = None

WEIGHT_NAMES = ["w_mod", "b_mod", "norm_g", "w_in", "hy_conv_w", "hy_conv_b", "hy_f_w1", "hy_f_b1", "hy_f_w2", "hy_f_b2",
                "hy_f_w3", "hy_f_freq", "hy_bias", "s5_lambda_re", "s5_lambda_im", "s5_log_step", "s5_B_re", "s5_B_im",
                "s5_C_re", "s5_C_im", "s5_D", "s5_glu_w", "s5_glu_b", "gdn_conv_w", "gdn_A_log", "gdn_dt_bias",
                "gdn_norm_g", "w_branch", "w_out", "final_norm_g"]


def make_in_maps(inputs, ncores=8):
    consts = host_consts()
    f = lambda a: np.ascontiguousarray(np.asarray(a, dtype=np.float32))
    w = {k: f(inputs[k]) for k in WEIGHT_NAMES}
    xs, xp, c, cctx = f(inputs["x_sample"]), f(inputs["x_prompt"]), f(inputs["c"]), f(inputs["c_ctx"])
    sg, ss = f(inputs["state_gdn"]), f(inputs["state_s5"])
    maps = []
    for b in range(ncores):
        m = dict(w)
        m.update(consts)
        m["xs"] = xs[b]
        m["xp"] = np.ascontiguousarray(xp[2 * b:2 * b + 2].reshape(512, 1024))
        m["cvec"] = np.ascontiguousarray(np.stack([cctx, c[b]], axis=0))
        m["sg"] = sg[b]
        m["ss"] = ss[b]
        maps.append(m)
    return maps


def kernel(**inputs):
    global _PROG
    if _PROG is None:
        _PROG = build_program()
    nc, K = _PROG
    maps = make_in_maps(inputs)
    res = run_bass_kernel_spmd(nc, maps, core_ids=list(range(8)))
    r = res.results
    y_sample = np.stack([r[b]["ys"] for b in range(8)], axis=0).astype(np.float32)
    y_prompt = np.concatenate([r[b]["yp"].reshape(2, 256, 1024) for b in range(8)], axis=0).astype(np.float32)
    ng = np.concatenate([r[b]["ng"] for b in range(8)], axis=0).astype(np.float32)
    ns = np.concatenate([r[b]["ns"] for b in range(8)], axis=0).astype(np.float32)
    return (y_prompt, y_sample, ng, ns)
```

```python
import numpy as np
import concourse.bass as bass
import concourse.mybir as mybir
from concourse.alu_op_type import AluOpType as ALU

F32 = mybir.dt.float32
BF16 = mybir.dt.bfloat16
AF = mybir.ActivationFunctionType
import os as _os0
FP32R = _os0.environ.get("FP32R", "0") == "1"
LOADQ = _os0.environ.get("LOADQ", "act")


class Prog:
    NDMA = 24

    def __init__(self, nc, same_engine_sync=True):
        self.nc = nc
        self.eng = {"pe": nc.tensor, "act": nc.scalar, "dve": nc.vector, "pool": nc.gpsimd, "sp": nc.sync}
        self.stack = []
        self.sem = {}
        for e in self.eng:
            cm = nc.semaphore("s_" + e)
            self.sem[e] = cm.__enter__()
            self.stack.append(cm)
        self.cnt = {e: 0 for e in self.eng}
        self.dsem = []
        for i in range(self.NDMA):
            cm = nc.semaphore("d_%d" % i)
            self.dsem.append(cm.__enter__())
            self.stack.append(cm)
        self.dval = [0] * self.NDMA
        self.dnext = 0
        self.seen = {e: {} for e in self.eng}
        self.recs = {}
        self.same = same_engine_sync
        self.out_dmas = []
        self.n_inst = 0
        self.tensors = {}
        self.ro = set()

    def sb(self, name, shape, dt=F32):
        self.uid = getattr(self, "uid", 0) + 1
        if not hasattr(self, "names"):
            self.names = {}
        self.names.setdefault(name, []).append("%s_%d" % (name, self.uid))
        name = "%s_%d" % (name, self.uid)
        cm = self.nc.sbuf_tensor(name, list(shape), dt)
        t = cm.__enter__()
        self.stack.append(cm)
        return t

    def ps(self, name, shape, dt=F32):
        cm = self.nc.psum_tensor(name, list(shape), dt)
        t = cm.__enter__()
        self.stack.append(cm)
        return t

    @staticmethod
    def region(ap):
        t = ap.tensor
        name = t.name
        shp = list(t.shape)
        space = str(ap.space)
        off = int(ap.offset)
        dims = [(int(s), int(c)) for (s, c) in ap.ap]
        if "DRAM" in space.upper() or "HBM" in space.upper() or type(t).__name__.startswith("DRam"):
            lo = off
            hi = off
            for s, c in dims:
                if s >= 0:
                    hi += s * (c - 1)
                else:
                    lo += s * (c - 1)
            return (name, 0, 1, lo, hi)
        if name.startswith("ps") and "PSUM" in space.upper():
            return (name, 0, 128, 0, 1 << 30)
        pitch = 1
        for d in shp[1:]:
            pitch *= int(d)
        p0 = off // pitch
        f0 = off % pitch
        ps_, pc = dims[0]
        if ps_ == 0:
            p1 = p0 + 1
        else:
            p1 = p0 + pc * max(1, ps_ // pitch)
        lo = f0
        hi = f0
        for s, c in dims[1:]:
            if s >= 0:
                hi += s * (c - 1)
            else:
                lo += s * (c - 1)
        return (name, p0, p1, lo, hi)

    def _deps(self, reads, writes):
        deps = set()
        for ap in reads:
            name, p0, p1, lo, hi = self.region(ap)
            for r in self.recs.get(name, ()):
                if r[0] < p1 and p0 < r[1] and r[2] <= hi and lo <= r[3]:
                    if r[4] is not None:
                        deps.add(r[4])
        for ap in writes:
            name, p0, p1, lo, hi = self.region(ap)
            for r in self.recs.get(name, ()):
                if r[0] < p1 and p0 < r[1] and r[2] <= hi and lo <= r[3]:
                    if r[4] is not None:
                        deps.add(r[4])
                    for rd in r[5].items():
                        deps.add(rd)
        return deps

    def _record(self, reads, writes, me):
        for ap in reads:
            name, p0, p1, lo, hi = self.region(ap)
            if name in self.ro:
                continue
            lst = self.recs.setdefault(name, [])
            found = False
            for r in lst:
                if r[0] == p0 and r[1] == p1 and r[2] == lo and r[3] == hi:
                    if r[5].get(me[0], 0) < me[1]:
                        r[5][me[0]] = me[1]
                    found = True
            if not found:
                lst.append([p0, p1, lo, hi, None, {me[0]: me[1]}])
        for ap in writes:
            name, p0, p1, lo, hi = self.region(ap)
            lst = self.recs.setdefault(name, [])
            keep = []
            for r in lst:
                if p0 <= r[0] and r[1] <= p1 and lo <= r[2] and r[3] <= hi:
                    continue
                keep.append(r)
            keep.append([p0, p1, lo, hi, me, {}])
            self.recs[name] = keep

    def _wait(self, e, deps):
        eng = self.eng[e]
        best = {}
        for (src, val) in deps:
            if src == e and (not self.same or e == "pe"):
                continue
            if val > best.get(src, 0):
                best[src] = val
        for src, val in best.items():
            if self.seen[e].get(src, 0) >= val:
                continue
            self.seen[e][src] = val
            sem = self.sem[src] if isinstance(src, str) else self.dsem[src]
            eng.wait_ge(sem, val)

    def op(self, e, fn, reads, writes):
        deps = self._deps(reads, writes)
        self._wait(e, deps)
        inst = fn(self.eng[e])
        self.cnt[e] += 1
        inst.then_inc(self.sem[e], 1)
        self._record(reads, writes, (e, self.cnt[e]))
        self.n_inst += 1
        return inst

    def dma(self, out, in_, q=None, is_output=False, **kw):
        if q is None:
            is_load = not type(out.tensor).__name__.startswith("DRam")
            q = LOADQ if is_load else "sp"
        deps = self._deps([in_], [out])
        k = self.dnext
        self.dnext = (self.dnext + 1) % self.NDMA
        if self.dval[k] > 0:
            deps.add((k, self.dval[k]))
        self._wait(q, deps)
        inst = self.eng[q].dma_start(out=out, in_=in_, **kw)
        self.dval[k] += 16
        inst.then_inc(self.dsem[k], 16)
        me = (k, self.dval[k])
        self._record([in_], [out], me)
        if is_output:
            self.out_dmas.append(me)
        self.n_inst += 1
        return inst

    def finish(self):
        e = "sp"
        self._wait(e, set(self.out_dmas) | {(k, v) for k, v in enumerate(self.dval) if v > 0}
                   | {(x, self.cnt[x]) for x in self.eng if x != e and self.cnt[x] > 0})

    def barrier(self):
        allw = {(x, self.cnt[x]) for x in self.eng if self.cnt[x] > 0} | {(k, v) for k, v in enumerate(self.dval) if v > 0}
        for e in self.eng:
            self._wait(e, set(allw))

    def mark(self):
        return len(self.stack)

    def release(self, m):
        self.barrier()
        while len(self.stack) > m:
            cm = self.stack.pop()
            cm.__exit__(None, None, None)

    def close(self):
        for cm in reversed(self.stack):
            cm.__exit__(None, None, None)
        self.stack = []

    def mm(self, out, lhsT, rhs, start=True, stop=True, **kw):
        if FP32R and lhsT.dtype == F32 and rhs.dtype == F32 and int(rhs.ap[-1][1]) % 2 == 0:
            lhsT = lhsT.bitcast(mybir.dt.float32r)
            rhs = rhs.bitcast(mybir.dt.float32r)
        return self.op("pe", lambda g: g.matmul(out, lhsT, rhs, start=start, stop=stop, **kw), [lhsT, rhs], [out])

    def transpose(self, out, in_, ident):
        return self.op("pe", lambda g: g.transpose(out, in_, ident), [in_, ident], [out])

    def act(self, out, in_, func, bias=None, scale=None, e="act", extra_reads=()):
        kw = {}
        rd = [in_] + list(extra_reads)
        if bias is not None:
            kw["bias"] = bias
            if not isinstance(bias, (int, float)):
                rd.append(bias)
        if scale is not None:
            kw["scale"] = scale
            if not isinstance(scale, (int, float)):
                rd.append(scale)
        return self.op(e, lambda g: g.activation(out, in_, func, **kw), rd, [out])

    def tt(self, out, in0, in1, op, e="dve"):
        return self.op(e, lambda g: g.tensor_tensor(out, in0, in1, op), [in0, in1], [out])

    def ts(self, out, in0, s1, s2, op0, op1=None, e="dve", accum_out=None):
        rd = [in0]
        for s in (s1, s2):
            if s is not None and not isinstance(s, (int, float)):
                rd.append(s)
        wr = [out]
        kw = {}
        if accum_out is not None:
            kw["accum_out"] = accum_out
            wr.append(accum_out)
        if op1 is None:
            return self.op(e, lambda g: g.tensor_scalar(out, in0, s1, s2, op0, **kw), rd, wr)
        return self.op(e, lambda g: g.tensor_scalar(out, in0, s1, s2, op0, op1, **kw), rd, wr)

    def stt(self, out, in0, scalar, in1, op0, op1, e="dve"):
        rd = [in0, in1]
        if not isinstance(scalar, (int, float)):
            rd.append(scalar)
        return self.op(e, lambda g: g.scalar_tensor_tensor(out, in0, scalar, in1, op0, op1), rd, [out])

    def copy(self, out, in_, e="dve"):
        if e == "act":
            return self.op(e, lambda g: g.copy(out, in_), [in_], [out])
        return self.op(e, lambda g: g.tensor_copy(out, in_), [in_], [out])

    def memset(self, ap, val, e="pool"):
        return self.op(e, lambda g: g.memset(ap, val), [], [ap])

from concourse.bass_utils import run_bass_kernel_spmd
import ml_dtypes
import math

I32 = mybir.dt.int32
T = 2560
NTT = 5
D_MODEL = 1024
SEQS = [(0, 2048), (2048, 256), (2304, 256)]
C_HYG, C_S5, C_S5G, C_GDN, C_BETA, C_ALPHA, C_GDNG, C_MERGE = 3072, 4096, 5120, 6144, 9216, 9232, 9248, 10272
IN_COLS = 13344
MIX = {"hy": True, "s5": True, "gdn": True}
NLAYER = 2


class Ctx:
    pass


def build_program(dbg=None, stop=None):
    nc = bass.Bass("TRN2", target_bir_lowering=False)
    P = Prog(nc)
    K = Ctx()
    K.nc, K.P = nc, P
    K.dbg = dbg
    K.stop = stop

    def din(name, shape, dt=F32):
        P.ro.add(name)
        return nc.dram_tensor(name, list(shape), dt, kind="ExternalInput").ap()

    def dout(name, shape, dt=F32):
        return nc.dram_tensor(name, list(shape), dt, kind="ExternalOutput").ap()

    def dscr(name, shape, dt=F32):
        return nc.dram_tensor(name, list(shape), dt, kind="Internal").ap()

    K.din, K.dout, K.dscr = din, dout, dscr
    I = Ctx()
    K.I = I
    I.xs = din("xs", [2048, 1024]); I.xp = din("xp", [512, 1024]); I.cvec = din("cvec", [2, 1024])
    I.sg = din("sg", [2, 2, 8, 128, 128]); I.ss = din("ss", [2, 2, 64, 64, 2])
    I.w_mod = din("w_mod", [2, 1024, 3072]); I.b_mod = din("b_mod", [2, 3072]); I.norm_g = din("norm_g", [2, 1024])
    I.w_in = din("w_in", [2, 1024, IN_COLS])
    I.hy_conv_w = din("hy_conv_w", [2, 3, 3072]); I.hy_conv_b = din("hy_conv_b", [2, 3072])
    I.hy_f_w1 = din("hy_f_w1", [2, 33, 64]); I.hy_f_b1 = din("hy_f_b1", [2, 64])
    I.hy_f_w2 = din("hy_f_w2", [2, 64, 64]); I.hy_f_b2 = din("hy_f_b2", [2, 64])
    I.hy_f_w3 = din("hy_f_w3", [2, 64, 4096]); I.hy_f_freq = din("hy_f_freq", [2, 64]); I.hy_bias = din("hy_bias", [2, 2, 1024])
    I.s5_lambda_re = din("s5_lambda_re", [2, 2, 64, 64]); I.s5_lambda_im = din("s5_lambda_im", [2, 2, 64, 64])
    I.s5_log_step = din("s5_log_step", [2, 2, 64])
    I.s5_B_re = din("s5_B_re", [2, 2, 64, 64, 16]); I.s5_B_im = din("s5_B_im", [2, 2, 64, 64, 16])
    I.s5_C_re = din("s5_C_re", [2, 2, 64, 16, 64]); I.s5_C_im = din("s5_C_im", [2, 2, 64, 16, 64])
    I.s5_D = din("s5_D", [2, 1024]); I.s5_glu_w = din("s5_glu_w", [2, 1024, 2048]); I.s5_glu_b = din("s5_glu_b", [2, 2048])
    I.gdn_conv_w = din("gdn_conv_w", [2, 3, 3072]); I.gdn_A_log = din("gdn_A_log", [2, 2, 8]); I.gdn_dt_bias = din("gdn_dt_bias", [2, 2, 8])
    I.gdn_norm_g = din("gdn_norm_g", [2, 128]); I.w_branch = din("w_branch", [2, 3, 1024, 1024]); I.w_out = din("w_out", [2, 1024, 1024])
    I.final_norm_g = din("final_norm_g", [1024])
    I.ident = din("ident", [128, 128]); I.pos = din("pos", [2048, 1024])
    O = Ctx()
    K.O = O
    O.ys = dout("ys", [2048, 1024]); O.yp = dout("yp", [512, 1024])
    O.ng = dout("ng", [2, 2, 2, 8, 128, 128]); O.ns = dout("ns", [2, 2, 2, 64, 64, 2])
    if dbg:
        K.dbg_out = dout("dbg", list(dbg), BF16)
        K.dbg32 = dout("dbg32", list(dbg))
    K.xT = dscr("xT_scr", [1024, T])
    K.xTv = K.xT.rearrange("(kc k) t -> k kc t", k=128)

    with nc.allow_non_contiguous_dma("layout"), nc.allow_low_precision("bf16 matmul per problem tolerance"):
        K.ps = [P.ps("ps%d" % i, [128, 512]) for i in range(8)]
        K.ident = P.sb("ident_sb", [128, 128]); P.dma(K.ident[:], I.ident)
        K.ones = P.sb("ones", [128, 128]); P.memset(K.ones[:], 1.0)
        K.wst = [P.sb("wst0", [128, 8, 128])] * 2
        K.wbf = [P.sb("wbf0", [128, 8, 128], BF16)] * 2
        K.eps = P.sb("eps", [128, 1]); P.memset(K.eps[:], 1e-6)
        K.one1 = P.sb("one1", [128, 1]); P.memset(K.one1[:], 1.0)
        K.wi = 0
        K.MG = dscr("mg_scr", [1024, T])
        K.MGv = K.MG.rearrange("(kc k) t -> k kc t", k=128)
        phase_mod(K)
        prep_all(K)
        K.h = P.sb("h", [128, 8, T], BF16)
        phase_input(K)
        for l in range(NLAYER):
            phase_norm(K, l)
            mk = P.mark()
            K.yb = P.sb("yb", [128, 8, T], BF16)
            for n, nm in enumerate(["hy", "s5", "gdn"]):
                if MIX[nm]:
                    {"hy": mixer_hyena, "s5": mixer_s5, "gdn": mixer_gdn}[nm](K, l)
                else:
                    mixer_stub(K, l, n)
                if K.stop == (nm, l):
                    for kc in range(8):
                        P.dma(K.dbg_out[kc * 128:(kc + 1) * 128, :], K.yb[:, kc, :], is_output=True)
                    P.finish(); P.close()
                    return nc, K
                phase_merge(K, l, n)
            P.release(mk)
            phase_out(K, l)
            if K.stop == ("out", l):
                mk_ = P.mark()
                xt_ = P.sb("dbgx", [128, 8, 512])
                for tt in range(NTT):
                    P.dma(xt_[:], K.xTv[:, :, tt * 512:(tt + 1) * 512])
                    P.dma(K.dbg32.rearrange("(kc k) t -> k kc t", k=128)[:, :, tt * 512:(tt + 1) * 512], xt_[:], is_output=True)
                P.finish(); P.close()
                return nc, K
        phase_final(K)
        P.finish()
    P.close()
    return nc, K


def load_w(K, src, ncols, eng="dve"):
    P = K.P
    i = K.wi
    K.wi ^= 1
    st = K.wst[i][:, :, 0:ncols]
    P.dma(st, src.rearrange("(kc k) c -> k kc c", k=128))
    bf = K.wbf[i][:, :, 0:ncols]
    P.copy(bf, st, e=eng)
    return bf


def proj(K, ps, wbf, c0, ncols, tt, act=None, ntok=512, t0=None):
    P = K.P
    if t0 is None:
        t0 = tt * 512
    src = K.h if act is None else act
    for kc in range(8):
        P.mm(ps, wbf[:, kc, c0:c0 + ncols], src[:, kc, t0:t0 + ntok], start=(kc == 0), stop=(kc == 7))


def phase_mod(K):
    P, I = K.P, K.I
    K.modT = P.sb("modT", [128, 2, 24, 2])
    K.gs = P.sb("gs", [128, 2, 8, 2])
    K.fg = P.sb("fgT", [128, 8]); P.dma(K.fg[:], I.final_norm_g.rearrange("(kc k) -> k kc", k=128))
    mk = P.mark()
    cT = P.sb("cT", [128, 8, 2]); [P.dma(cT[:, :, w], I.cvec[w].rearrange("(kc k) -> k kc", k=128)) for w in range(2)]
    sc = P.sb("sc", [128, 8, 2]); P.act(sc[:], cT[:], AF.Silu)
    bmodT = P.sb("bmodT", [128, 2, 24]); [P.dma(bmodT[:, l, :], I.b_mod[l].rearrange("(ft f) -> f ft", f=128)) for l in range(2)]
    wm = [P.sb("wm%d" % i, [128, 8, 512]) for i in range(2)]
    for l in range(NLAYER):
        for g6 in range(6):
            w = wm[g6 % 2]
            P.dma(w[:], I.w_mod[l].rearrange("(kc k) c -> k kc c", k=128)[:, :, g6 * 512:(g6 + 1) * 512])
            for f4 in range(4):
                ft = g6 * 4 + f4
                ps = K.ps[ft % 8][:, 0:2]
                for kc in range(8):
                    P.mm(ps, w[:, kc, f4 * 128:(f4 + 1) * 128], sc[:, kc, :], start=(kc == 0), stop=(kc == 7))
                P.ts(K.modT[:, l, ft, :], ps, bmodT[:, l, ft:ft + 1], None, ALU.add)
    ng = P.sb("normgT", [128, 2, 8]); [P.dma(ng[:, l, :], I.norm_g[l].rearrange("(kc k) -> k kc", k=128)) for l in range(2)]
    for l in range(NLAYER):
        for w in range(2):
            P.ts(K.gs[:, l, :, w], K.modT[:, l, 8:16, w], 1.0, None, ALU.add)
            P.tt(K.gs[:, l, :, w], K.gs[:, l, :, w], ng[:, l, :], ALU.mult)
    P.release(mk)


def phase_input(K):
    P, I = K.P, K.I
    mk = P.mark()
    xt = [P.sb("xin%d" % i, [128, 1024]) for i in range(2)]
    pt = [P.sb("pin%d" % i, [128, 1024]) for i in range(2)]
    xo = [P.sb("xo%d" % i, [128, 8, 128]) for i in range(2)]
    for tb in range(20):
        x = xt[tb % 2]
        if tb < 16:
            P.dma(x[:], I.xs[tb * 128:(tb + 1) * 128, :])
            p = pt[tb % 2]
            P.dma(p[:], I.pos[tb * 128:(tb + 1) * 128, :])
            P.tt(x[:], x[:], p[:], ALU.add)
        else:
            P.dma(x[:], I.xp[(tb - 16) * 128:(tb - 15) * 128, :])
        o = xo[tb % 2]
        for half in range(2):
            ps = K.ps[(tb * 2 + half) % 8]
            for q in range(4):
                kc = half * 4 + q
                P.transpose(ps[:, q * 128:(q + 1) * 128], x[:, kc * 128:(kc + 1) * 128], K.ident[:])
            P.copy(o[:, half * 4:(half + 1) * 4, :], ps[:].rearrange("p (q t) -> p q t", q=4), e=("act" if half else "dve"))
        P.dma(K.xTv[:, :, tb * 128:(tb + 1) * 128], o[:])
    P.release(mk)


def norm_tile(K, tt, xtile, sq, r, psS):
    P = K.P
    P.dma(xtile[:], K.xTv[:, :, tt * 512:(tt + 1) * 512])
    P.act(sq[:], xtile[:], AF.Square)
    for kc in range(8):
        P.mm(psS, K.ones[:], sq[:, kc, :], start=(kc == 0), stop=(kc == 7))
    P.act(r[:], psS, AF.Sqrt, bias=K.eps[:, 0:1], scale=1.0 / 1024.0)
    P.op("dve", lambda g: g.reciprocal(r[:], r[:]), [r[:]], [r[:]])


def phase_norm(K, l):
    P = K.P
    mk = P.mark()
    norm_alloc(K)
    for tt in range(NTT):
        x = K.nx[tt % 2]; r = K.nr[tt % 2]
        norm_tile(K, tt, x, K.nsq, r, K.ps[tt % 8][:])
        w = 1 if tt < 4 else 0
        for kc in range(8):
            P.tt(K.ntmp[:, kc, :], x[:, kc, :], r[:], ALU.mult)
            P.ts(K.h[:, kc, tt * 512:(tt + 1) * 512], K.ntmp[:, kc, :], K.gs[:, l, kc, w:w + 1], K.modT[:, l, kc, w:w + 1],
                 ALU.mult, ALU.add, e=("pool" if kc % 2 else "dve"))
    P.release(mk)


def norm_alloc(K):
    P = K.P
    K.nx = [P.sb("nx0", [128, 8, 512])] * 2
    K.nsq = P.sb("nsq", [128, 8, 512])
    K.nr = [P.sb("nr%d" % i, [128, 512]) for i in range(2)]
    K.ntmp = K.nsq


def mixer_stub(K, l, n):
    P = K.P
    for kc in range(8):
        P.copy(K.yb[:, kc, :], K.h[:, kc, :], e=("pool" if kc % 2 else "dve"))


def phase_merge(K, l, n):
    P, I = K.P, K.I
    mk = P.mark()
    msig = [P.sb("msig%d" % i, [128, 512]) for i in range(2)]
    mpr = [P.sb("mpr%d" % i, [128, 512]) for i in range(2)]
    mold = [P.sb("mold%d" % i, [128, 512]) for i in range(2)]
    mwst = [P.sb("mwst%d" % i, [128, 8, 256]) for i in range(2)]
    mwbf = [P.sb("mwbf%d" % i, [128, 8, 256], BF16) for i in range(2)]
    it = 0
    for ct in range(8):
        b2 = ct % 2
        P.dma(mwst[b2][:, :, 0:128], I.w_branch[l, n][:, ct * 128:(ct + 1) * 128].rearrange("(kc k) c -> k kc c", k=128))
        P.dma(mwst[b2][:, :, 128:256], I.w_in[l][:, C_MERGE + n * 1024 + ct * 128: C_MERGE + n * 1024 + (ct + 1) * 128].rearrange("(kc k) c -> k kc c", k=128))
        P.copy(mwbf[b2][:], mwst[b2][:])
        wb = mwbf[b2][:, :, 0:128]
        wm = mwbf[b2][:, :, 128:256]
        for tt in range(NTT):
            psp = K.ps[(it * 2) % 8][:]; psg = K.ps[(it * 2 + 1) % 8][:]
            sig = msig[it % 2]; pr = mpr[it % 2]; old = mold[it % 2]
            it += 1
            dst = K.MGv[:, ct, tt * 512:(tt + 1) * 512]
            if n > 0:
                P.dma(old[:], dst)
            proj(K, psg, wm, 0, 128, tt)
            proj(K, psp, wb, 0, 128, tt, act=K.yb)
            P.act(sig[:], psg, AF.Sigmoid)
            P.tt(pr[:], psp, sig[:], ALU.mult)
            if n > 0:
                P.tt(pr[:], pr[:], old[:], ALU.add)
            P.dma(dst, pr[:])
    P.release(mk)


def phase_out(K, l):
    P, I = K.P, K.I
    mk = P.mark()
    ox = [P.sb("ox%d" % i, [128, 512]) for i in range(2)]
    wo = P.sb("wo_res", [128, 8, 1024], BF16)
    mg = P.sb("mg_in", [128, 8, 512])
    mgb = [P.sb("mg_bf%d" % i, [128, 8, 512], BF16) for i in range(2)]
    for c4 in range(8):
        w = load_w(K, I.w_out[l][:, c4 * 128:(c4 + 1) * 128], 128)
        P.copy(wo[:, :, c4 * 128:(c4 + 1) * 128], w, e="pool")
    it = 0
    for tt in range(NTT):
        P.dma(mg[:], K.MGv[:, :, tt * 512:(tt + 1) * 512])
        m = mgb[tt % 2]
        P.copy(m[:], mg[:], e="act")
        for ct in range(8):
            ps = K.ps[it % 8][:]
            x = ox[it % 2]
            it += 1
            P.dma(x[:], K.xTv[:, ct, tt * 512:(tt + 1) * 512])
            for kc in range(8):
                P.mm(ps, wo[:, kc, ct * 128:(ct + 1) * 128], m[:, kc, :], start=(kc == 0), stop=(kc == 7))
            w = 1 if tt < 4 else 0
            P.stt(x[:], ps, K.modT[:, l, 16 + ct, w:w + 1], x[:], ALU.mult, ALU.add)
            P.dma(K.xTv[:, ct, tt * 512:(tt + 1) * 512], x[:])
    P.release(mk)


def phase_final(K):
    P, O = K.P, K.O
    mk = P.mark()
    norm_alloc(K)
    yo = [P.sb("yo%d" % i, [128, 1024]) for i in range(2)]
    it = 0
    for tt in range(NTT):
        x = K.nx[tt % 2]; r = K.nr[tt % 2]
        norm_tile(K, tt, x, K.nsq, r, K.ps[tt % 8][:])
        for kc in range(8):
            P.tt(K.ntmp[:, kc, :], x[:, kc, :], r[:], ALU.mult)
            P.ts(K.ntmp[:, kc, :], K.ntmp[:, kc, :], K.fg[:, kc:kc + 1], None, ALU.mult, e=("pool" if kc % 2 else "dve"))
        for b in range(4):
            y = yo[it % 2]
            for half in range(2):
                ps = K.ps[(it * 2 + half) % 8]
                for q in range(4):
                    kc = half * 4 + q
                    P.transpose(ps[:, q * 128:(q + 1) * 128], K.ntmp[:, kc, b * 128:(b + 1) * 128], K.ident[:])
                P.copy(y[:, half * 512:(half + 1) * 512], ps[:], e=("act" if half else "dve"))
            it += 1
            t0 = tt * 512 + b * 128
            if t0 < 2048:
                P.dma(O.ys[t0:t0 + 128, :], y[:], is_output=True)
            else:
                P.dma(O.yp[t0 - 2048:t0 - 2048 + 128, :], y[:], is_output=True)


HY_L = (2048, 256)


def _dft_consts(L):
    N = 2 * L
    t = np.arange(L, dtype=np.int64)[:, None]
    j = np.arange(N, dtype=np.int64)[None, :]
    f = np.where(j < L, j, j - L)
    ang = 2.0 * np.pi * ((t * f) % N).astype(np.float64) / N
    Dm = np.where(j < L, np.cos(ang), np.sin(ang))
    Dm[:, L] = np.where(np.arange(L) % 2 == 0, 1.0, -1.0)
    wg = np.full(N, 2.0 / N)
    wg[0] = 1.0 / N
    wg[L] = 1.0 / N
    return Dm, wg


def extra_consts():
    c = {}
    f32 = np.float32
    for L in HY_L:
        Dm, wg = _dft_consts(L)
        c["dft%d" % L] = np.ascontiguousarray(Dm.astype(ml_dtypes.bfloat16))
        c["dftT%d" % L] = np.ascontiguousarray(Dm.T.astype(ml_dtypes.bfloat16))
        c["wgt%d" % L] = np.ascontiguousarray(wg.astype(f32).reshape(-1, 128).T)
        t = np.linspace(0.0, 1.0, L, dtype=f32)[:, None]
        bands = 16
        fb = np.linspace(1e-4, bands - 1, bands, dtype=f32)[None, :]
        wpos = (f32(2.0 * math.pi) * np.arange(L, dtype=f32)[:, None] / f32(L)).astype(f32)
        z = np.concatenate([t, np.cos(fb * wpos), -np.sin(fb * wpos)], axis=-1).astype(f32)
        c["zfT%d" % L] = np.ascontiguousarray(z.T)
        c["negt%d" % L] = np.ascontiguousarray((-t[:, 0]).reshape(-1, 128).T.astype(f32))
    deltas = np.abs(np.linspace(math.log(1e-2) / 1.5, math.log(1e-2) / 0.3, 1024, dtype=f32))
    c["deltas"] = np.ascontiguousarray(np.broadcast_to(deltas[None, :], (128, 1024)).astype(f32))
    c.update(s5_host_consts())
    c.update(gdn_host_consts())
    return c


def sin_rr(K, out, x, tmp_i, tmp_f):
    P = K.P
    P.ts(tmp_i, x, 1.0 / (2.0 * math.pi), None, ALU.mult)
    P.copy(tmp_f, tmp_i)
    P.stt(tmp_f, tmp_f, -2.0 * math.pi, x, ALU.mult, ALU.add)
    P.ts(tmp_f, tmp_f, 3.1415925, -3.1415925, ALU.min, ALU.max)
    P.act(out, tmp_f, AF.Sin)


def prep_all(K):
    P, I = K.P, K.I
    din, dscr = K.din, K.dscr
    I.dft = {L: din("dft%d" % L, [L, 2 * L], BF16) for L in HY_L}
    I.dftT = {L: din("dftT%d" % L, [2 * L, L], BF16) for L in HY_L}
    I.wgt = {L: din("wgt%d" % L, [128, 2 * L // 128]) for L in HY_L}
    I.zfT = {L: din("zfT%d" % L, [33, L]) for L in HY_L}
    I.negt = {L: din("negt%d" % L, [128, L // 128]) for L in HY_L}
    I.deltas = din("deltas", [128, 1024])
    for nm in ("g_L", "g_Ls", "g_U", "g_Us", "g_bd", "g_nbd"):
        setattr(I, nm, din(nm, [128, 128]))
    I.s5_bm = din("s5_bm", [128, 128]); I.s5_bmg = din("s5_bmg", [128, 8]); I.s5_mask2 = din("s5_mask2", [128, 32])
    K.KH = {(l, L): dscr("kh_%d_%d" % (l, L), [2 * L, 2048]) for l in range(NLAYER) for L in HY_L}
    if not MIX["hy"]:
        return
    for l in range(NLAYER):
        for L in HY_L:
            filter_gen(K, l, L)


def filter_gen(K, l, L):
    P, I = K.P, K.I
    mk = P.mark()
    N = 2 * L
    NTB = L // 128
    NJ = N // 128
    CH = min(512, L)
    w1 = P.sb("fw1", [33, 64]); P.dma(w1[:], I.hy_f_w1[l])
    w2 = P.sb("fw2", [64, 64]); P.dma(w2[:], I.hy_f_w2[l])
    w3 = P.sb("fw3", [64, 4096]); P.dma(w3[:], I.hy_f_w3[l])
    b1 = P.sb("fb1", [64, 1]); P.dma(b1[:], I.hy_f_b1[l].rearrange("(f o) -> f o", o=1))
    b2 = P.sb("fb2", [64, 1]); P.dma(b2[:], I.hy_f_b2[l].rearrange("(f o) -> f o", o=1))
    fq = P.sb("ffq", [64, 1]); P.dma(fq[:], I.hy_f_freq[l].rearrange("(f o) -> f o", o=1))
    zf = P.sb("fzf", [33, L]); P.dma(zf[:], I.zfT[L])
    negt = P.sb("fnegt", [128, NTB]); P.dma(negt[:], I.negt[L])
    wgt = P.sb("fwgt", [128, NJ]); P.dma(wgt[:], I.wgt[L])
    dl = P.sb("fdl", [128, 1024]); P.dma(dl[:], I.deltas)
    h1 = P.sb("fh1", [64, L]); h2 = P.sb("fh2", [64, L])
    pre = P.sb("fpre", [64, 512]); ti = P.sb("fti", [64, 512], I32); tf = P.sb("ftf", [64, 512])
    for (wm, bb, src, dst, kk) in ((w1, b1, zf, h1, 33), (w2, b2, h1, h2, 64)):
        for c in range(L // CH):
            ps = K.ps[c % 8][0:64, 0:CH]
            P.mm(ps, wm[0:kk, :], src[0:kk, c * CH:(c + 1) * CH])
            P.ts(pre[:, 0:CH], ps, bb[:, 0:1], fq[:, 0:1], ALU.add, ALU.mult)
            sin_rr(K, dst[:, c * CH:(c + 1) * CH], pre[:, 0:CH], ti[:, 0:CH], tf[:, 0:CH])
    FS = P.sb("fFS", [128, NTB, 1024], BF16); FD = P.sb("fFD", [128, NTB, 1024], BF16)
    dec = P.sb("fdec", [128, 1024])
    ffw = P.sb("fffw", [128, 1024]); fbw = P.sb("ffbw", [128, 1024]); fab = P.sb("ffab", [128, 1024])
    rn = P.sb("frn", [128, 1024])
    Dt = [P.sb("fDt%d" % i, [128, NTB, 128], BF16) for i in range(2)]
    kh = [P.sb("fkh%d" % i, [128, 1024]) for i in range(2)]
    Dv = I.dft[L].rearrange("(tb t) j -> t tb j", t=128)
    for o in range(2):
        for tb in range(NTB):
            P.act(dec[:], dl[:], AF.Exp, scale=negt[:, tb:tb + 1])
            for d in range(2):
                for hf in range(2):
                    col = o * 2048 + d * 1024 + hf * 512
                    P.mm(K.ps[d * 2 + hf][:], h2[:, tb * 128:(tb + 1) * 128], w3[:, col:col + 512])
            for hf in range(2):
                P.tt(ffw[:, hf * 512:(hf + 1) * 512], K.ps[hf][:], dec[:, hf * 512:(hf + 1) * 512], ALU.mult)
                P.tt(fbw[:, hf * 512:(hf + 1) * 512], K.ps[2 + hf][:], dec[:, hf * 512:(hf + 1) * 512], ALU.mult)
            if tb == 0:
                P.memset(fbw[0:1, :], 0.0)
            first = (tb == 0)
            last = (tb == NTB - 1)
            P.act(fab[:], ffw[:], AF.Abs)
            for hf in range(2):
                P.mm(K.ps[4 + hf][:], K.ones[:], fab[:, hf * 512:(hf + 1) * 512], start=first, stop=False)
            P.act(fab[:], fbw[:], AF.Abs)
            for hf in range(2):
                P.mm(K.ps[4 + hf][:], K.ones[:], fab[:, hf * 512:(hf + 1) * 512], start=False, stop=last)
            P.tt(FS[:, tb, :], ffw[:], fbw[:], ALU.add)
            P.tt(FD[:, tb, :], ffw[:], fbw[:], ALU.subtract, e="pool")
        for q in range(2):
            P.op("dve", lambda g, q=q: g.reciprocal(rn[:, q * 512:(q + 1) * 512], K.ps[4 + q][:]),
                 [K.ps[4 + q][:]], [rn[:, q * 512:(q + 1) * 512]])
        for jt in range(NJ):
            dt_ = Dt[jt % 2]; ko = kh[jt % 2]
            P.dma(dt_[:], Dv[:, :, jt * 128:(jt + 1) * 128])
            src = FS if jt < NJ // 2 else FD
            for q in range(2):
                ps = K.ps[(jt % 2) * 2 + q][:]
                for tb in range(NTB):
                    P.mm(ps, dt_[:, tb, :], src[:, tb, q * 512:(q + 1) * 512], start=(tb == 0), stop=(tb == NTB - 1))
                P.stt(ko[:, q * 512:(q + 1) * 512], ps, wgt[:, jt:jt + 1], rn[:, q * 512:(q + 1) * 512], ALU.mult, ALU.mult)
            if jt == NJ // 2:
                for q in range(2):
                    ps = K.ps[6 + q][0:1, :]
                    for tb in range(NTB):
                        P.mm(ps, dt_[:, tb, 0:1], FS[:, tb, q * 512:(q + 1) * 512], start=(tb == 0), stop=(tb == NTB - 1))
                    P.stt(ko[0:1, q * 512:(q + 1) * 512], ps, wgt[0:1, jt:jt + 1], rn[0:1, q * 512:(q + 1) * 512], ALU.mult, ALU.mult)
            P.dma(K.KH[(l, L)][jt * 128:(jt + 1) * 128, o * 1024:(o + 1) * 1024], ko[:])
    P.release(mk)


def mixer_hyena(K, l):
    P, I = K.P, K.I
    if not hasattr(K, "hy_scr"):
        K.hy_scr = {nm: K.dscr("hy_" + nm, [1024, T]) for nm in ("x1", "x2", "gs", "zc")}
        K.hy_ch = K.dscr("hy_ch", [4096, 1024], BF16)
    X1, X2, GS, ZC = (K.hy_scr[n] for n in ("x1", "x2", "gs", "zc"))
    mk0 = P.mark()
    zT = P.sb("hzT", [128, 20, 1024], BF16)
    hcw = P.sb("hcw", [128, 3, 24])
    for k in range(3):
        P.dma(hcw[:, k, :], I.hy_conv_w[l, k].rearrange("(ft f) -> f ft", f=128))
    hcb = P.sb("hcb", [128, 24]); P.dma(hcb[:], I.hy_conv_b[l].rearrange("(ft f) -> f ft", f=128))
    hbc = P.sb("hbc", [128, 2, 8])
    for o in range(2):
        P.dma(hbc[:, o, :], I.hy_bias[l, o].rearrange("(ct c) -> c ct", c=128))
    mk = P.mark()
    raw = P.sb("hraw", [128, T]); cv = P.sb("hcv", [128, T])
    for cg in range(8):
        for s in range(4):
            c0 = s * 1024 + cg * 128
            w = load_w(K, I.w_in[l][:, c0:c0 + 128], 128)
            for tt in range(NTT):
                ps = K.ps[(s * NTT + tt) % 8][:]
                proj(K, ps, w, 0, 128, tt)
                if s < 3:
                    P.copy(raw[:, tt * 512:(tt + 1) * 512], ps, e="act")
                else:
                    P.act(raw[:, tt * 512:(tt + 1) * 512], ps, AF.Silu)
            if s == 3:
                P.dma(GS[cg * 128:(cg + 1) * 128, :], raw[:])
                continue
            ft = s * 8 + cg
            P.ts(cv[:], raw[:], hcw[:, 1, ft:ft + 1], hcb[:, ft:ft + 1], ALU.mult, ALU.add)
            for (s0, L) in SEQS:
                P.stt(cv[:, s0 + 1:s0 + L], raw[:, s0:s0 + L - 1], hcw[:, 0, ft:ft + 1], cv[:, s0 + 1:s0 + L], ALU.mult, ALU.add)
                P.stt(cv[:, s0:s0 + L - 1], raw[:, s0 + 1:s0 + L], hcw[:, 2, ft:ft + 1], cv[:, s0:s0 + L - 1], ALU.mult, ALU.add)
            if s == 0:
                P.dma(ZC[cg * 128:(cg + 1) * 128, :], cv[:])
                for g5 in range(5):
                    ps = K.ps[g5 % 8]
                    for q in range(4):
                        tb = g5 * 4 + q
                        P.transpose(ps[:, q * 128:(q + 1) * 128], cv[:, tb * 128:(tb + 1) * 128], K.ident[:])
                    P.copy(zT[:, g5 * 4:(g5 + 1) * 4, cg * 128:(cg + 1) * 128], ps[:].rearrange("p (q t) -> p q t", q=4), e="act")
            else:
                P.dma((X1 if s == 1 else X2)[cg * 128:(cg + 1) * 128, :], cv[:])
    P.release(mk)
    for o in range(2):
        for (s0, L) in SEQS:
            hyena_conv(K, l, o, s0, L, zT, hbc)
    P.release(mk0)


def hyena_conv(K, l, o, s0, L, zT, hbc):
    P, I = K.P, K.I
    X1, X2, GS, ZC = (K.hy_scr[n] for n in ("x1", "x2", "gs", "zc"))
    CHs = K.hy_ch
    N = 2 * L
    NTB = L // 128
    NJ = N // 128
    NJ2 = NJ // 2
    tb0 = s0 // 128
    KH = K.KH[(l, L)]
    Dv = I.dft[L].rearrange("(tb t) j -> t tb j", t=128)
    mk = P.mark()
    Dre = [P.sb("hDre%d" % i, [128, NTB, 128], BF16) for i in range(2)]
    Dim = [P.sb("hDim%d" % i, [128, NTB, 128], BF16) for i in range(2)]
    khr = [P.sb("hkhr%d" % i, [128, 1024]) for i in range(2)]
    khi = [P.sb("hkhi%d" % i, [128, 1024]) for i in range(2)]
    tmp = [P.sb("htmp%d" % i, [128, 512]) for i in range(4)]
    crb = [P.sb("hcrb%d" % i, [128, 1024], BF16) for i in range(2)]
    cib = [P.sb("hcib%d" % i, [128, 1024], BF16) for i in range(2)]
    for jp in range(NJ2):
        b = jp % 2
        P.dma(Dre[b][:], Dv[:, :, jp * 128:(jp + 1) * 128])
        P.dma(Dim[b][:], Dv[:, :, (jp + NJ2) * 128:(jp + NJ2 + 1) * 128])
        P.dma(khr[b][:], KH[jp * 128:(jp + 1) * 128, o * 1024:(o + 1) * 1024])
        P.dma(khi[b][:], KH[(jp + NJ2) * 128:(jp + NJ2 + 1) * 128, o * 1024:(o + 1) * 1024])
        for hf in range(2):
            psr = K.ps[b * 4 + hf * 2][:]; psi = K.ps[b * 4 + hf * 2 + 1][:]
            for tb in range(NTB):
                P.mm(psr, Dre[b][:, tb, :], zT[:, tb0 + tb, hf * 512:(hf + 1) * 512], start=(tb == 0), stop=(tb == NTB - 1))
            for tb in range(NTB):
                P.mm(psi, Dim[b][:, tb, :], zT[:, tb0 + tb, hf * 512:(hf + 1) * 512], start=(tb == 0), stop=(tb == NTB - 1))
            sl = slice(hf * 512, (hf + 1) * 512)
            P.tt(tmp[0][:], psr, khr[b][:, sl], ALU.mult)
            P.tt(tmp[1][:], psi, khi[b][:, sl], ALU.mult)
            P.tt(tmp[2][:], psr, khi[b][:, sl], ALU.mult)
            P.tt(tmp[3][:], psi, khr[b][:, sl], ALU.mult)
            P.tt(crb[b][:, sl], tmp[0][:], tmp[1][:], ALU.subtract, e="pool")
            P.tt(cib[b][:, sl], tmp[2][:], tmp[3][:], ALU.add, e="pool")
            if jp == 0:
                P.tt(crb[b][0:1, sl], psr[0:1, :], khr[b][0:1, sl], ALU.mult)
                P.tt(cib[b][0:1, sl], psi[0:1, :], khi[b][0:1, sl], ALU.mult)
        P.dma(CHs[jp * 128:(jp + 1) * 128, :], crb[b][:])
        P.dma(CHs[(jp + NJ2) * 128:(jp + NJ2 + 1) * 128, :], cib[b][:])
    P.release(mk)
    mk = P.mark()
    NTK = min(512, L)
    PJ = min(4, NJ)
    CHp = [P.sb("hCHp%d" % i, [128, PJ, 1024], BF16) for i in range(2)]
    DTp = [P.sb("hDTp%d" % i, [128, PJ, NTK], BF16) for i in range(2)]
    zt = [P.sb("hzt%d" % i, [128, NTK]) for i in range(2)]
    xt = [P.sb("hxt%d" % i, [128, NTK]) for i in range(2)]
    gt = [P.sb("hgt%d" % i, [128, NTK]) for i in range(2)]
    zn = [P.sb("hzn%d" % i, [128, NTK]) for i in range(2)]
    XS = X1 if o == 0 else X2
    it = 0
    for tq in range(L // NTK):
        t0 = s0 + tq * NTK
        npiece = NJ // PJ
        for pc in range(npiece):
            b = it % 2
            it += 1
            P.dma(CHp[b][:], CHs[pc * PJ * 128:(pc + 1) * PJ * 128, :].rearrange("(jl p) c -> p jl c", p=128))
            P.dma(DTp[b][:], I.dftT[L][pc * PJ * 128:(pc + 1) * PJ * 128, tq * NTK:(tq + 1) * NTK].rearrange("(jl p) t -> p jl t", p=128))
            for ct in range(8):
                for jl in range(PJ):
                    P.mm(K.ps[ct][:, 0:NTK], CHp[b][:, jl, ct * 128:(ct + 1) * 128], DTp[b][:, jl, :],
                         start=(pc == 0 and jl == 0), stop=(pc == npiece - 1 and jl == PJ - 1))
        for ct in range(8):
            b = ct % 2
            rows = slice(ct * 128, (ct + 1) * 128)
            P.dma(zt[b][:], ZC[rows, t0:t0 + NTK])
            P.dma(xt[b][:], XS[rows, t0:t0 + NTK])
            P.stt(zn[b][:], zt[b][:], hbc[:, o, ct:ct + 1], K.ps[ct][:, 0:NTK], ALU.mult, ALU.add)
            if o == 0:
                P.tt(zn[b][:], zn[b][:], xt[b][:], ALU.mult)
                P.dma(ZC[rows, t0:t0 + NTK], zn[b][:])
                nq = NTK // 128
                for q in range(nq):
                    P.transpose(K.ps[ct][:, q * 128:(q + 1) * 128], zn[b][:, q * 128:(q + 1) * 128], K.ident[:])
                tbs = t0 // 128
                P.copy(zT[:, tbs:tbs + nq, ct * 128:(ct + 1) * 128],
                       K.ps[ct][:, 0:NTK].rearrange("p (q t) -> p q t", q=nq), e="act")
            else:
                P.dma(gt[b][:], GS[rows, t0:t0 + NTK])
                P.tt(zn[b][:], zn[b][:], xt[b][:], ALU.mult)
                P.tt(K.yb[:, ct, t0:t0 + NTK], zn[b][:], gt[b][:], ALU.mult, e="pool")
    P.release(mk)


def s5_host_consts():
    c = {}
    c["s5_bm"] = np.kron(np.eye(8), np.ones((16, 16))).astype(np.float32)
    c["s5_bmg"] = np.kron(np.eye(8), np.ones((16, 1))).astype(np.float32)
    m2 = np.zeros((2, 64, 4, 8), np.float32)
    for g2 in range(2):
        for gp in range(4):
            m2[g2, :, gp, 2 * gp + g2] = 1.0
    c["s5_mask2"] = m2.reshape(128, 32)
    return c


def bc(ap, shape):
    return ap.to_broadcast(list(shape))


def cmul(P, outr, outi, ar, ai, br, bi, t1, t2, e2="pool"):
    P.tt(t1, ar, br, ALU.mult)
    P.tt(t2, ai, bi, ALU.mult)
    P.tt(outr, t1, t2, ALU.subtract)
    P.tt(t1, ar, bi, ALU.mult)
    P.tt(t2, ai, br, ALU.mult)
    P.tt(outi, t1, t2, ALU.add)


def s5_lambar(K, l, R, C, lam_src, step_fill, NK, tag):
    P, I = K.P, K.I
    lr = P.sb(tag + "lr", [R, C]); li = P.sb(tag + "li", [R, C]); st = P.sb(tag + "st", [R, C])
    P.dma(lr[:], lam_src(I.s5_lambda_re[l])); P.dma(li[:], lam_src(I.s5_lambda_im[l]))
    step_fill(st)
    P.act(st[:], st[:], AF.Exp)
    mk = P.mark()
    ar = P.sb(tag + "ar", [R, C]); ang = P.sb(tag + "ang", [R, C]); mag = P.sb(tag + "mag", [R, C])
    s = P.sb(tag + "s", [R, C]); c = P.sb(tag + "c", [R, C]); ti = P.sb(tag + "ti", [R, C], I32); tf = P.sb(tag + "tf", [R, C])
    P.tt(ar[:], lr[:], st[:], ALU.mult)
    P.act(mag[:], ar[:], AF.Exp)
    P.tt(ang[:], li[:], st[:], ALU.mult)
    sin_rr(K, s[:], ang[:], ti[:], tf[:])
    P.ts(ang[:], ang[:], math.pi / 2.0, None, ALU.add)
    sin_rr(K, c[:], ang[:], ti[:], tf[:])
    pwr = P.sb(tag + "pwr", [R, NK, C]); pwi = P.sb(tag + "pwi", [R, NK, C])
    return lr, li, st, mag, s, c, pwr, pwi, mk


def s5_prep(K, l):
    P, I = K.P, K.I
    S = Ctx()
    if not hasattr(K, "s5c"):
        pass
    S.bm = P.sb("s5bm", [128, 128]); P.dma(S.bm[:], I.s5_bm)
    S.bmg = P.sb("s5bmg", [128, 8]); P.dma(S.bmg[:], I.s5_bmg)
    S.mask2 = P.sb("s5m2", [128, 4, 8]); P.dma(S.mask2[:], I.s5_mask2.rearrange("p (a b) -> p a b", a=4))
    def fillA(st):
        P.dma(st[:], I.s5_log_step[l].rearrange("d g -> (d g)").partition_broadcast(64))
    S.pAr = P.sb("s5pAr", [64, 9, 128]); S.pAi = P.sb("s5pAi", [64, 9, 128])
    S.qr = P.sb("s5qr", [64, 128]); S.qi = P.sb("s5qi", [64, 128])
    mk = P.mark()
    lr, li, st, mag, s, c, _, _, _ = s5_lambar(K, l, 64, 128, lambda a: a.rearrange("d g p -> p (d g)"), fillA, 1, "A")
    t1 = P.sb("At1", [64, 128]); t2 = P.sb("At2", [64, 128])
    P.memset(S.pAr[:, 0, :], 1.0); P.memset(S.pAi[:, 0, :], 0.0)
    P.tt(S.pAr[:, 1, :], mag[:], c[:], ALU.mult); P.tt(S.pAi[:, 1, :], mag[:], s[:], ALU.mult)
    for k in range(1, 8):
        cmul(P, S.pAr[:, k + 1, :], S.pAi[:, k + 1, :], S.pAr[:, k, :], S.pAi[:, k, :], S.pAr[:, 1, :], S.pAi[:, 1, :], t1[:], t2[:])
    dd = P.sb("Add", [64, 128]); nr = P.sb("Anr", [64, 128])
    P.tt(dd[:], lr[:], lr[:], ALU.mult); P.tt(t1[:], li[:], li[:], ALU.mult); P.tt(dd[:], dd[:], t1[:], ALU.add)
    P.op("dve", lambda g: g.reciprocal(dd[:], dd[:]), [dd[:]], [dd[:]])
    P.ts(nr[:], S.pAr[:, 1, :], -1.0, None, ALU.add)
    P.tt(t1[:], nr[:], lr[:], ALU.mult); P.tt(t2[:], S.pAi[:, 1, :], li[:], ALU.mult); P.tt(t1[:], t1[:], t2[:], ALU.add)
    P.tt(S.qr[:], t1[:], dd[:], ALU.mult)
    P.tt(t1[:], S.pAi[:, 1, :], lr[:], ALU.mult); P.tt(t2[:], nr[:], li[:], ALU.mult); P.tt(t1[:], t1[:], t2[:], ALU.subtract)
    P.tt(S.qi[:], t1[:], dd[:], ALU.mult)
    P.release(mk)
    def fillB(st):
        for g2 in range(2):
            P.dma(st[g2 * 64:(g2 + 1) * 64, :], I.s5_log_step[l].rearrange("d (gp g2) -> g2 (d gp)", g2=2)[g2].partition_broadcast(64))
    S.pBr = P.sb("s5pBr", [128, 9, 64]); S.pBi = P.sb("s5pBi", [128, 9, 64])
    S.apr = P.sb("s5apr", [128, 17, 64]); S.api = P.sb("s5api", [128, 17, 64])
    mk = P.mark()
    lr, li, st, mag, s, c, _, _, _ = s5_lambar(K, l, 128, 64, lambda a: a.rearrange("d (gp g2) p -> (g2 p) (d gp)", g2=2), fillB, 1, "B")
    t1 = P.sb("Bt1", [128, 64]); t2 = P.sb("Bt2", [128, 64])
    P.memset(S.pBr[:, 0, :], 1.0); P.memset(S.pBi[:, 0, :], 0.0)
    P.tt(S.pBr[:, 1, :], mag[:], c[:], ALU.mult); P.tt(S.pBi[:, 1, :], mag[:], s[:], ALU.mult)
    for k in range(1, 8):
        cmul(P, S.pBr[:, k + 1, :], S.pBi[:, k + 1, :], S.pBr[:, k, :], S.pBi[:, k, :], S.pBr[:, 1, :], S.pBi[:, 1, :], t1[:], t2[:])
    P.memset(S.apr[:, 0, :], 1.0); P.memset(S.api[:, 0, :], 0.0)
    P.copy(S.apr[:, 1, :], S.pBr[:, 8, :]); P.copy(S.api[:, 1, :], S.pBi[:, 8, :])
    for j in range(1, 16):
        cmul(P, S.apr[:, j + 1, :], S.api[:, j + 1, :], S.apr[:, j, :], S.api[:, j, :], S.apr[:, 1, :], S.api[:, 1, :], t1[:], t2[:])
    P.release(mk)
    return S


class StopS5(Exception):
    pass


import os as _os
S5_CUT = float(_os.environ.get("S5_CUT", "99"))


def s5cut(stage):
    if S5_CUT <= stage:
        raise StopS5()


def mixer_s5(K, l):
    P, I = K.P, K.I
    if not hasattr(K, "s5G"):
        K.s5G = K.dscr("s5_G", [1024, T], BF16)
    mk0 = P.mark()
    try:
        S = s5_prep(K, l)
        s5cut(0)
        for gt in range(8):
            s5_gt(K, l, gt, S)
            s5cut(8)
        s5_glu(K, l)
    except StopS5:
        pass
    P.release(mk0)


def s5_gt(K, l, gt, S):
    P, I, O = K.P, K.I, K.O
    mk = P.mark()
    NCH = T // 8
    ub = P.sb("s5ub", [128, T], BF16); yacc = P.sb("s5y", [128, T])
    dsk = P.sb("s5dsk", [128, 1]); P.dma(dsk[:], I.s5_D[l, gt * 128:(gt + 1) * 128].rearrange("(c o) -> c o", o=1))
    m = [P.sb("s5m%d" % i, [128, 4 * NCH]) for i in range(2)]
    s5cut(0.2)
    w = load_w(K, I.w_in[l][:, C_S5 + gt * 128: C_S5 + (gt + 1) * 128], 128)
    s5cut(0.4)
    for tt in range(NTT):
        ps = K.ps[tt % 8][:]
        proj(K, ps, w, 0, 128, tt)
        s5cut(0.6)
        uf = m[tt % 2][:, 0:512]
        P.copy(uf, ps, e="act")
        s5cut(0.8)
        P.copy(ub[:, tt * 512:(tt + 1) * 512], uf, e="pool")
        P.ts(yacc[:, tt * 512:(tt + 1) * 512], uf, dsk[:, 0:1], None, ALU.mult)
    s5cut(1)
    ubv = ub[:].rearrange("p (n s) -> p n s", s=8)
    ubr = P.sb("s5ubr", [128, T], BF16)
    ubrv = ubr[:].rearrange("p (n s) -> p n s", s=8)
    P.copy(ubrv, ubv[:, ::-1, :], e="pool")
    yv = yacc[:].rearrange("p (n s) -> p n s", s=8)
    Bre = P.sb("s5Bre", [64, 8, 16]); Bim = P.sb("s5Bim", [64, 8, 16])
    CAr = P.sb("s5CAr", [64, 8, 16]); CAi = P.sb("s5CAi", [64, 8, 16])
    bbr = P.sb("s5bbr", [64, 8, 16]); bbi = P.sb("s5bbi", [64, 8, 16])
    tA1 = P.sb("s5tA1", [64, 8, 16]); tA2 = P.sb("s5tA2", [64, 8, 16])
    Wr = P.sb("s5Wr", [64, 8, 128]); Wi = P.sb("s5Wi", [64, 8, 128])
    Kb = P.sb("s5Kb", [128, 8, 128], BF16)
    Pm = P.sb("s5Pm", [128, 8, 8, 64], BF16)
    CBr = P.sb("s5CBr", [128, 4, 16]); CBi = P.sb("s5CBi", [128, 4, 16])
    CLr = P.sb("s5CLr", [128, 9, 4, 16]); CLi = P.sb("s5CLi", [128, 9, 4, 16])
    tB1 = P.sb("s5tB1", [128, 4, 16]); tB2 = P.sb("s5tB2", [128, 4, 16])
    CLm = [[P.sb("s5CLm%d%d" % (i, r), [128, 4, 128], BF16) for r in range(2)] for i in range(2)]
    Dst = [P.sb("s5D%d" % r, [128, 4, NCH]) for r in range(2)]
    big = P.sb("s5big", [128, 2720])
    Pl = [big[:, r * 1360:(r + 1) * 1360].rearrange("p (c s j) -> p c s j", c=4, j=17) for r in range(2)]
    S2 = [P.sb("s5S2%d" % r, [128, 4, 20]) for r in range(2)]
    fin = [P.sb("s5fin%d" % r, [128, 4]) for r in range(2)]
    Sb = [P.sb("s5Sb%d" % r, [128, 4, NCH], BF16) for r in range(2)]
    for d in range(2):
        cA = slice(d * 64 + gt * 8, d * 64 + gt * 8 + 8)
        cB = slice(d * 32 + gt * 4, d * 32 + gt * 4 + 4)
        gsl = slice(gt * 8, gt * 8 + 8)
        P.dma(Bre[:], I.s5_B_re[l, d, gsl].rearrange("g p h -> p g h"))
        P.dma(Bim[:], I.s5_B_im[l, d, gsl].rearrange("g p h -> p g h"))
        P.dma(CAr[:], I.s5_C_re[l, d, gsl].rearrange("g h p -> p g h"))
        P.dma(CAi[:], I.s5_C_im[l, d, gsl].rearrange("g h p -> p g h"))
        for g2 in range(2):
            for gp in range(4):
                P.dma(CBr[g2 * 64:(g2 + 1) * 64, gp, :], I.s5_C_re[l, d, gt * 8 + 2 * gp + g2].rearrange("h p -> p h"))
                P.dma(CBi[g2 * 64:(g2 + 1) * 64, gp, :], I.s5_C_im[l, d, gt * 8 + 2 * gp + g2].rearrange("h p -> p h"))
        qrb = bc(S.qr[:, cA].unsqueeze(2), [64, 8, 16]); qib = bc(S.qi[:, cA].unsqueeze(2), [64, 8, 16])
        cmul(P, bbr[:], bbi[:], qrb, qib, Bre[:], Bim[:], tA1[:], tA2[:])
        for k in range(8):
            pr = bc(S.pAr[:, k, cA].unsqueeze(2), [64, 8, 16]); pi_ = bc(S.pAi[:, k, cA].unsqueeze(2), [64, 8, 16])
            cmul(P, Wr[:, k, :].rearrange("p (g h) -> p g h", h=16), Wi[:, k, :].rearrange("p (g h) -> p g h", h=16),
                 pr, pi_, bbr[:], bbi[:], tA1[:], tA2[:])
        s5cut(2)
        P.ts(CAi[:], CAi[:], -1.0, None, ALU.mult)
        car = CAr[:].rearrange("p g h -> p (g h)"); cai = CAi[:].rearrange("p g h -> p (g h)")
        for dl in range(8):
            ps = K.ps[dl % 8][:, 0:128]
            P.mm(ps, Wr[:, dl, :], car, start=True, stop=False)
            P.mm(ps, Wi[:, dl, :], cai, start=False, stop=True)
            P.tt(Kb[:, dl, :], ps, S.bm[:], ALU.mult)
        s5cut(3)
        it = 0
        for ri, Wx in enumerate((Wr, Wi)):
            for sg in range(8):
                k = 7 - sg if d == 0 else sg
                ps = K.ps[(sg + ri) % 8][:, 0:64]
                P.transpose(ps, Wx[:, k, :], K.ident[0:64, 0:64])
                P.tt(Pm[:, sg, :, :], bc(ps.unsqueeze(1), [128, 8, 64]), bc(S.bmg[:].unsqueeze(2), [128, 8, 64]), ALU.mult)
            for gp in range(4):
                ps = K.ps[it % 8][:, 0:NCH]
                it += 1
                for sg in range(8):
                    P.mm(ps, Pm[:, sg, 2 * gp:2 * gp + 2, :].rearrange("p g q -> p (g q)"), (ubv if d == 0 else ubrv)[:, :, sg],
                         start=(sg == 0), stop=(sg == 7))
                P.copy(Dst[ri][:, gp, :], ps, e=("act" if it % 2 else "dve"))
        s5cut(4)
        s5_scan(K, l, gt, d, S, Dst, Pl, S2, fin, m, cB)
        s5cut(5)
        P.copy(Sb[0][:], Dst[0][:])
        P.ts(Sb[1][:], Dst[1][:], -1.0, None, ALU.mult)
        for k in range(1, 9):
            pr = bc(S.pBr[:, k, cB].unsqueeze(2), [128, 4, 16]); pi_ = bc(S.pBi[:, k, cB].unsqueeze(2), [128, 4, 16])
            cmul(P, CLr[:, k], CLi[:, k], pr, pi_, CBr[:], CBi[:], tB1[:], tB2[:])
        for tau in range(8):
            k = tau + 1 if d == 0 else 8 - tau
            cl = CLm[tau % 2]
            for ri, CL in enumerate((CLr, CLi)):
                P.tt(cl[ri][:].rearrange("p a (b h) -> p a b h", h=16), bc(CL[:, k].unsqueeze(2), [128, 4, 8, 16]),
                     bc(S.mask2[:].unsqueeze(3), [128, 4, 8, 16]), ALU.mult)
            ps = K.ps[tau % 8][:, 0:NCH]
            uu = ubv if d == 0 else ubrv
            nd = (tau + 1) if d == 0 else (8 - tau)
            for dl in range(nd):
                P.mm(ps, Kb[:, dl, :], uu[:, :, (tau - dl) if d == 0 else (tau + dl)], start=(dl == 0), stop=False)
            i2 = 0
            for gp in range(4):
                for ri in range(2):
                    P.mm(ps, cl[ri][:, gp, :], Sb[ri][:, gp, :], start=False, stop=(i2 == 7))
                    i2 += 1
            src = ps if d == 0 else ps[:, ::-1]
            P.tt(yv[:, :, tau], yv[:, :, tau], src, ALU.add)
        s5cut(6)
    t = big[:, 0:T]; gb = ub
    P.tt(t, yacc[:], yacc[:], ALU.mult)
    P.ts(t, t, 0.044715, 1.0, ALU.mult, ALU.add)
    P.tt(t, t, yacc[:], ALU.mult)
    P.act(t, t, AF.Sigmoid, scale=2.0 * math.sqrt(2.0 / math.pi))
    P.tt(gb[:], t, yacc[:], ALU.mult)
    if K.dbg and K.stop == ("s5", l):
        P.dma(K.dbg32[gt * 128:(gt + 1) * 128, :], yacc[:], is_output=True)
    P.dma(K.s5G[gt * 128:(gt + 1) * 128, :], gb[:])
    P.release(mk)
    s5cut(7)


def s5_scan(K, l, gt, d, S, Dst, Pl, S2, fin, m, cB):
    P, I, O = K.P, K.I, K.O
    Ar = bc(S.apr[:, 1, cB].unsqueeze(2), [128, 4, 20]); Ai = bc(S.api[:, 1, cB].unsqueeze(2), [128, 4, 20])
    Dv = [Dst[r][:].rearrange("p c (s j) -> p c s j", j=16) for r in range(2)]
    m1 = m[0][:, 0:80].rearrange("p (c s) -> p c s", c=4); m2 = m[1][:, 0:80].rearrange("p (c s) -> p c s", c=4)
    for r in range(2):
        P.memset(Pl[r][:, :, :, 0], 0.0)
        P.copy(Pl[r][:, :, :, 1], Dv[r][:, :, :, 0])
    for j in range(1, 16):
        pr, pi_ = Pl[0][:, :, :, j], Pl[1][:, :, :, j]
        P.tt(m1, pr, Ar, ALU.mult); P.tt(m2, pi_, Ai, ALU.mult)
        P.tt(m1, m1, m2, ALU.subtract)
        P.tt(Pl[0][:, :, :, j + 1], m1, Dv[0][:, :, :, j], ALU.add)
        P.tt(m1, pr, Ai, ALU.mult); P.tt(m2, pi_, Ar, ALU.mult)
        P.tt(m1, m1, m2, ALU.add)
        P.tt(Pl[1][:, :, :, j + 1], m1, Dv[1][:, :, :, j], ALU.add)
    A16r, A16i = S.apr[:, 16, cB], S.api[:, 16, cB]
    seqs = [("s", 0, 16), ("pA", 16, 2), ("pB", 18, 2)] if d == 0 else [("pB", 0, 2), ("pA", 2, 2), ("s", 4, 16)]
    t1 = m[0][:, 100:104]; t2 = m[1][:, 100:104]

    def step(outr, outi, sr, si, dr, di):
        P.tt(t1, sr, A16r, ALU.mult); P.tt(t2, si, A16i, ALU.mult)
        P.tt(t1, t1, t2, ALU.subtract)
        P.tt(outr, t1, dr, ALU.add)
        P.tt(t1, sr, A16i, ALU.mult); P.tt(t2, si, A16r, ALU.mult)
        P.tt(t1, t1, t2, ALU.add)
        P.tt(outi, t1, di, ALU.add)

    for (nm, a, cnt) in seqs:
        if nm == "s":
            for r in range(2):
                for g2 in range(2):
                    P.dma(S2[r][g2 * 64:(g2 + 1) * 64, :, a],
                          I.ss[l, d, gt * 8 + g2:gt * 8 + 8:2, :, r].rearrange("gp p -> p gp"))
            for k in range(cnt - 1):
                step(S2[0][:, :, a + k + 1], S2[1][:, :, a + k + 1], S2[0][:, :, a + k], S2[1][:, :, a + k],
                     Pl[0][:, :, a + k, 16], Pl[1][:, :, a + k, 16])
        else:
            for r in range(2):
                P.memset(S2[r][:, :, a], 0.0)
                P.copy(S2[r][:, :, a + 1], Pl[r][:, :, a, 16])
            step(fin[0][:], fin[1][:], S2[0][:, :, a + 1], S2[1][:, :, a + 1], Pl[0][:, :, a + 1, 16], Pl[1][:, :, a + 1, 16])
            pr_i = 0 if nm == "pA" else 1
            for r in range(2):
                for g2 in range(2):
                    P.dma(O.ns[pr_i, l, d, gt * 8 + g2:gt * 8 + 8:2, :, r].rearrange("gp p -> p gp"),
                          fin[r][g2 * 64:(g2 + 1) * 64, :], is_output=True)
    apr = bc(S.apr[:, 0:16, cB].rearrange("p j c -> p c j").unsqueeze(2), [128, 4, 20, 16])
    api = bc(S.api[:, 0:16, cB].rearrange("p j c -> p c j").unsqueeze(2), [128, 4, 20, 16])
    s2r = bc(S2[0][:].unsqueeze(3), [128, 4, 20, 16]); s2i = bc(S2[1][:].unsqueeze(3), [128, 4, 20, 16])
    M1 = m[0][:].rearrange("p (c s j) -> p c s j", c=4, j=16); M2 = m[1][:].rearrange("p (c s j) -> p c s j", c=4, j=16)
    P.tt(M1, s2r, apr, ALU.mult); P.tt(M2, s2i, api, ALU.mult)
    P.tt(M1, M1, M2, ALU.subtract)
    P.tt(Dv[0], M1, Pl[0][:, :, :, 0:16], ALU.add)
    P.tt(M1, s2r, api, ALU.mult); P.tt(M2, s2i, apr, ALU.mult)
    P.tt(M1, M1, M2, ALU.add)
    P.tt(Dv[1], M1, Pl[1][:, :, :, 0:16], ALU.add)


def s5_glu(K, l):
    P, I = K.P, K.I
    mk = P.mark()
    gbT = P.sb("s5gbT", [128, 2, 8]);
    for hh in range(2):
        P.dma(gbT[:, hh, :], I.s5_glu_b[l, hh * 1024:(hh + 1) * 1024].rearrange("(ct c) -> c ct", c=128))
    Gt = [P.sb("s5Gt%d" % i, [128, 8, 512], BF16) for i in range(2)]
    Gv = K.s5G.rearrange("(kc k) t -> k kc t", k=128)
    wa = P.sb("s5wa", [128, 8, 1024], BF16); wg = P.sb("s5wg", [128, 8, 1024], BF16); wz = P.sb("s5wz", [128, 8, 1024], BF16)
    for c4 in range(8):
        for (dst, src) in ((wa, I.s5_glu_w[l][:, c4 * 128:(c4 + 1) * 128]), (wg, I.s5_glu_w[l][:, 1024 + c4 * 128:1024 + (c4 + 1) * 128]),
                           (wz, I.w_in[l][:, C_S5G + c4 * 128:C_S5G + (c4 + 1) * 128])):
            w = load_w(K, src, 128)
            P.copy(dst[:, :, c4 * 128:(c4 + 1) * 128], w, e="pool")
    sg = [P.sb("s5sg%d" % i, [128, 512]) for i in range(2)]
    sl = [P.sb("s5sl%d" % i, [128, 512]) for i in range(2)]
    tm = [P.sb("s5tm%d" % i, [128, 512]) for i in range(2)]
    it = 0
    for tt in range(NTT):
        G = Gt[tt % 2]
        P.dma(G[:], Gv[:, :, tt * 512:(tt + 1) * 512])
        for ct in range(8):
            b = it % 2
            pa = K.ps[(it * 3) % 8][:]; pg = K.ps[(it * 3 + 1) % 8][:]; pz = K.ps[(it * 3 + 2) % 8][:]
            it += 1
            for kc in range(8):
                P.mm(pa, wa[:, kc, ct * 128:(ct + 1) * 128], G[:, kc, :], start=(kc == 0), stop=(kc == 7))
            for kc in range(8):
                P.mm(pg, wg[:, kc, ct * 128:(ct + 1) * 128], G[:, kc, :], start=(kc == 0), stop=(kc == 7))
            for kc in range(8):
                P.mm(pz, wz[:, kc, ct * 128:(ct + 1) * 128], K.h[:, kc, tt * 512:(tt + 1) * 512], start=(kc == 0), stop=(kc == 7))
            P.act(sg[b][:], pg, AF.Sigmoid, bias=gbT[:, 1, ct:ct + 1])
            P.act(sl[b][:], pz, AF.Silu)
            P.stt(tm[b][:], pa, gbT[:, 0, ct:ct + 1], sg[b][:], ALU.add, ALU.mult)
            P.tt(K.yb[:, ct, tt * 512:(tt + 1) * 512], tm[b][:], sl[b][:], ALU.mult, e="pool")
    P.release(mk)


def gdn_host_consts():
    c = {}
    L = np.tril(np.ones((128, 128), np.float32))
    c["g_L"] = L; c["g_Ls"] = np.tril(np.ones((128, 128), np.float32), -1)
    c["g_U"] = np.ascontiguousarray(L.T); c["g_Us"] = np.triu(np.ones((128, 128), np.float32), 1)
    bd = np.kron(np.eye(4), np.ones((32, 32))).astype(np.float32)
    c["g_bd"] = bd; c["g_nbd"] = (1.0 - bd).astype(np.float32)
    return c


class StopGDN(Exception):
    pass


GDN_STOP = _os.environ.get("GDN_STOP", "")


def mixer_gdn(K, l):
    try:
        mixer_gdn_(K, l)
    except StopGDN:
        pass


def mixer_gdn_(K, l):
    P, I, O = K.P, K.I, K.O
    mk0 = P.mark()
    G = Ctx()
    G.M = {}
    for nm in ("g_L", "g_Ls", "g_U", "g_Us", "g_bd", "g_nbd"):
        G.M[nm] = P.sb(nm, [128, 128]); P.dma(G.M[nm][:], getattr(I, nm))
    G.gcw = P.sb("gcw", [128, 3, 24])
    for k in range(3):
        P.dma(G.gcw[:, k, :], I.gdn_conv_w[l, k].rearrange("(ft f) -> f ft", f=128))
    G.ng = P.sb("gng", [128, 128]); P.dma(G.ng[:], I.gdn_norm_g[l].partition_broadcast(128))
    alog = P.sb("galog", [128, 16]); P.dma(alog[:], I.gdn_A_log[l].rearrange("d h -> (d h)").partition_broadcast(128))
    dtb = P.sb("gdtb", [128, 16]); P.dma(dtb[:], I.gdn_dt_bias[l].rearrange("d h -> (d h)").partition_broadcast(128))
    P.act(alog[:], alog[:], AF.Exp)
    P.ts(alog[:], alog[:], -1.0, None, ALU.mult)
    G.bt = P.sb("gbt", [128, 20, 17]); G.gt = P.sb("ggt", [128, 20, 17]); G.nbt = P.sb("gnbt", [128, 20, 17])
    P.memset(G.gt[:], 0.0); P.memset(G.bt[:], 0.0)
    w32 = load_w(K, I.w_in[l][:, C_BETA:C_BETA + 32], 32)
    mkb = P.mark()
    bgraw = P.sb("gbgraw", [128, 20, 32])
    for c in range(20):
        ps = K.ps[c % 8][:, 0:32]
        for kc in range(8):
            P.mm(ps, K.h[:, kc, c * 128:(c + 1) * 128], w32[:, kc, :], start=(kc == 0), stop=(kc == 7))
        P.copy(bgraw[:, c, :], ps, e="act")
        P.act(G.bt[:, c, 0:16], bgraw[:, c, 0:16], AF.Sigmoid)
        P.tt(G.gt[:, c, 0:16], bgraw[:, c, 16:32], dtb[:], ALU.add)
    P.act(G.gt[:], G.gt[:], AF.Exp)
    P.act(G.gt[:], G.gt[:], AF.Ln, bias=K.one1[:, 0:1])
    for c in range(20):
        P.tt(G.gt[:, c, 0:16], G.gt[:, c, 0:16], alog[:], ALU.mult)
    P.ts(G.nbt[:], G.bt[:], -1.0, None, ALU.mult)
    P.release(mkb)
    for hd in range(8):
        gdn_head(K, l, hd, G)
    P.release(mk0)


def gdn_head(K, l, hd, G):
    P, I, O = K.P, K.I, K.O
    mk = P.mark()
    qT = P.sb("gqT", [128, T]); kT = P.sb("gkT", [128, T])
    ktok = P.sb("gktok", [128, 20, 128]); vtok = P.sb("gvtok", [128, 20, 128])
    oacc = P.sb("goacc", [128, 20, 128])
    mk1 = P.mark()
    vT = P.sb("gvT", [128, T]); raw = P.sb("graw", [128, T])
    rs = P.sb("grs", [128, 512])
    for s, dst in enumerate((qT, kT, vT)):
        c0 = C_GDN + s * 1024 + hd * 128
        w = load_w(K, I.w_in[l][:, c0:c0 + 128], 128)
        for tt in range(NTT):
            ps = K.ps[tt % 8][:]
            proj(K, ps, w, 0, 128, tt)
            P.copy(raw[:, tt * 512:(tt + 1) * 512], ps, e="act")
        ft = s * 8 + hd
        P.ts(dst[:], raw[:], G.gcw[:, 1, ft:ft + 1], None, ALU.mult)
        for (s0, L) in SEQS:
            P.stt(dst[:, s0 + 1:s0 + L], raw[:, s0:s0 + L - 1], G.gcw[:, 0, ft:ft + 1], dst[:, s0 + 1:s0 + L], ALU.mult, ALU.add)
            P.stt(dst[:, s0:s0 + L - 1], raw[:, s0 + 1:s0 + L], G.gcw[:, 2, ft:ft + 1], dst[:, s0:s0 + L - 1], ALU.mult, ALU.add)
        P.act(dst[:], dst[:], AF.Silu)
        if s < 2:
            P.tt(raw[:], dst[:], dst[:], ALU.mult)
            for tt in range(NTT):
                ps = K.ps[(tt + 5) % 8][:]
                P.mm(ps, K.ones[:], raw[:, tt * 512:(tt + 1) * 512])
                P.act(rs[:], ps, AF.Sqrt, bias=K.eps[:, 0:1])
                P.op("dve", lambda g: g.reciprocal(rs[:], rs[:]), [rs[:]], [rs[:]])
                if s == 0:
                    P.stt(dst[:, tt * 512:(tt + 1) * 512], dst[:, tt * 512:(tt + 1) * 512], 128.0 ** -0.5, rs[:], ALU.mult, ALU.mult)
                else:
                    P.tt(dst[:, tt * 512:(tt + 1) * 512], dst[:, tt * 512:(tt + 1) * 512], rs[:], ALU.mult)
    for (src, dstk) in ((kT, ktok), (vT, vtok)):
        for g5 in range(5):
            ps = K.ps[g5 % 8]
            for q in range(4):
                c = g5 * 4 + q
                P.transpose(ps[:, q * 128:(q + 1) * 128], src[:, c * 128:(c + 1) * 128], K.ident[:])
            P.copy(dstk[:, g5 * 4:(g5 + 1) * 4, :], ps[:].rearrange("p (q t) -> p q t", q=4), e="act")
    P.release(mk1)
    for d in range(2):
        mk2 = P.mark()
        col = d * 8 + hd
        uall = P.sb("guall", [128, 20, 128]); wTall = P.sb("gwTall", [128, 20, 128])
        qkTall = P.sb("gqkTall", [128, 20, 128])
        egc = P.sb("gegc", [128, 20]); egl = P.sb("gegl", [128, 20]); e2all = P.sb("ge2all", [128, 20])
        NS = 4
        mks = P.mark()
        slots = []
        for s in range(NS):
            sl = Ctx()
            sl.Pm = [P.sb("gP%d_%d" % (s, i), [128, 128]) for i in range(2)]
            sl.PT = [P.sb("gPT%d_%d" % (s, i), [128, 128]) for i in range(2)]
            sl.TT = [P.sb("gTT%d_%d" % (s, i), [128, 128]) for i in range(2)]
            sl.big = P.sb("gbig%d" % s, [128, 512])
            sl.Rm = sl.big[:, 0:128]; sl.E = sl.big[:, 128:256]; sl.t1 = sl.big[:, 256:384]; sl.qk = sl.big[:, 384:512]
            sl.Y = sl.big[:, 0:256]; sl.Z = sl.big[:, 256:512]
            sl.X = P.sb("gX%d" % s, [128, 256]); sl.bv = sl.X[:, 0:128]; sl.kbg = sl.X[:, 128:256]
            sl.Bf = P.sb("gBf%d" % s, [128, 128]); sl.NAo = P.sb("gNAo%d" % s, [128, 128])
            sl.cols = P.sb("gcols%d" % s, [128, 8])
            sl.banks = [K.ps[(2 * s) % 8], K.ps[(2 * s + 1) % 8]]
            slots.append(sl)
        Minc_ij = G.M["g_L"] if d == 0 else G.M["g_U"]
        Mstr_ij = G.M["g_Ls"] if d == 0 else G.M["g_Us"]
        Mkj = G.M["g_U"] if d == 0 else G.M["g_L"]
        jl = 127 if d == 0 else 0

        def R(sl, i):
            return sl.banks[(i // 4) % 2][:, (i % 4) * 128:(i % 4 + 1) * 128]

        def st1(sl, c):
            ch = slice(c * 128, (c + 1) * 128)
            P.ts(sl.Rm, Mkj[:], G.gt[:, c, col:col + 1], None, ALU.mult)
            P.mm(R(sl, 0)[:, 0:2], Mkj[:], G.gt[:, c, col:col + 2])
            P.mm(R(sl, 1), K.ones[:], sl.Rm)
            P.mm(R(sl, 2), kT[:, ch], kT[:, ch])
            P.mm(R(sl, 3), qT[:, ch], kT[:, ch])

        def st2(sl, c):
            P.copy(sl.cols[:, 0:1], R(sl, 0)[:, 0:1])
            P.copy(sl.cols[:, 1:2], R(sl, 1)[:, jl:jl + 1])
            P.ts(sl.t1, R(sl, 1), sl.cols[:, 0:1], 0.0, ALU.subtract, ALU.max)
            P.act(sl.E, sl.t1, AF.Exp, scale=-1.0)
            P.act(egc[:, c:c + 1], sl.cols[:, 0:1], AF.Exp)
            P.act(egl[:, c:c + 1], sl.cols[:, 1:2], AF.Exp)
            P.act(e2all[:, c:c + 1], sl.cols[:, 0:1], AF.Exp, scale=-1.0, bias=sl.cols[:, 1:2])
            P.tt(sl.cols[:, 3:4], G.bt[:, c, col:col + 1], egc[:, c:c + 1], ALU.mult)

        def st3(sl, c):
            P.tt(sl.t1, R(sl, 2), sl.E, ALU.mult)
            P.stt(sl.Bf[:], sl.t1, G.nbt[:, c, col:col + 1], Mstr_ij[:], ALU.mult, ALU.mult)
            P.tt(sl.t1, R(sl, 3), sl.E, ALU.mult)
            P.tt(sl.qk, sl.t1, Minc_ij[:], ALU.mult, e="pool")
            P.ts(sl.kbg, ktok[:, c, :], sl.cols[:, 3:4], None, ALU.mult, e="pool")
            P.ts(sl.bv, vtok[:, c, :], G.bt[:, c, col:col + 1], None, ALU.mult, e="pool")

        def st4(sl, c):
            P.transpose(R(sl, 0), sl.Bf[:], K.ident[:])
            P.transpose(R(sl, 5), sl.qk, K.ident[:])
            P.tt(sl.Pm[0][:], sl.Bf[:], G.M["g_bd"][:], ALU.mult, e="pool")
            P.tt(sl.PT[0][:], R(sl, 0), G.M["g_bd"][:], ALU.mult)
            P.tt(sl.NAo[:], R(sl, 0), G.M["g_nbd"][:], ALU.mult)
            P.copy(qkTall[:, c, :], R(sl, 5), e="act")
            P.tt(sl.TT[0][:], sl.PT[0][:], K.ident[:], ALU.add, e="pool")

        def st5a(sl, c, a):
            b = 1 - a
            P.mm(R(sl, 6), sl.PT[a][:], sl.Pm[a][:])
            P.mm(R(sl, 7), sl.Pm[a][:], sl.PT[a][:])
            P.copy(sl.Pm[b][:], R(sl, 6), e="act")
            P.copy(sl.PT[b][:], R(sl, 7), e="act")

        def st5b(sl, c, a):
            b = 1 - a
            P.mm(R(sl, 0), sl.Pm[b][:], sl.TT[a][:])
            P.tt(sl.TT[b][:], R(sl, 0), sl.TT[a][:], ALU.add)

        def st6a(sl, c, a):
            P.mm(sl.banks[0][:, 0:256], sl.TT[a][:], sl.X[:])
            P.copy(sl.Y, sl.banks[0][:, 0:256])

        def st6b(sl, c, a):
            P.mm(sl.banks[0][:, 256:512], sl.NAo[:], sl.Y)
            P.tt(sl.Z, sl.X[:], sl.banks[0][:, 256:512], ALU.add)
            P.mm(sl.banks[0][:, 0:256], sl.TT[a][:], sl.Z)
            P.copy(sl.Y, sl.banks[0][:, 0:256])

        def st6c(sl, c, a):
            P.copy(uall[:, c, :], sl.Y[:, 0:128], e="pool")
            P.transpose(R(sl, 6), sl.Y[:, 128:256], K.ident[:])
            P.copy(wTall[:, c, :], R(sl, 6), e="act")

        for c0 in range(0, min(20, int(_os.environ.get("GDN_NB", "99")) * NS), NS):
            batch = [(slots[i], c0 + i) for i in range(NS) if c0 + i < 20]
            for st in (st1, st2, st3, st4):
                for (sl, c) in batch:
                    st(sl, c)
            a = 0
            for lev in range(4):
                for (sl, c) in batch:
                    st5a(sl, c, a)
                for (sl, c) in batch:
                    st5b(sl, c, a)
                a = 1 - a
            for (sl, c) in batch:
                st6a(sl, c, a)
            for it in range(3):
                for (sl, c) in batch:
                    st6b(sl, c, a)
            for (sl, c) in batch:
                st6c(sl, c, a)
        if GDN_STOP == "%d,%d,g2" % (hd, d):
            raise StopGDN()
        P.release(mks)
        chains = [("s", list(range(0, 16))), ("pA", [16, 17]), ("pB", [18, 19])]
        if d == 1:
            chains = [(nm, cs[::-1]) for nm, cs in chains]
        Sst = {nm: [P.sb("gS%s%d" % (nm, i), [128, 128]) for i in range(2)] for nm, _ in chains}
        vnew = [P.sb("gvnew%d" % i, [128, 128]) for i in range(3)]
        tq = [P.sb("gtq%d" % i, [128, 128]) for i in range(3)]
        kg2 = [P.sb("gkg2%d" % i, [128, 128]) for i in range(3)]
        P.dma(Sst["s"][0][:], I.sg[l, d, hd])
        P.memset(Sst["pA"][0][:], 0.0); P.memset(Sst["pB"][0][:], 0.0)
        cur = {nm: 0 for nm, _ in chains}
        for step in range(16):
            for ci, (nm, cs) in enumerate(chains):
                if step >= len(cs):
                    continue
                c = cs[step]
                ch = slice(c * 128, (c + 1) * 128)
                S = Sst[nm][cur[nm]]; Sn = Sst[nm][1 - cur[nm]]
                bk = K.ps[6 + (ci % 2)] if ci < 2 else K.ps[0]
                r0, r1, r2, r3 = (bk[:, i * 128:(i + 1) * 128] for i in range(4))
                vn = vnew[ci]; t = tq[ci]
                P.mm(r0, wTall[:, c, :], S[:])
                P.mm(r1, qT[:, ch], S[:])
                P.tt(vn[:], uall[:, c, :], r0, ALU.subtract)
                P.mm(r2, qkTall[:, c, :], vn[:])
                P.ts(kg2[ci][:], ktok[:, c, :], e2all[:, c:c + 1], None, ALU.mult, e="pool")
                P.mm(r3, kg2[ci][:], vn[:])
                P.ts(t[:], r1, egc[:, c:c + 1], None, ALU.mult)
                if d == 0:
                    P.tt(oacc[:, c, :], r2, t[:], ALU.add)
                else:
                    P.tt(t[:], r2, t[:], ALU.add)
                    P.tt(oacc[:, c, :], oacc[:, c, :], t[:], ALU.add, e="pool")
                P.stt(Sn[:], S[:], egl[:, c:c + 1], r3, ALU.mult, ALU.add)
                cur[nm] = 1 - cur[nm]
        if GDN_STOP == "%d,%d,g3" % (hd, d):
            raise StopGDN()
        for pi, nm in enumerate(("pA", "pB")):
            P.dma(O.ng[pi, l, d, hd], Sst[nm][cur[nm]][:], is_output=True)
        P.release(mk2)
    mk3 = P.mark()
    gsl = P.sb("ggsl", [128, T])
    w = load_w(K, I.w_in[l][:, C_GDNG + hd * 128:C_GDNG + (hd + 1) * 128], 128)
    for tt in range(NTT):
        ps = K.ps[tt % 8][:]
        proj(K, ps, w, 0, 128, tt)
        P.act(gsl[:, tt * 512:(tt + 1) * 512], ps, AF.Silu)
    sq = P.sb("gsq", [128, 128]); ssum = P.sb("gssum", [128, 20])
    for c in range(20):
        P.tt(sq[:], oacc[:, c, :], oacc[:, c, :], ALU.mult)
        P.op("dve", lambda g, c=c: g.tensor_reduce(ssum[:, c:c + 1], sq[:], mybir.AxisListType.X, ALU.add),
             [sq[:]], [ssum[:, c:c + 1]])
    P.act(ssum[:], ssum[:], AF.Sqrt, bias=K.eps[:, 0:1], scale=1.0 / 128.0)
    P.op("dve", lambda g: g.reciprocal(ssum[:], ssum[:]), [ssum[:]], [ssum[:]])
    for c in range(20):
        P.stt(oacc[:, c, :], oacc[:, c, :], ssum[:, c:c + 1], G.ng[:], ALU.mult, ALU.mult)
    for g5 in range(5):
        ps = K.ps[g5 % 8]
        for q in range(4):
            c = g5 * 4 + q
            P.transpose(ps[:, q * 128:(q + 1) * 128], oacc[:, c, :], K.ident[:])
        P.tt(K.yb[:, hd, g5 * 512:(g5 + 1) * 512], ps[:], gsl[:, g5 * 512:(g5 + 1) * 512], ALU.mult)
    if K.dbg and K.stop == ("gdn", l):
        pass
    P.release(mk3)
    P.release(mk)


def _grid_pos(n_tokens, dim):
    t = np.arange(n_tokens)
    r = (t // 64).astype(np.float32)
    col = (t % 64).astype(np.float32)
    quarter = dim // 4
    omega = (1.0 / (10000.0 ** (np.arange(quarter, dtype=np.float32) / np.float32(quarter)))).astype(np.float32)
    er = r[:, None] * omega
    ec = col[:, None] * omega
    return np.concatenate([np.sin(er), np.cos(er), np.sin(ec), np.cos(ec)], axis=-1).astype(np.float32)


_CONSTS = None


def host_consts():
    global _CONSTS
    if _CONSTS is None:
        c = {}
        c["ident"] = np.eye(128, dtype=np.float32)
        c["pos"] = _grid_pos(2048, 1024)
        c.update(extra_consts())
        _CONSTS = c
    return _CONSTS


_PROG = None

WEIGHT_NAMES = ["w_mod", "b_mod", "norm_g", "w_in", "hy_conv_w", "hy_conv_b", "hy_f_w1", "hy_f_b1", "hy_f_w2", "hy_f_b2",
                "hy_f_w3", "hy_f_freq", "hy_bias", "s5_lambda_re", "s5_lambda_im", "s5_log_step", "s5_B_re", "s5_B_im",
                "s5_C_re", "s5_C_im", "s5_D", "s5_glu_w", "s5_glu_b", "gdn_conv_w", "gdn_A_log", "gdn_dt_bias",
                "gdn_norm_g", "w_branch", "w_out", "final_norm_g"]


def make_in_maps(inputs, ncores=8):
    consts = host_consts()
    f = lambda a: np.ascontiguousarray(np.asarray(a, dtype=np.float32))
    w = {k: f(inputs[k]) for k in WEIGHT_NAMES}
    xs, xp, c, cctx = f(inputs["x_sample"]), f(inputs["x_prompt"]), f(inputs["c"]), f(inputs["c_ctx"])
    sg, ss = f(inputs["state_gdn"]), f(inputs["state_s5"])
    maps = []
    for b in range(ncores):
        m = dict(w)
        m.update(consts)
        m["xs"] = xs[b]
        m["xp"] = np.ascontiguousarray(xp[2 * b:2 * b + 2].reshape(512, 1024))
        m["cvec"] = np.ascontiguousarray(np.stack([cctx, c[b]], axis=0))
        m["sg"] = sg[b]
        m["ss"] = ss[b]
        maps.append(m)
    return maps


def kernel(**inputs):
    global _PROG
    if _PROG is None:
        _PROG = build_program()
    nc, K = _PROG
    maps = make_in_maps(inputs)
    res = run_bass_kernel_spmd(nc, maps, core_ids=list(range(8)))
    r = res.results
    y_sample = np.stack([r[b]["ys"] for b in range(8)], axis=0).astype(np.float32)
    y_prompt = np.concatenate([r[b]["yp"].reshape(2, 256, 1024) for b in range(8)], axis=0).astype(np.float32)
    ng = np.concatenate([r[b]["ng"] for b in range(8)], axis=0).astype(np.float32)
    ns = np.concatenate([r[b]["ns"] for b in range(8)], axis=0).astype(np.float32)
    return (y_prompt, y_sample, ng, ns)
```
